# Optimizing a Trainium2 kernel written in Bass

```python
import math
import jax, jax.numpy as jnp
from jax import lax
import numpy as np

D_MODEL = 4096
BATCH = 4
SEQ = 2048
DEPTH = 2
DEC_BATCH = 8
DEC_SEQ = 1
PAST_LEN = 16384
PAGE_SIZE = 128

W_A = D_MODEL // 4
W_B = D_MODEL // 4
HEAD_DIM = 128
N_ATTN_GROUPS = 3
HEADS_PER_GROUP = 4
N_ATTN_HEADS = N_ATTN_GROUPS * HEADS_PER_GROUP
W_C_QKV = N_ATTN_HEADS * HEAD_DIM
W_C = HEADS_PER_GROUP * HEAD_DIM
W_D = D_MODEL // 4
CHUNK = 128
D_GROUP_DIM = 128
D_GROUPS = W_D // D_GROUP_DIM
CONV_A = 3
CONV_B = 31
WINDOWS = (128, 512, 2048)
DILATIONS = (1, 4, 16)
ROPE_THETA = 10000.0
N_BRANCH = 4
LN_EPS = 1e-5
ALPHA = (2 * DEPTH) ** 0.25
BETA = (8 * DEPTH) ** -0.25
SPLIT_SIZES = (W_A, W_A, W_A, W_A,
               W_B, W_B, W_B,
               W_C_QKV, W_C_QKV, W_C_QKV, W_C,
               W_D, W_D, W_D,
               N_BRANCH * D_MODEL)
N_IN = sum(SPLIT_SIZES)

kernel_name = "hybrid_conv_dilated_gmlp_decoder_step"


def _split_cols(h):
    offs = []
    acc = 0
    for s in SPLIT_SIZES[:-1]:
        acc += s
        offs.append(acc)
    return jnp.split(h, offs, axis=-1)


def _layernorm(x, g, b):
    xf = x.astype(jnp.float32)
    mu = jnp.mean(xf, axis=-1, keepdims=True)
    var = jnp.mean(jnp.square(xf - mu), axis=-1, keepdims=True)
    return ((xf - mu) * lax.rsqrt(var + LN_EPS) * g.astype(jnp.float32)
            + b.astype(jnp.float32)).astype(x.dtype)


def _rope(x, pos):
    half = HEAD_DIM // 2
    inv = ROPE_THETA ** (-jnp.arange(half, dtype=jnp.float32) / half)
    ang = pos.astype(jnp.float32)[:, None] * inv[None, :]
    cos = jnp.cos(ang)[None, :, None, :]
    sin = jnp.sin(ang)[None, :, None, :]
    x1 = x[..., :half].astype(jnp.float32)
    x2 = x[..., half:].astype(jnp.float32)
    return jnp.concatenate([x1 * cos - x2 * sin, x2 * cos + x1 * sin], axis=-1).astype(x.dtype)


def _dwconv(u_ext, w):
    return lax.conv_general_dilated(
        u_ext, w.astype(u_ext.dtype)[:, None, :], window_strides=(1,), padding='VALID',
        dimension_numbers=('NWC', 'WIO', 'NWC'), feature_group_count=u_ext.shape[-1])


def _dilated_prompt(q, k, v, dil, span):
    nb_, s_len, nh, dh = q.shape
    blk = span
    unit = dil * blk
    sp = -(-s_len // unit) * unit
    m_len = sp // dil
    nblk = m_len // blk

    def split(a):
        a = jnp.pad(a, ((0, 0), (0, sp - s_len), (0, 0), (0, 0)))
        a = a.reshape(nb_, m_len, dil, nh, dh).transpose(0, 2, 1, 3, 4)
        return a.reshape(nb_, dil, nblk, blk, nh, dh)

    def with_prev(a):
        prev = jnp.pad(a, ((0, 0), (0, 0), (1, 0), (0, 0), (0, 0), (0, 0)))[:, :, :-1]
        return jnp.concatenate([prev, a], axis=3)

    qs = split(q)
    ks = with_prev(split(k))
    vs = with_prev(split(v))
    s = jnp.einsum('brnqhd,brnkhd->brnhqk', qs, ks,
                   preferred_element_type=jnp.float32) * (dh ** -0.5)
    i = jnp.arange(blk)[:, None]
    j = jnp.arange(2 * blk)[None, :]
    band = jnp.where(j < blk, j >= i, (j - blk) <= i)
    has_prev = (jnp.arange(nblk) > 0)[:, None, None]
    valid = band[None] & (has_prev | (j >= blk)[None])
    s = jnp.where(valid[None, None, :, None], s, -jnp.inf)
    mx = jnp.max(s, axis=-1, keepdims=True)
    p = jnp.exp(s - mx)
    den = jnp.sum(p, axis=-1, keepdims=True)
    o = jnp.einsum('brnhqk,brnkhd->brnqhd', (p / den).astype(v.dtype), vs)
    lse = (mx + jnp.log(den))[..., 0]
    o = o.reshape(nb_, dil, m_len, nh, dh).transpose(0, 2, 1, 3, 4).reshape(nb_, sp, nh, dh)[:, :s_len]
    lse = lse.transpose(0, 1, 2, 4, 3).reshape(nb_, dil, m_len, nh)
    lse = lse.transpose(0, 2, 1, 3).reshape(nb_, sp, nh)[:, :s_len]
    return o, lse


def _dilated_decode(q, k_ext, v_ext, n_hist, dil, span):
    t = q.shape[1]
    idx = n_hist + jnp.arange(t)[:, None] - dil * jnp.arange(span + 1)[None, :]
    valid = idx >= 0
    idx = jnp.maximum(idx, 0)
    kg = k_ext[:, idx]
    vg = v_ext[:, idx]
    s = jnp.einsum('bthd,btkhd->bthk', q, kg,
                   preferred_element_type=jnp.float32) * (q.shape[-1] ** -0.5)
    s = jnp.where(valid[None, :, None, :], s, -jnp.inf)
    mx = jnp.max(s, axis=-1, keepdims=True)
    p = jnp.exp(s - mx)
    den = jnp.sum(p, axis=-1, keepdims=True)
    o = jnp.einsum('bthk,btkhd->bthd', (p / den).astype(vg.dtype), vg)
    lse = (mx + jnp.log(den))[..., 0]
    return o, lse


def _chunk_mix(v, w_s, b_s):
    nb_, t, _ = v.shape
    tp = -(-t // CHUNK) * CHUNK
    vp = jnp.pad(v, ((0, 0), (0, tp - t), (0, 0))).reshape(nb_, tp // CHUNK, CHUNK, D_GROUPS, D_GROUP_DIM)
    causal = jnp.tril(jnp.ones((CHUNK, CHUNK), dtype=bool))
    wm = jnp.where(causal[None], w_s, 0.0).astype(v.dtype)
    out = jnp.einsum('gts,bcsgd->bctgd', wm, vp) + b_s.T.astype(v.dtype)[None, None, :, :, None]
    return out.reshape(nb_, tp, W_D)[:, :t]


def _layer(x, pos, hist_a, hist_b, attend, lw):
    (w_in, conv_a_w, conv_b_w, conv_b_bias, ln_b_g, ln_b_b, ln_d_g, ln_d_b, w_s, b_s,
     w_out_a, w_out_b, w_out_c, w_out_d, b_gate, w_o, ln_g, ln_b) = lw
    nb_, t, _ = x.shape
    h = jnp.einsum('btd,dn->btn', x, w_in)
    (a_b, a_c, a_h, a_z, b_a, b_g, b_z, c_q, c_k, c_v, c_z,
     d_u, d_v, d_z, g_log) = _split_cols(h)

    ua = jnp.concatenate([hist_a.astype(x.dtype), a_c * a_h], axis=1)
    y_a = a_b * _dwconv(ua, conv_a_w) * jax.nn.silu(a_z)

    ub = jnp.concatenate([hist_b.astype(x.dtype), b_a * jax.nn.sigmoid(b_g)], axis=1)
    cb = _dwconv(ub, conv_b_w) + conv_b_bias.astype(x.dtype)
    y_b = jax.nn.silu(_layernorm(cb, ln_b_g, ln_b_b)) * jax.nn.silu(b_z)

    q = _rope(c_q.reshape(nb_, t, N_ATTN_HEADS, HEAD_DIM), pos)
    k = _rope(c_k.reshape(nb_, t, N_ATTN_HEADS, HEAD_DIM), pos)
    q = q.reshape(nb_, t, N_ATTN_GROUPS, HEADS_PER_GROUP, HEAD_DIM)
    k = k.reshape(nb_, t, N_ATTN_GROUPS, HEADS_PER_GROUP, HEAD_DIM)
    v = c_v.reshape(nb_, t, N_ATTN_GROUPS, HEADS_PER_GROUP, HEAD_DIM)
    outs, lses, kv_new = [], [], []
    for g in range(N_ATTN_GROUPS):
        o_g, lse_g, kv_g = attend(g, q[:, :, g], k[:, :, g], v[:, :, g])
        outs.append(o_g)
        lses.append(lse_g)
        kv_new.append(kv_g)
    wts = jax.nn.softmax(jnp.stack(lses, axis=2), axis=2)
    o = jnp.einsum('btgh,btghd->bthd', wts.astype(x.dtype), jnp.stack(outs, axis=2))
    y_c = o.reshape(nb_, t, W_C) * jax.nn.silu(c_z)

    vn = _layernorm(d_v, ln_d_g, ln_d_b)
    y_d = d_u * _chunk_mix(vn, w_s, b_s) * jax.nn.silu(d_z)

    gate = jax.nn.sigmoid(g_log + b_gate.astype(x.dtype)).reshape(nb_, t, N_BRANCH, D_MODEL)
    merged = (gate[:, :, 0] * (y_a @ w_out_a) + gate[:, :, 1] * (y_b @ w_out_b)
              + gate[:, :, 2] * (y_c @ w_out_c) + gate[:, :, 3] * (y_d @ w_out_d))
    y = _layernorm(ALPHA * x + merged @ w_o, ln_g, ln_b)
    return y, ua[:, -(CONV_A - 1):], ub[:, -(CONV_B - 1):], kv_new, vn


def setup_inputs(seed: int = 0) -> dict:
    key = jax.random.key(seed)
    ks = jax.random.split(key, 26)

    def nrm(k, shape, scale):
        return jax.random.normal(k, shape, jnp.float32) * scale

    lens = [min(w, PAST_LEN) for w in WINDOWS]
    kvs = (HEADS_PER_GROUP, HEAD_DIM)
    return {
        "x_prompt": nrm(ks[0], (BATCH, SEQ, D_MODEL), 1.0),
        "x_sample": nrm(ks[1], (DEC_BATCH, DEC_SEQ, D_MODEL), 1.0),
        "state_conv_a": nrm(ks[2], (DEPTH, DEC_BATCH, CONV_A - 1, W_A), 1.0),
        "state_conv_b": nrm(ks[3], (DEPTH, DEC_BATCH, CONV_B - 1, W_B), 1.0),
        "cache_kv_w128": nrm(ks[4], (DEPTH, DEC_BATCH, lens[0], 2) + kvs, 1.0),
        "cache_kv_w512": nrm(ks[5], (DEPTH, DEC_BATCH, lens[1], 2) + kvs, 1.0),
        "cache_kv_w2048": nrm(ks[6], (DEPTH, DEC_BATCH, lens[2], 2) + kvs, 1.0),
        "w_in": nrm(ks[7], (DEPTH, D_MODEL, N_IN), D_MODEL ** -0.5),
        "conv_a_w": nrm(ks[8], (DEPTH, CONV_A, W_A), CONV_A ** -0.5),
        "conv_b_w": nrm(ks[9], (DEPTH, CONV_B, W_B), CONV_B ** -0.5),
        "conv_b_bias": nrm(ks[10], (DEPTH, W_B), 0.01),
        "ln_b_g": 1.0 + nrm(ks[11], (DEPTH, W_B), 0.01),
        "ln_b_b": nrm(ks[12], (DEPTH, W_B), 0.01),
        "ln_d_g": 1.0 + nrm(ks[13], (DEPTH, W_D), 0.01),
        "ln_d_b": nrm(ks[14], (DEPTH, W_D), 0.01),
        "w_s": nrm(ks[15], (DEPTH, D_GROUPS, CHUNK, CHUNK), CHUNK ** -0.5),
        "b_s": 1.0 + nrm(ks[16], (DEPTH, D_GROUPS, CHUNK), 0.1),
        "w_out_a": nrm(ks[17], (DEPTH, W_A, D_MODEL), BETA * W_A ** -0.5),
        "w_out_b": nrm(ks[18], (DEPTH, W_B, D_MODEL), BETA * W_B ** -0.5),
        "w_out_c": nrm(ks[19], (DEPTH, W_C, D_MODEL), BETA * W_C ** -0.5),
        "w_out_d": nrm(ks[20], (DEPTH, W_D, D_MODEL), BETA * W_D ** -0.5),
        "b_gate": nrm(ks[21], (DEPTH, N_BRANCH * D_MODEL), 0.01),
        "w_o": nrm(ks[22], (DEPTH, D_MODEL, D_MODEL), BETA * D_MODEL ** -0.5),
        "ln_g": 1.0 + nrm(ks[23], (DEPTH, D_MODEL), 0.01),
        "ln_b": nrm(ks[24], (DEPTH, D_MODEL), 0.01),
    }


def reference(x_prompt, x_sample, state_conv_a, state_conv_b, cache_kv_w128, cache_kv_w512,
              cache_kv_w2048, w_in, conv_a_w, conv_b_w, conv_b_bias, ln_b_g, ln_b_b, ln_d_g,
              ln_d_b, w_s, b_s, w_out_a, w_out_b, w_out_c, w_out_d, b_gate, w_o, ln_g, ln_b):
    params = (w_in, conv_a_w, conv_b_w, conv_b_bias, ln_b_g, ln_b_b, ln_d_g, ln_d_b, w_s, b_s,
              w_out_a, w_out_b, w_out_c, w_out_d, b_gate, w_o, ln_g, ln_b)
    caches = (cache_kv_w128, cache_kv_w512, cache_kv_w2048)
    xp, xs = x_prompt, x_sample
    bp, tp, _ = xp.shape
    ts = xs.shape[1]
    pos_p = jnp.arange(tp, dtype=jnp.int32)
    pos_s = PAST_LEN + jnp.arange(ts, dtype=jnp.int32)

    ca_p_l, ca_s_l, cb_p_l, cb_s_l, dv_s_l = [], [], [], [], []
    kv_p_l = [[] for _ in range(N_ATTN_GROUPS)]
    kv_s_l = [[] for _ in range(N_ATTN_GROUPS)]
    for l in range(DEPTH):
        lw = tuple(p[l] for p in params)

        def attend_prompt(g, q, k, v):
            span = WINDOWS[g] // DILATIONS[g]
            o, lse = _dilated_prompt(q, k, v, DILATIONS[g], span)
            keep = min(WINDOWS[g], tp)
            return o, lse, jnp.stack([k, v], axis=2)[:, -keep:]

        def attend_sample(g, q, k, v, l=l):
            buf = caches[g][l].astype(k.dtype)
            n_hist = buf.shape[1]
            k_ext = jnp.concatenate([buf[:, :, 0], k], axis=1)
            v_ext = jnp.concatenate([buf[:, :, 1], v], axis=1)
            span = WINDOWS[g] // DILATIONS[g]
            o, lse = _dilated_decode(q, k_ext, v_ext, n_hist, DILATIONS[g], span)
            return o, lse, jnp.stack([k, v], axis=2)

        hist_a0 = jnp.zeros((bp, CONV_A - 1, W_A), xp.dtype)
        hist_b0 = jnp.zeros((bp, CONV_B - 1, W_B), xp.dtype)
        xp, ca_p, cb_p, kv_p, _ = _layer(xp, pos_p, hist_a0, hist_b0, attend_prompt, lw)
        xs, ca_s, cb_s, kv_s, vn_s = _layer(xs, pos_s, state_conv_a[l], state_conv_b[l],
                                            attend_sample, lw)
        ca_p_l.append(ca_p)
        ca_s_l.append(ca_s)
        cb_p_l.append(cb_p)
        cb_s_l.append(cb_s)
        dv_s_l.append(vn_s)
        for g in range(N_ATTN_GROUPS):
            kv_p_l[g].append(kv_p[g])
            kv_s_l[g].append(kv_s[g])

    new_conv_a_prompt = jnp.stack(ca_p_l)
    new_conv_a_sample = jnp.stack(ca_s_l)
    new_conv_b_prompt = jnp.stack(cb_p_l)
    new_conv_b_sample = jnp.stack(cb_s_l)
    new_kv_w128_prompt = jnp.stack(kv_p_l[0])
    new_kv_w128_sample = jnp.stack(kv_s_l[0])
    new_kv_w512_prompt = jnp.stack(kv_p_l[1])
    new_kv_w512_sample = jnp.stack(kv_s_l[1])
    new_kv_w2048_prompt = jnp.stack(kv_p_l[2])
    new_kv_w2048_sample = jnp.stack(kv_s_l[2])
    new_gmlp_v_sample = jnp.stack(dv_s_l)
    return (xp, xs, new_conv_a_prompt, new_conv_a_sample, new_conv_b_prompt, new_conv_b_sample,
            new_kv_w128_prompt, new_kv_w128_sample, new_kv_w512_prompt, new_kv_w512_sample,
            new_kv_w2048_prompt, new_kv_w2048_sample, new_gmlp_v_sample)
```

```python
import numpy as np
from contextlib import ExitStack
import concourse.bass as bass
import concourse.mybir as mybir
from concourse.bass_utils import run_bass_kernel_spmd

F32 = mybir.dt.float32
BF16 = mybir.dt.bfloat16
AF = mybir.ActivationFunctionType
ALU = mybir.AluOpType

HD = 128
NH = 12
WINDOWS = (128, 512, 2048)
DILS = (1, 4, 16)
LN_EPS = 1e-5
ROPE_THETA = 10000.0
SELF_WAIT = True


class Cfg:
    def __init__(self, D=4096, SEQ=2048, L=2, NS=2, PAST=16384, NCORES=4):
        self.D, self.SEQ, self.L, self.NS, self.PAST, self.NCORES = D, SEQ, L, NS, PAST, NCORES
        self.KC = D // 128
        self.a = D // 512
        self.WA = D // 4
        a = self.a
        self.cA = (0, a, 2 * a, 3 * a)
        self.cB = (4 * a, 5 * a, 6 * a)
        self.cQ, self.cK, self.cV = 7 * a, 7 * a + 12, 7 * a + 24
        self.cZ = 7 * a + 36
        self.cD = (7 * a + 40, 8 * a + 40, 9 * a + 40)
        self.cG = 10 * a + 40
        self.NCH = self.cG + 4 * self.KC
        self.KCO = 3 * a + 4
        self.TP = 256
        self.NP = SEQ // self.TP
        self.QT = 512
        self.NQ = SEQ // 512
        self.ALPHA = (2 * L) ** 0.25
        o = 0
        self.pp = {}
        for name, n in (("caw", 3 * a), ("cbw", 31 * a), ("cbb", a), ("lbg", a), ("lbb", a),
                        ("ldg", a), ("ldb", a), ("bg", 4 * self.KC), ("lg", self.KC), ("lb", self.KC),
                        ("ws00", a), ("bs0", a)):
            self.pp[name] = o
            o += n
        self.NPP = o
        self.keep = [min(w, SEQ) for w in WINDOWS]
        self.LIMIT_L = L
        self.DO_ATTN = True
        self.DO_SAMPLE = True
        self.clen = [min(w, PAST) for w in WINDOWS]


class Buf:
    __slots__ = ("name", "w", "r")

    def __init__(self, name):
        self.name, self.w, self.r = name, None, {}


class Eng:
    def __init__(self, name, sem):
        self.name, self.sem, self.cnt, self.seen, self.prog = name, sem, 0, {}, []
        self.dsems, self.dnext = [], 0


class Tracker:
    def __init__(self, nc, stack):
        self.nc, self.stack = nc, stack
        self.E = {}
        for n in ("pe", "act", "dve", "pool", "sp"):
            self.E[n] = Eng(n, stack.enter_context(nc.semaphore("s_" + n)))
        for n, k in (("sp", 12), ("pool", 8), ("act", 4)):
            for i in range(k):
                self.E[n].dsems.append([stack.enter_context(nc.semaphore("d_%s%d" % (n, i))), 0])
        self.final = []

    def _wait(self, e, toks):
        need = {}
        for sem, val in toks:
            if sem is e.sem and not SELF_WAIT:
                continue
            k = id(sem)
            if e.seen.get(k, 0) < val and need.get(k, (None, 0))[1] < val:
                need[k] = (sem, val)
        for k, (sem, val) in need.items():
            e.seen[k] = val
            e.prog.append(lambda h, sem=sem, val=val: h.wait_ge(sem, val))

    def _deps(self, reads, writes):
        toks = []
        for b in reads:
            if b.w:
                toks.append(b.w)
        for b in writes:
            if b.w:
                toks.append(b.w)
            toks.extend(b.r.values())
        return toks

    def _mark(self, tok, reads, writes):
        for b in reads:
            b.r[id(tok[0])] = tok
        for b in writes:
            b.w, b.r = tok, {}

    def op(self, en, fn, reads=(), writes=()):
        self.group(en, [fn], reads, writes)

    def group(self, en, fns, reads=(), writes=()):
        e = self.E[en]
        toks = self._deps(reads, writes)
        if en == "pe":
            toks = [t for t in toks if t[0] is not e.sem]
        self._wait(e, toks)
        e.cnt += 1
        sem, cnt = e.sem, e.cnt
        for f in fns[:-1]:
            e.prog.append(lambda h, f=f: f(h))
        last = fns[-1]
        e.prog.append(lambda h, f=last, sem=sem: f(h).then_inc(sem, 1))
        self._mark((sem, cnt), reads, writes)

    def dma(self, qn, out, in_, reads=(), writes=(), **kw):
        e = self.E[qn]
        slot = e.dsems[e.dnext % len(e.dsems)]
        e.dnext += 1
        toks = self._deps(reads, writes)
        if slot[1] > 0:
            toks.append((slot[0], slot[1]))
        self._wait(e, toks)
        slot[1] += 16
        sem, val = slot[0], slot[1]
        e.prog.append(lambda h, sem=sem: h.dma_start(out=out, in_=in_, **kw).then_inc(sem, 16))
        self._mark((sem, val), reads, writes)
        return (sem, val)

    def finish(self, bufs):
        e = self.E["sp"]
        toks = []
        for b in bufs:
            if b.w:
                toks.append(b.w)
        for q in ("sp", "pool", "act"):
            for s in self.E[q].dsems:
                if s[1] > 0:
                    toks.append((s[0], s[1]))
        self._wait(e, toks)

    def replay(self):
        nc = self.nc
        with nc.Block() as block:
            @block.tensor
            def _(h):
                for f in self.E["pe"].prog:
                    f(h)

            @block.scalar
            def _(h):
                for f in self.E["act"].prog:
                    f(h)

            @block.vector
            def _(h):
                for f in self.E["dve"].prog:
                    f(h)

            @block.gpsimd
            def _(h):
                for f in self.E["pool"].prog:
                    f(h)

            @block.sync
            def _(h):
                for f in self.E["sp"].prog:
                    f(h)


def build(cfg):
    c = cfg
    D, SEQ, L, NS, KC, a, TP, NP = c.D, c.SEQ, c.L, c.NS, c.KC, c.a, c.TP, c.NP
    NCH, KCO = c.NCH, c.KCO
    nc = bass.Bass("TRN2", target_bir_lowering=False)
    stack = ExitStack()
    T = Tracker(nc, stack)

    def din(name, shape, dt=F32):
        return nc.dram_tensor(name, list(shape), dt, kind="ExternalInput").ap()

    def dout(name, shape, dt=F32):
        return nc.dram_tensor(name, list(shape), dt, kind="ExternalOutput").ap()

    def dscr(name, shape, dt):
        return nc.dram_tensor(name, list(shape), dt, kind="Internal").ap()

    def sb(name, shape, dt=F32):
        return stack.enter_context(nc.sbuf_tensor(name, list(shape), dt))

    def ps(name, shape=(128, 512), dt=F32):
        return stack.enter_context(nc.psum_tensor(name, list(shape), dt))

    xpT = din("xpT", [KC, 128, SEQ])
    xsT = din("xsT", [KC, 128, NS])
    w_in = din("w_in", [L, NCH, 128, KC * 128])
    w_out = din("w_out", [L, KC, 128, KCO * 128])
    w_o = din("w_o", [L, KC, 128, KC * 128])
    ppd = din("pp", [128, L * c.NPP])
    wsT = din("wsT", [L, a, 128, 128])
    bsr = din("bsr", [L, a, 128, 128])
    cst = din("cst", [128, 4 * 128])
    ropep = din("ropep", [128, 2, SEQ])
    ropes = din("ropes", [128, 2, 1])
    sca = din("sca", [128, L, a, 2, NS])
    scb = din("scb", [128, L, a, 30, NS])
    cache = [din("cache%d" % g, [L, NS, c.clen[g], 2, 4, 128]) for g in range(3)]

    o_ypT = dout("o_ypT", [KC, 128, SEQ])
    o_ysT = dout("o_ysT", [KC, 128, NS])
    o_ca_p = dout("o_ca_p", [128, L, a, 2])
    o_ca_s = dout("o_ca_s", [128, L, a, 2, NS])
    o_cb_p = dout("o_cb_p", [128, L, a, 30])
    o_cb_s = dout("o_cb_s", [128, L, a, 30, NS])
    o_kv_p = [dout("o_kv_p%d" % g, [L, 2, 4, 128, c.keep[g]]) for g in range(3)]
    o_kv_s = [dout("o_kv_s%d" % g, [L, 2, 4, 128, NS]) for g in range(3)]
    o_gv_s = dout("o_gv_s", [128, L, a, NS])
    out_bufs = [Buf("out%d" % i) for i in range(16)]
    OB = dict(yp=out_bufs[0], ys=out_bufs[1], ca_p=out_bufs[2], ca_s=out_bufs[3], cb_p=out_bufs[4],
              cb_s=out_bufs[5], kv_p=out_bufs[6], kv_s=out_bufs[7], gv=out_bufs[8])

    WG = 32
    _wbin = [[dscr("wb_in%d_%d" % (l, gi), [min(WG, NCH - gi * WG), 128, KC * 128], BF16)
              for gi in range((NCH + WG - 1) // WG)] for l in range(L)]

    class _WbIn:
        def __getitem__(self, lm):
            l, m = lm
            return _wbin[l][m // WG][m % WG]
    wb_in = _WbIn()
    _wbout = [dscr("wb_out%d" % l, [KC, 128, KCO * 128], BF16) for l in range(L)]
    _wbo = [dscr("wb_o%d" % l, [KC, 128, KC * 128], BF16) for l in range(L)]

    class _Wb2:
        def __init__(self, ts):
            self.ts = ts

        def __getitem__(self, lj):
            return self.ts[lj[0]][lj[1]]
    wb_out = _Wb2(_wbout)
    wb_o = _Wb2(_wbo)
    actT = [dscr("actT%d" % i, [KC, 128, SEQ], F32) for i in range(max(1, L - 1))]
    actsT = [dscr("actsT%d" % i, [KC, 128, NS], F32) for i in range(max(1, L - 1))]
    rT = dscr("rT", [KC, 128, TP], F32)
    xb = [dscr("xb%d" % i, [KC, 128, SEQ], BF16) for i in range(L)]
    xbs = [dscr("xbs%d" % i, [KC, 128, NS], BF16) for i in range(L)]
    B_xb = [[Buf("xb%d_%d" % (i, p)) for p in range(NP)] for i in range(L)]
    B_xbs = [Buf("xbs%d" % i) for i in range(L)]
    qT_s = [dscr("qT_s%d" % g, [128, 4, SEQ], BF16) for g in range(3)]
    kT_s = [dscr("kT_s%d" % g, [128, 4, SEQ], BF16) for g in range(3)]
    v_s = [dscr("v_s%d" % g, [4, SEQ, 128], BF16) for g in range(3)]
    B_wb = {}
    B_act = [[Buf("actT%d_%d" % (i, p)) for p in range(NP)] for i in range(len(actT))]
    B_acts = [Buf("actsT%d" % i) for i in range(len(actsT))]
    B_rT = [Buf("rT%d" % j) for j in range(KC)]
    B_q = [[Buf("q%d_%d" % (g, h)) for h in range(4)] for g in range(3)]
    B_k = [[Buf("k%d_%d" % (g, h)) for h in range(4)] for g in range(3)]
    B_v = [[Buf("v%d_%d" % (g, h)) for h in range(4)] for g in range(3)]

    NSLOT = 4
    wslot = [sb("wslot%d" % i, [128, KC * 128], BF16) for i in range(NSLOT)]
    B_ws = [Buf("ws%d" % i) for i in range(NSLOT)]
    wso = sb("wso", [128, KCO * 128], BF16)
    B_wso = Buf("wso")
    xT = sb("xT", [128, KC, TP], BF16)
    B_xT = Buf("xT")
    yT = sb("yT", [128, KCO, TP], BF16)
    B_y = [Buf("y%d" % i) for i in range(KCO)]
    mT = sb("mT", [128, KC, TP], BF16)
    B_m = [Buf("m%d" % i) for i in range(KC)]
    ycall = sb("ycall", [128, 4, SEQ], BF16)
    B_yc = [[Buf("yc%d_%d" % (h, p)) for p in range(c.NQ)] for h in range(4)]
    ycs = sb("ycs", [128, 4, NS], BF16)
    B_ycs = Buf("ycs")
    ppt = sb("ppt", [128, L * c.NPP])
    B_pp = Buf("pp")
    cstt = sb("cstt", [128, 4 * 128])
    B_cst = Buf("cst")
    identb = sb("identb", [128, 128], BF16)
    onesb = sb("onesb", [128, 128], BF16)
    onesf = sb("onesf", [128, 128])
    ULb = sb("ULb", [128, 256], BF16)
    Mg2 = sb("Mg2", [128, 4, 512], BF16)
    B_const = Buf("const")
    rope = sb("rope", [128, 2, TP])
    B_rope = Buf("rope")
    ropest = sb("ropest", [128, 2, 1])
    NW = 10
    wk = [sb("wk%d" % i, [128, TP]) for i in range(NW)]
    B_wk = [Buf("wk%d" % i) for i in range(NW)]
    wkb = [sb("wkb%d" % i, [128, TP], BF16) for i in range(8)]
    B_wkb = [Buf("wkb%d" % i) for i in range(8)]
    cbuf = sb("cbuf", [128, a, TP])
    B_cb = [Buf("cb%d" % i) for i in range(a)]
    ext = sb("ext", [128, (30 + TP)])
    B_ext = Buf("ext")
    hista = sb("hista", [128, a, 2 * NS])
    histb = sb("histb", [128, a, 30 * NS])
    B_ha = [Buf("ha%d" % i) for i in range(a)]
    B_hb = [Buf("hb%d" % i) for i in range(a)]
    wsm = sb("wsm", [128, a, 128], BF16)
    bsm = sb("bsm", [128, a, 128])
    B_wsm = Buf("wsm")
    vnT = sb("vnT", [128, a, TP], BF16)
    B_vn = [Buf("vn%d" % i) for i in range(a)]
    stat = [sb("stat%d" % i, [128, TP]) for i in range(3)]
    B_stat = [Buf("stat%d" % i) for i in range(3)]
    qd = [sb("qd%d" % g, [128, SEQ], BF16) for g in range(3)]
    kd = [sb("kd%d" % g, [128, SEQ], BF16) for g in range(3)]
    vd = [sb("vd%d" % g, [128, SEQ // 128, 128], BF16) for g in range(3)]
    B_qd = [Buf("qd%d" % g) for g in range(3)]
    B_kd = [Buf("kd%d" % g) for g in range(3)]
    B_vd = [Buf("vd%d" % g) for g in range(3)]
    pT = [sb("pT%d" % i, [128, 512], BF16) for i in range(3)]
    B_pT = [Buf("pT%d" % i) for i in range(3)]
    vst = sb("vst", [128, 16, 128], BF16)
    B_vst = Buf("vst")
    rdt = sb("rdt", [128, 512])
    B_rdt = Buf("rdt")
    sm = sb("sm", [128, 1024])
    B_sm = Buf("sm")
    smb = sb("smb", [128, 1024], BF16)
    B_smb = Buf("smb")

    pz = [ps("pz%d" % i) for i in range(3)]
    B_pz = [Buf("pz%d" % i) for i in range(3)]
    pst = [ps("pst%d" % i) for i in range(2)]
    B_pst = [Buf("pst%d" % i) for i in range(2)]
    pm = [ps("pm%d" % i) for i in range(2)]
    B_pm = [Buf("pm%d" % i) for i in range(2)]
    pmb = ps("pmb", (128, 1024), BF16)
    B_pmb = Buf("pmb")
    rot = {"pz": 0, "pm": 0, "wk": 0, "wkb": 0, "ws": 0, "pT": 0}

    def nxt(kind, n):
        i = rot[kind] % n
        rot[kind] += 1
        return i

    def PP(l, name, j):
        o = l * c.NPP + c.pp[name] + j
        return ppt[:, o:o + 1]

    T.dma("sp", ppt[:], ppd, writes=[B_pp])
    T.dma("sp", cstt[:], cst, writes=[B_cst])
    T.op("dve", lambda h: h.tensor_copy(out=identb[:], in_=cstt[:, 0:128]), [B_cst], [B_const])
    T.op("dve", lambda h: h.memset(onesb[:], 1.0), [], [B_const])
    T.op("dve", lambda h: h.memset(onesf[:], 1.0), [], [B_const])
    T.op("dve", lambda h: h.tensor_copy(out=ULb[:], in_=cstt[:, 128:384]), [B_cst], [B_const])
    for mi in range(4):
        for r in range(16):
            T.op("dve", lambda h, mi=mi, r=r: h.tensor_copy(
                out=Mg2[:, mi, r * 32:(r + 1) * 32], in_=cstt[:, 128 + mi * 32:128 + mi * 32 + 32]),
                [B_cst], [B_const])
    identf = cstt[:, 0:128]
    swapf = cstt[:, 384:512]

    def cast_w(src, dst, key):
        b = Buf(str(key))
        B_wb[key] = b
        T.dma("pool", dst, src, writes=[b], max_dma_last_dim=4096)

    def cast_layer(l):
        for m in list(range(c.cQ, c.cQ + 36)) + [m for m in range(NCH) if not (c.cQ <= m < c.cQ + 36)]:
            cast_w(w_in[l, m], wb_in[l, m], ("in", l, m))
        for j in range(KC):
            cast_w(w_out[l, j], wb_out[l, j], ("out", l, j))
        for j in range(KC):
            cast_w(w_o[l, j], wb_o[l, j], ("o", l, j))

    for kc in range(KC):
        T.dma("pool", xb[0][kc], xpT[kc], writes=B_xb[0], max_dma_last_dim=4096)
    for kc in range(KC):
        T.dma("pool", xbs[0][kc], xsT[kc], writes=[B_xbs[0]], max_dma_last_dim=4096)
    cast_q = []

    def cast_w_lazy(src, dst, key):
        cast_q.append((src, dst, key))

    def pump(n=1, need=None):
        while cast_q and (n > 0 or (need is not None and need not in B_wb)):
            s_, d_, k_ = cast_q.pop(0)
            cast_w(s_, d_, k_)
            n -= 1

    _cw = cast_w
    for l in range(L):
        for m in list(range(c.cQ, c.cQ + 36)) + [m for m in range(NCH) if not (c.cQ <= m < c.cQ + 36)]:
            cast_w_lazy(w_in[l, m], wb_in[l, m], ("in", l, m))
        for j in range(KC):
            cast_w_lazy(w_out[l, j], wb_out[l, j], ("out", l, j))
        for j in range(KC):
            cast_w_lazy(w_o[l, j], wb_o[l, j], ("o", l, j))

    def lin(key, wsrc, kcn, rhs_fn, rhs_bufs, ncol, consume):
        pump(2, need=key)
        si = nxt("ws", NSLOT)
        T.dma("sp", wslot[si][:, 0:kcn * 128], wsrc, reads=[B_wb[key]], writes=[B_ws[si]])
        pi = nxt("pz", 3)
        fns = []
        for kc in range(kcn):
            fns.append(lambda h, kc=kc, si=si, pi=pi: h.matmul(
                pz[pi][:, 0:ncol], lhsT=wslot[si][:, kc * 128:(kc + 1) * 128], rhs=rhs_fn(kc),
                start=(kc == 0), stop=(kc == kcn - 1)))
        T.group("pe", fns, reads=[B_ws[si]] + list(rhs_bufs), writes=[B_pz[pi]])
        consume(pz[pi][:, 0:ncol], B_pz[pi])

    def lin_in(l, m, ncol, consume):
        lin(("in", l, m), wb_in[l, m], KC, lambda kc: xT[:, kc, 0:ncol], [B_xT], ncol, consume)

    def evac(psrc, bsrc, func=AF.Copy, dt=F32, scale=None, bias=None):
        ncol = psrc.shape[-1]
        if dt == F32:
            i = nxt("wk", NW)
            dst, b = wk[i][:, 0:ncol], B_wk[i]
        else:
            i = nxt("wkb", 8)
            dst, b = wkb[i][:, 0:ncol], B_wkb[i]
        kw = {}
        if scale is not None:
            kw["scale"] = scale
        if bias is not None:
            kw["bias"] = bias
        rd = [bsrc, B_pp] if (scale is not None or bias is not None) else [bsrc]
        T.op("act", lambda h: h.activation(out=dst, in_=psrc, func=func, **kw), rd, [b])
        return dst, b

    def newwk(ncol):
        i = nxt("wk", NW)
        return wk[i][:, 0:ncol], B_wk[i]

    def ln_stats(tiles, ncol, nch_total):
        n = len(tiles)
        fns0, fns1 = [], []
        sq = []
        for i, (ap, b) in enumerate(tiles):
            s_ap, s_b = newwk(ncol)
            T.op("act", lambda h, ap=ap, s_ap=s_ap: h.activation(out=s_ap, in_=ap, func=AF.Square), [b], [s_b])
            sq.append((s_ap, s_b))
            T.op("pe", lambda h, ap=ap, i=i: h.matmul(pst[0][:, 0:ncol], lhsT=onesf[:], rhs=ap,
                                                      start=(i == 0), stop=(i == n - 1)),
                 [b, B_const], [B_pst[0]])
            T.op("pe", lambda h, s_ap=s_ap, i=i: h.matmul(pst[1][:, 0:ncol], lhsT=onesf[:], rhs=s_ap,
                                                          start=(i == 0), stop=(i == n - 1)),
                 [s_b, B_const], [B_pst[1]])
        mean, rstd, tmp = stat[0][:, 0:ncol], stat[1][:, 0:ncol], stat[2][:, 0:ncol]
        inv = 1.0 / nch_total
        T.op("act", lambda h: h.activation(out=mean, in_=pst[0][:, 0:ncol], func=AF.Copy, scale=inv),
             [B_pst[0]], [B_stat[0]])
        T.op("dve", lambda h: h.tensor_tensor(out=tmp, in0=mean, in1=mean, op=ALU.mult), [B_stat[0]], [B_stat[2]])
        T.op("dve", lambda h: h.scalar_tensor_tensor(out=tmp, in0=pst[1][:, 0:ncol], scalar=inv, in1=tmp,
                                                     op0=ALU.mult, op1=ALU.subtract),
             [B_pst[1], B_stat[2]], [B_stat[2]])
        T.op("dve", lambda h: h.tensor_scalar(out=tmp, in0=tmp, scalar1=LN_EPS, scalar2=None, op0=ALU.add),
             [B_stat[2]], [B_stat[2]])
        T.op("act", lambda h: h.activation(out=tmp, in_=tmp, func=AF.Sqrt), [B_stat[2]], [B_stat[2]])
        T.op("dve", lambda h: h.reciprocal(out=rstd, in_=tmp), [B_stat[2]], [B_stat[1]])
        return mean, rstd

    def normalize(ap, b, mean, rstd, out_ap, out_b):
        T.op("dve", lambda h: h.tensor_tensor(out=out_ap, in0=ap, in1=mean, op=ALU.subtract),
             [b, B_stat[0]], [out_b])
        T.op("dve", lambda h: h.tensor_tensor(out=out_ap, in0=out_ap, in1=rstd, op=ALU.mult),
             [out_b, B_stat[1]], [out_b])

    def load_xT(src_t, bsrc, col0, ncol):
        step = max(1, KC // 4)
        for k0 in range(0, KC, step):
            T.dma("sp", xT[:, k0:k0 + step, 0:ncol],
                  src_t[k0:k0 + step, :, col0:col0 + ncol].rearrange("k p t -> p k t"), reads=[bsrc], writes=[B_xT])

    def rope_apply(xap, xb, ncol, cos, sin):
        pi = nxt("pm", 2)
        T.op("pe", lambda h: h.matmul(pm[pi][:, 0:ncol], lhsT=swapf, rhs=xap, start=True, stop=True),
             [xb, B_cst], [B_pm[pi]])
        t1, b1 = newwk(ncol)
        T.op("dve", lambda h: h.tensor_tensor(out=t1, in0=xap, in1=cos, op=ALU.mult), [xb, B_rope], [b1])
        t2, b2 = newwk(ncol)
        T.op("dve", lambda h: h.tensor_tensor(out=t2, in0=pm[pi][:, 0:ncol], in1=sin, op=ALU.mult),
             [B_pm[pi], B_rope], [b2])
        T.op("dve", lambda h: h.tensor_tensor(out=t1, in0=t1, in1=t2, op=ALU.add), [b1, b2], [b1])
        return t1, b1

    def phase1_prompt(l, p, src_t, bsrc):
        t0 = p * TP
        load_xT(src_t, bsrc, t0, TP)
        T.dma("sp", rope[:], ropep[:, :, t0:t0 + TP], writes=[B_rope])
        cos, sin = rope[:, 0, :], rope[:, 1, :]
        for which, cbase, scr, Bs in (("q", c.cQ, qT_s, B_q), ("k", c.cK, kT_s, B_k)):
            for hh in range(NH):
                g, hs = hh // 4, hh % 4
                dil = DILS[g]
                res = {}
                lin_in(l, cbase + hh, TP, lambda pa, pb: res.update(x=evac(pa, pb)))
                xap, xb = res["x"]
                rt, rb = rope_apply(xap, xb, TP, cos, sin)
                if which == "k":
                    keep = c.keep[g]
                    lo = max(t0, SEQ - keep)
                    if lo < t0 + TP:
                        T.dma("sp", o_kv_p[g][l, 0, hs, :, lo - (SEQ - keep):t0 + TP - (SEQ - keep)],
                              rt[:, lo - t0:TP], reads=[rb], writes=[OB["kv_p"]])
                i = nxt("wkb", 8)
                dst, db = wkb[i], B_wkb[i]
                T.op("act", lambda h, dst=dst, rt=rt: h.activation(out=dst[:, 0:TP], in_=rt, func=AF.Copy), [rb], [db])
                T.dma("sp", scr[g][:, hs, t0:t0 + TP], dst[:, 0:TP], reads=[db], writes=[Bs[g][hs]])
        for hh in range(NH):
            g, hs = hh // 4, hh % 4
            dil = DILS[g]
            res = {}
            lin_in(l, c.cV + hh, TP, lambda pa, pb: res.update(x=evac(pa, pb)))
            xap, xb = res["x"]
            keep = c.keep[g]
            lo = max(t0, SEQ - keep)
            if lo < t0 + TP:
                T.dma("sp", o_kv_p[g][l, 1, hs, :, lo - (SEQ - keep):t0 + TP - (SEQ - keep)],
                      xap[:, lo - t0:TP], reads=[xb], writes=[OB["kv_p"]])
            i = nxt("wkb", 8)
            vb16, vbb = wkb[i], B_wkb[i]
            T.op("act", lambda h, vb16=vb16, xap=xap: h.activation(out=vb16[:], in_=xap, func=AF.Copy), [xb], [vbb])
            nm = TP // dil
            if dil == 1:
                blocks = [(vb16[:, j * 128:(j + 1) * 128], 128, j) for j in range(TP // 128)]
            else:
                v3 = vb16[:].rearrange("p (m r) -> p r m", r=dil)
                blocks = [(v3[:, r, :], nm, r) for r in range(dil)]
            for s0 in range(0, len(blocks), 8):
                sub = blocks[s0:s0 + 8]
                fns = [(lambda h, src=src, n=n, jj=jj: h.transpose(pmb[0:n, jj * 128:(jj + 1) * 128], src, identb[:]))
                       for jj, (src, n, j) in enumerate(sub)]
                T.group("pe", fns, [vbb, B_const], [B_pmb])
                n = sub[0][1]
                nb = len(sub)
                T.op("act", lambda h, n=n, nb=nb, s0=s0: h.activation(
                    out=vst[0:n, s0:s0 + nb, :], in_=pmb[0:n, 0:nb * 128].rearrange("p (b d) -> p b d", d=128),
                    func=AF.Copy), [B_pmb], [B_vst])
            M = SEQ // dil
            m0 = t0 // dil
            if dil == 1:
                T.dma("sp", v_s[g][hs, t0:t0 + TP, :].rearrange("(j p) d -> p j d", p=128), vst[:, 0:TP // 128, :],
                      reads=[B_vst], writes=[B_v[g][hs]])
            else:
                T.dma("sp", v_s[g][hs].rearrange("(r m) d -> m r d", r=dil)[m0:m0 + nm, :, :], vst[0:nm, 0:dil, :],
                      reads=[B_vst], writes=[B_v[g][hs]])

    SCALE = 1.0 / np.sqrt(HD)

    def phase2(l):
        for hs in range(4):
            for g in range(3):
                T.dma("sp", qd[g][:], qT_s[g][:, hs, :], reads=[B_q[g][hs]], writes=[B_qd[g]])
                T.dma("sp", kd[g][:], kT_s[g][:, hs, :], reads=[B_k[g][hs]], writes=[B_kd[g]])
                T.dma("sp", vd[g][:], v_s[g][hs].rearrange("(j p) d -> p j d", p=128), reads=[B_v[g][hs]],
                      writes=[B_vd[g]])
            for Qb in range(c.NQ):
                jobs = []
                t0 = Qb * 512
                for kb in range(4 * Qb - 1, 4 * Qb + 4):
                    if kb < 0:
                        continue
                    q_lo = max(kb, 4 * Qb)
                    q_hi = min(kb + 1, 4 * Qb + 3)
                    nq = (q_hi - q_lo + 1) * 128
                    moff = 0 if q_lo == kb else 128
                    jobs.append((0, kd[0][:, kb * 128:(kb + 1) * 128], 128, qd[0][:, q_lo * 128:q_lo * 128 + nq], nq,
                                 ULb[:, moff:moff + nq], vd[0][:, kb, :],
                                 lambda P, q_lo=q_lo, nq=nq, t0=t0: P[:, q_lo * 128 - t0:q_lo * 128 - t0 + nq]))
                M1 = SEQ // 4
                k1v = kd[1][:].rearrange("p (m r) -> p r m", r=4)
                q1v = qd[1][:].rearrange("p (m r) -> p r m", r=4)
                for r in range(4):
                    for kbm in (Qb - 1, Qb):
                        if kbm < 0:
                            continue
                        moff = 0 if kbm == Qb else 128
                        base = r * M1
                        jobs.append((1, k1v[:, r, kbm * 128:(kbm + 1) * 128], 128,
                                     q1v[:, r, Qb * 128:(Qb + 1) * 128], 128,
                                     ULb[:, moff:moff + 128], vd[1][:, (base + kbm * 128) // 128, :],
                                     lambda P, r=r: P.rearrange("p (m r) -> p r m", r=4)[:, r, :]))
                M2 = SEQ // 16
                m0 = 32 * Qb
                nk = m0 + 32
                g2jobs = []
                k2v = kd[2][:].rearrange("p (m r) -> p r m", r=16)
                q2v = qd[2][:].rearrange("p (m r) -> p r m", r=16)
                for r in range(16):
                    base = r * M2
                    g2jobs.append((k2v[:, r, 0:nk], q2v[:, r, m0:m0 + 32],
                                   vd[2][0:nk, base // 128, :] if M2 == 128 else None, r, base))
                plist = []
                for (g, lk, nkk, rq, nq, mk, vv, oc) in jobs:
                    pi = nxt("pz", 3)
                    T.op("pe", lambda h, lk=lk, rq=rq, nkk=nkk, nq=nq, pi=pi: h.matmul(
                        pz[pi][0:nkk, 0:nq], lhsT=lk, rhs=rq, start=True, stop=True),
                        [B_kd[g], B_qd[g]], [B_pz[pi]])
                    ti = nxt("pT", 3)
                    T.op("act", lambda h, pi=pi, ti=ti, nkk=nkk, nq=nq: h.activation(
                        out=pT[ti][0:nkk, 0:nq], in_=pz[pi][0:nkk, 0:nq], func=AF.Exp, scale=float(SCALE)),
                        [B_pz[pi]], [B_pT[ti]])
                    T.op("dve", lambda h, ti=ti, nkk=nkk, nq=nq, mk=mk: h.tensor_tensor(
                        out=pT[ti][0:nkk, 0:nq], in0=pT[ti][0:nkk, 0:nq], in1=mk, op=ALU.mult),
                        [B_pT[ti], B_const], [B_pT[ti]])
                    plist.append((g, ti, nkk, nq, vv, oc))
                    _pv(plist, Qb, hs, last=False)
                    plist = []
                pi = nxt("pz", 3)
                fns = []
                for (lk, rq, vv, r, base) in g2jobs:
                    fns.append(lambda h, lk=lk, rq=rq, r=r, pi=pi, nk=nk: h.matmul(
                        pz[pi][0:nk, r * 32:(r + 1) * 32], lhsT=lk, rhs=rq, start=True, stop=True))
                T.group("pe", fns, [B_kd[2], B_qd[2]], [B_pz[pi]])
                ti = nxt("pT", 3)
                T.op("act", lambda h, pi=pi, ti=ti, nk=nk: h.activation(out=pT[ti][0:nk, :], in_=pz[pi][0:nk, :],
                                                                 func=AF.Exp, scale=float(SCALE)),
                     [B_pz[pi]], [B_pT[ti]])
                T.op("dve", lambda h, ti=ti, Qb=Qb, nk=nk: h.tensor_tensor(out=pT[ti][0:nk, :], in0=pT[ti][0:nk, :],
                                                                    in1=Mg2[0:nk, Qb % 4, :], op=ALU.mult),
                     [B_pT[ti], B_const], [B_pT[ti]])
                pl = []
                for (lk, rq, vv, r, base) in g2jobs:
                    vblk = vd[2][0:nk, (base * 1) // 128, :] if False else None
                    pl.append((2, ti, nk, 32, ("g2", r, base), (lambda P, r=r: P.rearrange("p (m r) -> p r m", r=16)[:, r, :]), r))
                _pv_g2(pl, ti, nk, Qb, hs)
                rd_ap, rd_b = rdt[:], B_rdt
                T.op("dve", lambda h, rd_ap=rd_ap: h.reciprocal(out=rd_ap, in_=pst[1][:]), [B_pst[1]], [rd_b])
                T.op("dve", lambda h, rd_ap=rd_ap, hs=hs, t0=t0: h.tensor_tensor(
                    out=ycall[:, hs, t0:t0 + 512], in0=pst[0][:], in1=rd_ap, op=ALU.mult),
                    [B_pst[0], rd_b], [B_yc[hs][Qb]])
                acc_state["first"] = True

    acc_state = {"first": True}

    def _acc_mm(lhsT_v, nkk, rhs, ocf, reads):
        first = acc_state["first"]
        acc_state["first"] = False
        T.op("pe", lambda h: h.matmul(ocf(pst[0][:]), lhsT=lhsT_v, rhs=rhs, start=first, stop=False,
                                      skip_group_check=True), reads, [B_pst[0]])
        T.op("pe", lambda h: h.matmul(ocf(pst[1][:]), lhsT=onesb[0:nkk, :], rhs=rhs, start=first, stop=False,
                                      skip_group_check=True), reads + [B_const], [B_pst[1]])

    def _pv(plist, Qb, hs, last):
        for (g, ti, nkk, nq, vv, oc) in plist:
            _acc_mm(vv, nkk, pT[ti][0:nkk, 0:nq], oc, [B_pT[ti], B_vd[g]])

    def _pv_g2(pl, ti, nk, Qb, hs):
        M2 = SEQ // 16
        for (g, ti_, nkk, nq, tag, oc, r) in pl:
            base = r * M2
            j0, p0 = divmod(base, 128)
            if p0 + nk <= 128:
                vv = vd[2][p0:p0 + nk, j0, :]
                _acc_mm_g2(vv, p0, nk, pT[ti][0:nk, r * 32:(r + 1) * 32], oc, ti)
            else:
                raise NotImplementedError

    def _acc_mm_g2(vv, p0, nk, rhs, ocf, ti):
        first = acc_state["first"]
        acc_state["first"] = False
        T.op("pe", lambda h: h.matmul(ocf(pst[0][:]), lhsT=vv, rhs=rhs, start=first, stop=False,
                                      skip_group_check=True), [B_pT[ti], B_vd[2]], [B_pst[0]])
        T.op("pe", lambda h: h.matmul(ocf(pst[1][:]), lhsT=onesb[0:nk, :], rhs=rhs, start=first, stop=False,
                                      skip_group_check=True), [B_pT[ti], B_const], [B_pst[1]])

    def evac_to(psrc, bsrc, dst, dstb, func=AF.Copy, scale=None, bias=None):
        kw = {}
        if scale is not None:
            kw["scale"] = scale
        if bias is not None:
            kw["bias"] = bias
        rd = [bsrc, B_pp] if kw else [bsrc]
        T.op("act", lambda h: h.activation(out=dst, in_=psrc, func=func, **kw), rd, [dstb])

    def setup_layer(l):
        for j in range(a):
            t_ap, t_b = newwk(128)
            T.dma("sp", t_ap, wsT[l, j], writes=[t_b])
            T.op("dve", lambda h, t_ap=t_ap, j=j: h.tensor_tensor(out=wsm[:, j, :], in0=t_ap, in1=cstt[:, 128:256],
                                                                  op=ALU.mult), [t_b, B_cst], [B_wsm])
            T.dma("sp", bsm[:, j, :], bsr[l, j], writes=[B_wsm])

    def phase3(l, ncol, ns, src_ap_fn, bsrc, dst_ap_fn, dstb, t0, sample, last_pass):
        if sample:
            load_xT(xbs[l], B_xbs[l], 0, ncol)
        else:
            load_xT(xb[l], B_xb[l][t0 // TP], t0, ncol)
        res = {}

        def grab(pa, pb):
            res["p"] = (pa, pb)

        for j in range(a):
            h2 = 2 * ns
            lin_in(l, c.cA[1] + j, ncol, lambda pa, pb: res.update(c=evac(pa, pb)))
            tcap, tcb = res["c"]
            lin_in(l, c.cA[2] + j, ncol, grab)
            pa, pb = res["p"]
            T.op("dve", lambda h, j=j, h2=h2: h.tensor_copy(out=ext[:, 0:h2], in_=hista[:, j, 0:h2]), [B_ha[j]], [B_ext])
            T.op("dve", lambda h, pa=pa, tcap=tcap, h2=h2: h.tensor_tensor(out=ext[:, h2:h2 + ncol], in0=tcap, in1=pa,
                                                                          op=ALU.mult), [pb, tcb], [B_ext])
            T.op("dve", lambda h, j=j, h2=h2: h.tensor_copy(out=hista[:, j, 0:h2], in_=ext[:, ncol:ncol + h2]),
                 [B_ext], [B_ha[j]])
            acc, accb = newwk(ncol)
            T.op("dve", lambda h, acc=acc, j=j: h.tensor_scalar(out=acc, in0=ext[:, 0:ncol], scalar1=PP(l, "caw", j * 3),
                                                                scalar2=None, op0=ALU.mult), [B_ext, B_pp], [accb])
            for k in (1, 2):
                T.op("dve", lambda h, acc=acc, j=j, k=k: h.scalar_tensor_tensor(
                    out=acc, in0=ext[:, k * ns:k * ns + ncol], scalar=PP(l, "caw", j * 3 + k), in1=acc,
                    op0=ALU.mult, op1=ALU.add), [B_ext, B_pp, accb], [accb])
            lin_in(l, c.cA[3] + j, ncol, lambda pa, pb: res.update(z=evac(pa, pb, func=AF.Silu)))
            szap, szb = res["z"]
            lin_in(l, c.cA[0] + j, ncol, grab)
            pa, pb = res["p"]
            T.op("dve", lambda h, acc=acc, pa=pa: h.tensor_tensor(out=acc, in0=acc, in1=pa, op=ALU.mult), [accb, pb], [accb])
            T.op("dve", lambda h, acc=acc, szap=szap, j=j: h.tensor_tensor(out=yT[:, j, 0:ncol], in0=acc, in1=szap,
                                                                          op=ALU.mult), [accb, szb], [B_y[j]])
            if sample:
                T.dma("sp", o_ca_s[:, l, j, :, :], hista[:, j, 0:h2].rearrange("p (k s) -> p k s", s=ns),
                      reads=[B_ha[j]], writes=[OB["ca_s"]])
            elif last_pass:
                T.dma("sp", o_ca_p[:, l, j, :], hista[:, j, 0:2], reads=[B_ha[j]], writes=[OB["ca_p"]])
        for j in range(a):
            h30 = 30 * ns
            lin_in(l, c.cB[1] + j, ncol, lambda pa, pb: res.update(g=evac(pa, pb, func=AF.Sigmoid)))
            tg, tgb = res["g"]
            lin_in(l, c.cB[0] + j, ncol, grab)
            pa, pb = res["p"]
            T.op("dve", lambda h, j=j, h30=h30: h.tensor_copy(out=ext[:, 0:h30], in_=histb[:, j, 0:h30]), [B_hb[j]], [B_ext])
            T.op("dve", lambda h, pa=pa, tg=tg, h30=h30: h.tensor_tensor(out=ext[:, h30:h30 + ncol], in0=tg, in1=pa,
                                                                        op=ALU.mult), [pb, tgb], [B_ext])
            T.op("dve", lambda h, j=j, h30=h30: h.tensor_copy(out=histb[:, j, 0:h30], in_=ext[:, ncol:ncol + h30]),
                 [B_ext], [B_hb[j]])
            cb = cbuf[:, j, 0:ncol]
            T.op("dve", lambda h, cb=cb, j=j: h.tensor_scalar(out=cb, in0=ext[:, 0:ncol], scalar1=PP(l, "cbw", j * 31),
                                                              scalar2=PP(l, "cbb", j), op0=ALU.mult, op1=ALU.add),
                 [B_ext, B_pp], [B_cb[j]])
            for k in range(1, 31):
                T.op("dve", lambda h, cb=cb, j=j, k=k: h.scalar_tensor_tensor(
                    out=cb, in0=ext[:, k * ns:k * ns + ncol], scalar=PP(l, "cbw", j * 31 + k), in1=cb,
                    op0=ALU.mult, op1=ALU.add), [B_ext, B_pp, B_cb[j]], [B_cb[j]])
            lin_in(l, c.cB[2] + j, ncol, lambda pa, pb, j=j: evac_to(pa, pb, yT[:, a + j, 0:ncol], B_y[a + j], func=AF.Silu))
            if sample:
                T.dma("sp", o_cb_s[:, l, j, :, :], histb[:, j, 0:h30].rearrange("p (k s) -> p k s", s=ns),
                      reads=[B_hb[j]], writes=[OB["cb_s"]])
            elif last_pass:
                T.dma("sp", o_cb_p[:, l, j, :], histb[:, j, 0:30], reads=[B_hb[j]], writes=[OB["cb_p"]])
        mean, rstd = ln_stats([(cbuf[:, j, 0:ncol], B_cb[j]) for j in range(a)], ncol, c.WA)
        for j in range(a):
            n_ap, n_b = newwk(ncol)
            normalize(cbuf[:, j, 0:ncol], B_cb[j], mean, rstd, n_ap, n_b)
            T.op("act", lambda h, n_ap=n_ap, j=j: h.activation(out=n_ap, in_=n_ap, func=AF.Silu,
                                                               scale=PP(l, "lbg", j), bias=PP(l, "lbb", j)),
                 [n_b, B_pp], [n_b])
            T.op("dve", lambda h, n_ap=n_ap, j=j: h.tensor_tensor(out=yT[:, a + j, 0:ncol], in0=n_ap,
                                                                  in1=yT[:, a + j, 0:ncol], op=ALU.mult),
                 [n_b, B_y[a + j]], [B_y[a + j]])
        for hs in range(4):
            lin_in(l, c.cZ + hs, ncol, lambda pa, pb: res.update(z=evac(pa, pb, func=AF.Silu)))
            szap, szb = res["z"]
            if sample:
                T.op("dve", lambda h, szap=szap, hs=hs: h.tensor_tensor(out=yT[:, 2 * a + hs, 0:ncol], in0=ycs[:, hs, :],
                                                                        in1=szap, op=ALU.mult),
                     [szb, B_ycs], [B_y[2 * a + hs]])
            else:
                p = t0 // 512
                T.op("dve", lambda h, szap=szap, hs=hs: h.tensor_tensor(out=yT[:, 2 * a + hs, 0:ncol],
                                                                        in0=ycall[:, hs, t0:t0 + ncol], in1=szap,
                                                                        op=ALU.mult),
                     [szb, B_yc[hs][p]], [B_y[2 * a + hs]])
        for j in range(a):
            lin_in(l, c.cD[1] + j, ncol, lambda pa, pb, j=j: evac_to(pa, pb, cbuf[:, j, 0:ncol], B_cb[j]))
        mean, rstd = ln_stats([(cbuf[:, j, 0:ncol], B_cb[j]) for j in range(a)], ncol, c.WA)
        for j in range(a):
            yj = 2 * a + 4 + j
            n_ap, n_b = newwk(ncol)
            normalize(cbuf[:, j, 0:ncol], B_cb[j], mean, rstd, n_ap, n_b)
            mix, mixb = newwk(ncol)
            if sample:
                T.op("act", lambda h, n_ap=n_ap, j=j: h.activation(out=n_ap, in_=n_ap, func=AF.Identity,
                                                                   scale=PP(l, "ldg", j), bias=PP(l, "ldb", j)),
                     [n_b, B_pp], [n_b])
                T.dma("sp", o_gv_s[:, l, j, :], n_ap, reads=[n_b], writes=[OB["gv"]])
                T.op("dve", lambda h, n_ap=n_ap, mix=mix, j=j: h.tensor_scalar(
                    out=mix, in0=n_ap, scalar1=PP(l, "ws00", j), scalar2=PP(l, "bs0", j), op0=ALU.mult, op1=ALU.add),
                    [n_b, B_pp], [mixb])
            else:
                i = nxt("wkb", 8)
                vb, vbb = wkb[i], B_wkb[i]
                T.op("act", lambda h, n_ap=n_ap, vb=vb, j=j: h.activation(out=vb[:, 0:ncol], in_=n_ap, func=AF.Identity,
                                                                         scale=PP(l, "ldg", j), bias=PP(l, "ldb", j)),
                     [n_b, B_pp], [vbb])
                nq = ncol // 128
                fns = [(lambda h, q=q, vb=vb: h.transpose(pmb[:, q * 128:(q + 1) * 128], vb[:, q * 128:(q + 1) * 128],
                                                          identb[:])) for q in range(nq)]
                T.group("pe", fns, [vbb, B_const], [B_pmb])
                T.op("act", lambda h, j=j: h.activation(out=vnT[:, j, 0:ncol], in_=pmb[:, 0:ncol], func=AF.Copy),
                     [B_pmb], [B_vn[j]])
                pi = nxt("pm", 2)
                fns = [(lambda h, q=q, j=j, pi=pi: h.matmul(pm[pi][:, q * 128:(q + 1) * 128],
                                                            lhsT=vnT[:, j, q * 128:(q + 1) * 128], rhs=wsm[:, j, :],
                                                            start=True, stop=True)) for q in range(nq)]
                T.group("pe", fns, [B_vn[j], B_wsm], [B_pm[pi]])
                for q in range(nq):
                    T.op("dve", lambda h, q=q, j=j, pi=pi, mix=mix: h.tensor_tensor(
                        out=mix[:, q * 128:(q + 1) * 128], in0=pm[pi][:, q * 128:(q + 1) * 128], in1=bsm[:, j, :],
                        op=ALU.add), [B_pm[pi], B_wsm], [mixb])
            lin_in(l, c.cD[0] + j, ncol, grab)
            pa, pb = res["p"]
            T.op("dve", lambda h, mix=mix, pa=pa: h.tensor_tensor(out=mix, in0=mix, in1=pa, op=ALU.mult), [mixb, pb], [mixb])
            lin_in(l, c.cD[2] + j, ncol, lambda pa, pb: res.update(z=evac(pa, pb, func=AF.Silu)))
            szap, szb = res["z"]
            T.op("dve", lambda h, mix=mix, szap=szap, yj=yj: h.tensor_tensor(out=yT[:, yj, 0:ncol], in0=mix, in1=szap,
                                                                            op=ALU.mult), [mixb, szb], [B_y[yj]])
        kr = ((0, a), (a, 2 * a), (2 * a, 2 * a + 4), (2 * a + 4, 3 * a + 4))
        for jo in range(KC):
            pump(1, need=("out", l, jo))
            T.dma("sp", wso[:], wb_out[l, jo], reads=[B_wb[("out", l, jo)]], writes=[B_wso])
            macc, maccb = newwk(ncol)
            for bi in range(4):
                lin_in(l, c.cG + bi * KC + jo, ncol,
                       lambda pa, pb, bi=bi: res.update(g=evac(pa, pb, func=AF.Sigmoid, bias=PP(l, "bg", bi * KC + jo))))
                sg, sgb = res["g"]
                pi = nxt("pm", 2)
                k0, k1 = kr[bi]
                fns = [(lambda h, kc=kc, pi=pi, k0=k0, k1=k1: h.matmul(
                    pm[pi][:, 0:ncol], lhsT=wso[:, kc * 128:(kc + 1) * 128], rhs=yT[:, kc, 0:ncol],
                    start=(kc == k0), stop=(kc == k1 - 1))) for kc in range(k0, k1)]
                T.group("pe", fns, [B_wso] + [B_y[kc] for kc in range(k0, k1)], [B_pm[pi]])
                if bi == 0:
                    T.op("dve", lambda h, sg=sg, pi=pi, macc=macc: h.tensor_tensor(out=macc, in0=sg, in1=pm[pi][:, 0:ncol],
                                                                                  op=ALU.mult), [sgb, B_pm[pi]], [maccb])
                else:
                    T.op("dve", lambda h, sg=sg, pi=pi: h.tensor_tensor(out=sg, in0=sg, in1=pm[pi][:, 0:ncol], op=ALU.mult),
                         [sgb, B_pm[pi]], [sgb])
                    if bi < 3:
                        T.op("dve", lambda h, sg=sg, macc=macc: h.tensor_tensor(out=macc, in0=macc, in1=sg, op=ALU.add),
                             [sgb, maccb], [maccb])
                    else:
                        T.op("dve", lambda h, sg=sg, macc=macc, jo=jo: h.tensor_tensor(out=mT[:, jo, 0:ncol], in0=macc,
                                                                                      in1=sg, op=ALU.add),
                             [sgb, maccb], [B_m[jo]])
        for jo in range(KC):
            xr, xrb = newwk(ncol)
            T.dma("sp", xr, src_ap_fn(jo), reads=[bsrc], writes=[xrb])
            lin(("o", l, jo), wb_o[l, jo], KC, lambda kc: mT[:, kc, 0:ncol], B_m, ncol, grab)
            pa, pb = res["p"]
            T.op("dve", lambda h, xr=xr, pa=pa: h.scalar_tensor_tensor(out=xr, in0=xr, scalar=float(c.ALPHA), in1=pa,
                                                                       op0=ALU.mult, op1=ALU.add), [xrb, pb], [xrb])
            sq, sqb = newwk(ncol)
            T.op("act", lambda h, xr=xr, sq=sq: h.activation(out=sq, in_=xr, func=AF.Square), [xrb], [sqb])
            T.op("pe", lambda h, xr=xr, jo=jo: h.matmul(pst[0][:, 0:ncol], lhsT=onesf[:], rhs=xr, start=(jo == 0),
                                                        stop=(jo == KC - 1)), [xrb, B_const], [B_pst[0]])
            T.op("pe", lambda h, sq=sq, jo=jo: h.matmul(pst[1][:, 0:ncol], lhsT=onesf[:], rhs=sq, start=(jo == 0),
                                                        stop=(jo == KC - 1)), [sqb, B_const], [B_pst[1]])
            T.dma("sp", rT[jo, :, 0:ncol], xr, reads=[xrb], writes=[B_rT[jo]])
        mean, rstd = ln_finish(ncol, D)
        for jo in range(KC):
            xr, xrb = newwk(ncol)
            T.dma("sp", xr, rT[jo, :, 0:ncol], reads=[B_rT[jo]], writes=[xrb])
            normalize(xr, xrb, mean, rstd, xr, xrb)
            T.op("act", lambda h, xr=xr, jo=jo: h.activation(out=xr, in_=xr, func=AF.Identity, scale=PP(l, "lg", jo),
                                                             bias=PP(l, "lb", jo)), [xrb, B_pp], [xrb])
            T.dma("sp", dst_ap_fn(jo), xr, reads=[xrb], writes=[dstb])
            if l + 1 < L:
                xbt, xbtb = newwkb()
                T.op("act", lambda h, xbt=xbt, xr=xr: h.activation(out=xbt[:, 0:ncol], in_=xr, func=AF.Copy), [xrb], [xbtb])
                if sample:
                    T.dma("sp", xbs[l + 1][jo, :, :], xbt[:, 0:ncol], reads=[xbtb], writes=[B_xbs[l + 1]])
                else:
                    T.dma("sp", xb[l + 1][jo, :, t0:t0 + ncol], xbt[:, 0:ncol], reads=[xbtb], writes=[B_xb[l + 1][t0 // TP]])

    def ln_finish(ncol, nch_total):
        mean, rstd, tmp = stat[0][:, 0:ncol], stat[1][:, 0:ncol], stat[2][:, 0:ncol]
        inv = 1.0 / nch_total
        T.op("act", lambda h: h.activation(out=mean, in_=pst[0][:, 0:ncol], func=AF.Copy, scale=inv),
             [B_pst[0]], [B_stat[0]])
        T.op("dve", lambda h: h.tensor_tensor(out=tmp, in0=mean, in1=mean, op=ALU.mult), [B_stat[0]], [B_stat[2]])
        T.op("dve", lambda h: h.scalar_tensor_tensor(out=tmp, in0=pst[1][:, 0:ncol], scalar=inv, in1=tmp,
                                                     op0=ALU.mult, op1=ALU.subtract),
             [B_pst[1], B_stat[2]], [B_stat[2]])
        T.op("dve", lambda h: h.tensor_scalar(out=tmp, in0=tmp, scalar1=LN_EPS, scalar2=None, op0=ALU.add),
             [B_stat[2]], [B_stat[2]])
        T.op("act", lambda h: h.activation(out=tmp, in_=tmp, func=AF.Sqrt), [B_stat[2]], [B_stat[2]])
        T.op("dve", lambda h: h.reciprocal(out=rstd, in_=tmp), [B_stat[2]], [B_stat[1]])
        return mean, rstd

    def newwkb():
        i = nxt("wkb", 8)
        return wkb[i], B_wkb[i]

    def sample_attn(l, src_fn, srcb):
        load_xT(xbs[l], B_xbs[l], 0, NS)
        T.dma("sp", ropest[:], ropes, writes=[B_rope])
        res = {}
        for which, cbase, off in (("q", c.cQ, 0), ("k", c.cK, 12 * NS), ("v", c.cV, 24 * NS)):
            for hh in range(NH):
                g, hs = hh // 4, hh % 4
                lin_in(l, cbase + hh, NS, lambda pa, pb: res.update(x=evac(pa, pb)))
                xap, xb = res["x"]
                dst = sm[:, off + hh * NS:off + (hh + 1) * NS]
                if which == "v":
                    T.op("dve", lambda h, dst=dst, xap=xap: h.tensor_copy(out=dst, in_=xap), [xb], [B_sm])
                else:
                    pi = nxt("pm", 2)
                    T.op("pe", lambda h, pi=pi, xap=xap: h.matmul(pm[pi][:, 0:NS], lhsT=swapf, rhs=xap, start=True, stop=True),
                         [xb, B_cst], [B_pm[pi]])
                    t2, b2 = newwk(NS)
                    T.op("dve", lambda h, t2=t2, pi=pi: h.tensor_scalar(out=t2, in0=pm[pi][:, 0:NS], scalar1=ropest[:, 1, 0:1],
                                                                        scalar2=None, op0=ALU.mult), [B_pm[pi], B_rope], [b2])
                    T.op("dve", lambda h, dst=dst, xap=xap, t2=t2: h.scalar_tensor_tensor(
                        out=dst, in0=xap, scalar=ropest[:, 0, 0:1], in1=t2, op0=ALU.mult, op1=ALU.add),
                        [xb, b2, B_rope], [B_sm])
                if which != "q":
                    T.dma("sp", o_kv_s[g][l, 0 if which == "k" else 1, hs, :, :], dst, reads=[B_sm], writes=[OB["kv_s"]])
        T.op("act", lambda h: h.activation(out=smb[:, 0:36 * NS], in_=sm[:, 0:36 * NS], func=AF.Copy), [B_sm], [B_smb])
        first = [True]
        for s in range(NS):
            for hs in range(4):
                col = hs * NS + s
                for g in range(3):
                    dil = DILS[g]
                    hh = g * 4 + hs
                    kc_t, kc_b = newwk(128)
                    vc_t, vc_b = newwk(128)
                    T.dma("sp", kc_t, cache[g][l, s, :, 0, hs, :].rearrange("(i r) d -> r i d", r=dil)[0], writes=[kc_b])
                    T.dma("sp", vc_t, cache[g][l, s, :, 1, hs, :].rearrange("(i r) d -> r i d", r=dil)[0], writes=[vc_b])
                    pi = nxt("pm", 2)
                    T.op("pe", lambda h, pi=pi, kc_t=kc_t: h.transpose(pm[pi][:, 0:128], kc_t, identf), [kc_b, B_cst], [B_pm[pi]])
                    kTb, kTbb = newwkb()
                    T.op("act", lambda h, kTb=kTb, pi=pi: h.activation(out=kTb[:, 0:128], in_=pm[pi][:, 0:128], func=AF.Copy),
                         [B_pm[pi]], [kTbb])
                    kcol = 12 * NS + hh * NS + s
                    T.op("act", lambda h, kTb=kTb, kcol=kcol: h.activation(out=kTb[:, 128:129], in_=smb[:, kcol:kcol + 1],
                                                                          func=AF.Copy), [B_smb, kTbb], [kTbb])
                    qcol = hh * NS + s
                    pz_i = nxt("pz", 3)
                    T.op("pe", lambda h, pz_i=pz_i, qcol=qcol, kTb=kTb: h.matmul(
                        pz[pz_i][0:1, 0:129], lhsT=smb[:, qcol:qcol + 1], rhs=kTb[:, 0:129], start=True, stop=True),
                        [B_smb, kTbb], [B_pz[pz_i]])
                    prow, prowb = newwk(129)
                    T.op("act", lambda h, prow=prow, pz_i=pz_i: h.activation(out=prow[0:1, :], in_=pz[pz_i][0:1, 0:129],
                                                                            func=AF.Exp, scale=float(SCALE)),
                         [B_pz[pz_i]], [prowb])
                    pi2 = nxt("pm", 2)
                    T.op("pe", lambda h, pi2=pi2, prow=prow: h.transpose(pm[pi2][:, 0:1], prow[0:1, 0:128], cstt[0:1, 0:1]),
                         [prowb, B_cst], [B_pm[pi2]])
                    pcol, pcolb = newwkb()
                    T.op("act", lambda h, pcol=pcol, pi2=pi2: h.activation(out=pcol[:, 0:1], in_=pm[pi2][:, 0:1], func=AF.Copy),
                         [B_pm[pi2]], [pcolb])
                    T.op("act", lambda h, pcol=pcol, prow=prow: h.activation(out=pcol[0:1, 8:9], in_=prow[0:1, 128:129],
                                                                            func=AF.Copy), [prowb, pcolb], [pcolb])
                    vcol = 24 * NS + hh * NS + s
                    pi3 = nxt("pm", 2)
                    T.op("pe", lambda h, pi3=pi3, vcol=vcol: h.transpose(pm[pi3][0:1, 0:128], sm[:, vcol:vcol + 1], identf),
                         [B_sm, B_cst], [B_pm[pi3]])
                    vb_t, vb_b = newwkb()
                    T.op("act", lambda h, vb_t=vb_t, pi3=pi3: h.activation(out=vb_t[0:1, 128:256], in_=pm[pi3][0:1, 0:128],
                                                                          func=AF.Copy), [B_pm[pi3]], [vb_b])
                    T.op("act", lambda h, vb_t=vb_t, vc_t=vc_t: h.activation(out=vb_t[:, 0:128], in_=vc_t, func=AF.Copy),
                         [vc_b, vb_b], [vb_b])
                    f0 = first[0]
                    first[0] = False
                    T.op("pe", lambda h, vb_t=vb_t, pcol=pcol, col=col, f0=f0: h.matmul(
                        pst[0][:, col:col + 1], lhsT=vb_t[:, 0:128], rhs=pcol[:, 0:1], start=f0, stop=False,
                        skip_group_check=True), [vb_b, pcolb], [B_pst[0]])
                    T.op("pe", lambda h, vb_t=vb_t, pcol=pcol, col=col: h.matmul(
                        pst[0][:, col:col + 1], lhsT=vb_t[0:1, 128:256], rhs=pcol[0:1, 8:9], start=False, stop=False,
                        skip_group_check=True), [vb_b, pcolb], [B_pst[0]])
                    T.op("pe", lambda h, pcol=pcol, col=col, f0=f0: h.matmul(
                        pst[1][:, col:col + 1], lhsT=onesb[:, :], rhs=pcol[:, 0:1], start=f0, stop=False,
                        skip_group_check=True), [B_const, pcolb], [B_pst[1]])
                    T.op("pe", lambda h, pcol=pcol, col=col: h.matmul(
                        pst[1][:, col:col + 1], lhsT=onesb[0:1, :], rhs=pcol[0:1, 8:9], start=False, stop=False,
                        skip_group_check=True), [B_const, pcolb], [B_pst[1]])
        rd_ap, rd_b = newwk(4 * NS)
        T.op("dve", lambda h: h.reciprocal(out=rd_ap, in_=pst[1][:, 0:4 * NS]), [B_pst[1]], [rd_b])
        T.op("dve", lambda h: h.tensor_tensor(out=ycs[:].rearrange("p h s -> p (h s)"), in0=pst[0][:, 0:4 * NS], in1=rd_ap,
                                              op=ALU.mult), [B_pst[0], rd_b], [B_ycs])

    def sample_layer(l):
        lastl = (l == L - 1)
        src_t = xsT if l == 0 else actsT[(l - 1) % len(actsT)]
        srcb = Bxs if l == 0 else B_acts[(l - 1) % len(actsT)]
        sfn = lambda jo, src_t=src_t: src_t[jo, :, :]
        sample_attn(l, sfn, srcb)
        for j in range(a):
            T.dma("sp", hista[:, j, :].rearrange("p (k s) -> p k s", s=NS), sca[:, l, j], writes=[B_ha[j]])
            T.dma("sp", histb[:, j, :].rearrange("p (k s) -> p k s", s=NS), scb[:, l, j], writes=[B_hb[j]])
        if lastl:
            dfn, dbuf = (lambda jo: o_ysT[jo, :, :]), OB["ys"]
        else:
            dfn, dbuf = (lambda jo, l=l: actsT[l % len(actsT)][jo, :, :]), B_acts[l % len(actsT)]
        phase3(l, NS, NS, sfn, srcb, dfn, dbuf, 0, True, False)

    Bxs = Buf("xsin")
    Bx = Buf("xin")
    for l in range(min(L, c.LIMIT_L)):
        lastl = (l == L - 1)
        src_t = xpT if l == 0 else actT[(l - 1) % len(actT)]
        srcb = (lambda p: Bx) if l == 0 else (lambda p, l=l: B_act[(l - 1) % len(actT)][p])
        setup_layer(l)
        for p in range(NP):
            phase1_prompt(l, p, xb[l], B_xb[l][p])
        if c.DO_ATTN:
            phase2(l)
        for j in range(a):
            T.op("dve", lambda h, j=j: h.memset(hista[:, j, :], 0.0), [], [B_ha[j]])
            T.op("dve", lambda h, j=j: h.memset(histb[:, j, :], 0.0), [], [B_hb[j]])
        for p in range(NP):
            t0 = p * TP
            if lastl:
                dfn, dbuf = (lambda jo, t0=t0: o_ypT[jo, :, t0:t0 + TP]), OB["yp"]
            else:
                dfn, dbuf = (lambda jo, t0=t0, l=l: actT[l % len(actT)][jo, :, t0:t0 + TP]), B_act[l % len(actT)][p]
            phase3(l, TP, 1, (lambda jo, t0=t0, src_t=src_t: src_t[jo, :, t0:t0 + TP]), srcb(p), dfn, dbuf, t0,
                   False, p == NP - 1)
        if c.DO_SAMPLE:
            sample_layer(l)
    T.finish(out_bufs)
    T.replay()
    return nc


def _chunk_w(w, kcn, ncn):
    return np.ascontiguousarray(w.reshape(kcn, 128, ncn, 128).transpose(2, 1, 0, 3)).reshape(ncn, 128, kcn * 128)


def _fm(x, nchunk):
    return np.ascontiguousarray(x.reshape(x.shape[0], nchunk, 128).transpose(1, 2, 0))


def _rope_tables(pos):
    half = HD // 2
    inv = (ROPE_THETA ** (-np.arange(half, dtype=np.float32) / half)).astype(np.float32)
    ang = pos.astype(np.float32)[:, None] * inv[None, :]
    cos, sin = np.cos(ang).astype(np.float32), np.sin(ang).astype(np.float32)
    cosT = np.concatenate([cos, cos], 1).T
    sinT = np.concatenate([-sin, sin], 1).T
    return np.ascontiguousarray(np.stack([cosT, sinT], 1)).astype(np.float32)


def _consts():
    i = np.arange(128)
    ident = np.eye(128, dtype=np.float32)
    U = (i[:, None] <= i[None, :]).astype(np.float32)
    Lm = (i[None, :] <= i[:, None]).astype(np.float32)
    sw = np.zeros((128, 128), np.float32)
    sw[(i + 64) % 128, i] = 1.0
    return np.ascontiguousarray(np.concatenate([ident, U, Lm, sw], 1))


def prep_inputs(c, inp):
    L, a, KC, NS = c.L, c.a, c.KC, c.NS
    f = lambda k: np.asarray(inp[k], dtype=np.float32)
    w_in = np.stack([_chunk_w(f("w_in")[l], KC, c.NCH) for l in range(L)])
    wcat = [np.concatenate([f("w_out_a")[l], f("w_out_b")[l], f("w_out_c")[l], f("w_out_d")[l]], 0) for l in range(L)]
    w_out = np.stack([_chunk_w(wcat[l], c.KCO, KC) for l in range(L)])
    w_o = np.stack([_chunk_w(f("w_o")[l], KC, KC) for l in range(L)])
    pp = np.zeros((128, L, c.NPP), np.float32)

    def put(name, l, arr):
        n = arr.shape[0] // 128
        o = c.pp[name]
        pp[:, l, o:o + n] = arr.reshape(n, 128).T

    for l in range(L):
        caw = f("conv_a_w")[l]
        pp[:, l, c.pp["caw"]:c.pp["caw"] + 3 * a] = caw.reshape(3, a, 128).transpose(2, 1, 0).reshape(128, 3 * a)
        cbw = f("conv_b_w")[l]
        pp[:, l, c.pp["cbw"]:c.pp["cbw"] + 31 * a] = cbw.reshape(31, a, 128).transpose(2, 1, 0).reshape(128, 31 * a)
        put("cbb", l, f("conv_b_bias")[l]); put("lbg", l, f("ln_b_g")[l]); put("lbb", l, f("ln_b_b")[l])
        put("ldg", l, f("ln_d_g")[l]); put("ldb", l, f("ln_d_b")[l]); put("bg", l, f("b_gate")[l])
        put("lg", l, f("ln_g")[l]); put("lb", l, f("ln_b")[l])
        pp[:, l, c.pp["ws00"]:c.pp["ws00"] + a] = f("w_s")[l][:, 0, 0][None, :]
        pp[:, l, c.pp["bs0"]:c.pp["bs0"] + a] = f("b_s")[l][:, 0][None, :]
    pp = np.ascontiguousarray(pp.reshape(128, L * c.NPP))
    wsT = np.ascontiguousarray(f("w_s").transpose(0, 1, 3, 2))
    bsr = np.ascontiguousarray(np.broadcast_to(f("b_s")[:, :, None, :], (L, a, 128, 128)))
    cst = _consts()
    ropep = _rope_tables(np.arange(c.SEQ))
    ropes = _rope_tables(np.array([c.PAST]))
    maps = []
    for ci in range(c.NCORES):
        s0 = ci * NS
        sca = f("state_conv_a")[:, s0:s0 + NS]
        scb = f("state_conv_b")[:, s0:s0 + NS]
        m = {
            "xpT": _fm(f("x_prompt")[ci], KC),
            "xsT": _fm(f("x_sample")[s0:s0 + NS, 0], KC),
            "w_in": w_in, "w_out": w_out, "w_o": w_o, "pp": pp, "wsT": wsT, "bsr": bsr, "cst": cst,
            "ropep": ropep, "ropes": ropes,
            "sca": np.ascontiguousarray(sca.reshape(L, NS, 2, a, 128).transpose(4, 0, 3, 2, 1)),
            "scb": np.ascontiguousarray(scb.reshape(L, NS, 30, a, 128).transpose(4, 0, 3, 2, 1)),
        }
        for g, k in enumerate(("cache_kv_w128", "cache_kv_w512", "cache_kv_w2048")):
            m["cache%d" % g] = np.ascontiguousarray(f(k)[:, s0:s0 + NS])
        maps.append(m)
    return maps


def assemble(c, res):
    L, a, NS, nco = c.L, c.a, c.NS, c.NCORES
    R = res

    def unfm(x):
        return np.ascontiguousarray(x.transpose(2, 0, 1).reshape(x.shape[2], -1))

    y_p = np.stack([unfm(R[i]["o_ypT"]) for i in range(nco)])
    y_s = np.concatenate([unfm(R[i]["o_ysT"]) for i in range(nco)])[:, None, :]
    ca_p = np.stack([R[i]["o_ca_p"].transpose(1, 3, 2, 0).reshape(L, 2, a * 128) for i in range(nco)], 1)
    ca_s = np.concatenate([R[i]["o_ca_s"].transpose(1, 4, 3, 2, 0).reshape(L, NS, 2, a * 128) for i in range(nco)], 1)
    cb_p = np.stack([R[i]["o_cb_p"].transpose(1, 3, 2, 0).reshape(L, 30, a * 128) for i in range(nco)], 1)
    cb_s = np.concatenate([R[i]["o_cb_s"].transpose(1, 4, 3, 2, 0).reshape(L, NS, 30, a * 128) for i in range(nco)], 1)
    outs = [y_p, y_s, ca_p, ca_s, cb_p, cb_s]
    for g in range(3):
        kp = np.stack([R[i]["o_kv_p%d" % g].transpose(0, 4, 1, 2, 3) for i in range(nco)], 1)
        ks = np.concatenate([R[i]["o_kv_s%d" % g].transpose(0, 4, 1, 2, 3) for i in range(nco)], 1)[:, :, None]
        outs += [kp, ks]
    gv = np.concatenate([R[i]["o_gv_s"].transpose(1, 3, 2, 0).reshape(L, NS, a * 128) for i in range(nco)], 1)[:, :, None, :]
    outs.append(gv)
    return tuple(np.ascontiguousarray(o, dtype=np.float32) for o in outs)


def run_cfg(c, inp, trace=False):
    nc = build(c)
    maps = prep_inputs(c, inp)
    res = run_bass_kernel_spmd(nc, maps, core_ids=list(range(c.NCORES)), **({"trace": True} if trace else {}))
    return assemble(c, res.results), res


def kernel(**inputs):
    c = Cfg()
    outs, _ = run_cfg(c, inputs)
    return outs
```

```python
import numpy as np
from contextlib import ExitStack
import concourse.bass as bass
import concourse.mybir as mybir
from concourse.bass_utils import run_bass_kernel_spmd

F32 = mybir.dt.float32
BF16 = mybir.dt.bfloat16
AF = mybir.ActivationFunctionType
ALU = mybir.AluOpType

HD = 128
NH = 12
WINDOWS = (128, 512, 2048)
DILS = (1, 4, 16)
LN_EPS = 1e-5
ROPE_THETA = 10000.0
SELF_WAIT = True


class Cfg:
    def __init__(self, D=4096, SEQ=2048, L=2, NS=2, PAST=16384, NCORES=4):
        self.D, self.SEQ, self.L, self.NS, self.PAST, self.NCORES = D, SEQ, L, NS, PAST, NCORES
        self.KC = D // 128
        self.a = D // 512
        self.WA = D // 4
        a = self.a
        self.cA = (0, a, 2 * a, 3 * a)
        self.cB = (4 * a, 5 * a, 6 * a)
        self.cQ, self.cK, self.cV = 7 * a, 7 * a + 12, 7 * a + 24
        self.cZ = 7 * a + 36
        self.cD = (7 * a + 40, 8 * a + 40, 9 * a + 40)
        self.cG = 10 * a + 40
        self.NCH = self.cG + 4 * self.KC
        self.KCO = 3 * a + 4
        self.TP = 256
        self.NP = SEQ // self.TP
        self.QT = 512
        self.NQ = SEQ // 512
        self.ALPHA = (2 * L) ** 0.25
        o = 0
        self.pp = {}
        for name, n in (("caw", 3 * a), ("cbw", 31 * a), ("cbb", a), ("lbg", a), ("lbb", a),
                        ("ldg", a), ("ldb", a), ("bg", 4 * self.KC), ("lg", self.KC), ("lb", self.KC),
                        ("ws00", a), ("bs0", a)):
            self.pp[name] = o
            o += n
        self.NPP = o
        self.keep = [min(w, SEQ) for w in WINDOWS]
        self.LIMIT_L = L
        self.DO_ATTN = True
        self.DO_SAMPLE = True
        self.clen = [min(w, PAST) for w in WINDOWS]


class Buf:
    __slots__ = ("name", "w", "r")

    def __init__(self, name):
        self.name, self.w, self.r = name, None, {}


class Eng:
    def __init__(self, name, sem):
        self.name, self.sem, self.cnt, self.seen, self.prog = name, sem, 0, {}, []
        self.dsems, self.dnext = [], 0


class Tracker:
    def __init__(self, nc, stack):
        self.nc, self.stack = nc, stack
        self.E = {}
        for n in ("pe", "act", "dve", "pool", "sp"):
            self.E[n] = Eng(n, stack.enter_context(nc.semaphore("s_" + n)))
        for n, k in (("sp", 12), ("pool", 8), ("act", 4)):
            for i in range(k):
                self.E[n].dsems.append([stack.enter_context(nc.semaphore("d_%s%d" % (n, i))), 0])
        self.final = []

    def _wait(self, e, toks):
        need = {}
        for sem, val in toks:
            if sem is e.sem and not SELF_WAIT:
                continue
            k = id(sem)
            if e.seen.get(k, 0) < val and need.get(k, (None, 0))[1] < val:
                need[k] = (sem, val)
        for k, (sem, val) in need.items():
            e.seen[k] = val
            e.prog.append(lambda h, sem=sem, val=val: h.wait_ge(sem, val))

    def _deps(self, reads, writes):
        toks = []
        for b in reads:
            if b.w:
                toks.append(b.w)
        for b in writes:
            if b.w:
                toks.append(b.w)
            toks.extend(b.r.values())
        return toks

    def _mark(self, tok, reads, writes):
        for b in reads:
            b.r[id(tok[0])] = tok
        for b in writes:
            b.w, b.r = tok, {}

    def op(self, en, fn, reads=(), writes=()):
        self.group(en, [fn], reads, writes)

    def group(self, en, fns, reads=(), writes=()):
        e = self.E[en]
        toks = self._deps(reads, writes)
        if en == "pe":
            toks = [t for t in toks if t[0] is not e.sem]
        self._wait(e, toks)
        e.cnt += 1
        sem, cnt = e.sem, e.cnt
        for f in fns[:-1]:
            e.prog.append(lambda h, f=f: f(h))
        last = fns[-1]
        e.prog.append(lambda h, f=last, sem=sem: f(h).then_inc(sem, 1))
        self._mark((sem, cnt), reads, writes)

    def dma(self, qn, out, in_, reads=(), writes=(), **kw):
        e = self.E[qn]
        slot = e.dsems[e.dnext % len(e.dsems)]
        e.dnext += 1
        toks = self._deps(reads, writes)
        if slot[1] > 0:
            toks.append((slot[0], slot[1]))
        self._wait(e, toks)
        slot[1] += 16
        sem, val = slot[0], slot[1]
        e.prog.append(lambda h, sem=sem: h.dma_start(out=out, in_=in_, **kw).then_inc(sem, 16))
        self._mark((sem, val), reads, writes)
        return (sem, val)

    def finish(self, bufs):
        e = self.E["sp"]
        toks = []
        for b in bufs:
            if b.w:
                toks.append(b.w)
        for q in ("sp", "pool", "act"):
            for s in self.E[q].dsems:
                if s[1] > 0:
                    toks.append((s[0], s[1]))
        self._wait(e, toks)

    def replay(self):
        nc = self.nc
        with nc.Block() as block:
            @block.tensor
            def _(h):
                for f in self.E["pe"].prog:
                    f(h)

            @block.scalar
            def _(h):
                for f in self.E["act"].prog:
                    f(h)

            @block.vector
            def _(h):
                for f in self.E["dve"].prog:
                    f(h)

            @block.gpsimd
            def _(h):
                for f in self.E["pool"].prog:
                    f(h)

            @block.sync
            def _(h):
                for f in self.E["sp"].prog:
                    f(h)


def build(cfg):
    c = cfg
    D, SEQ, L, NS, KC, a, TP, NP = c.D, c.SEQ, c.L, c.NS, c.KC, c.a, c.TP, c.NP
    NCH, KCO = c.NCH, c.KCO
    nc = bass.Bass("TRN2", target_bir_lowering=False)
    stack = ExitStack()
    T = Tracker(nc, stack)

    def din(name, shape, dt=F32):
        return nc.dram_tensor(name, list(shape), dt, kind="ExternalInput").ap()

    def dout(name, shape, dt=F32):
        return nc.dram_tensor(name, list(shape), dt, kind="ExternalOutput").ap()

    def dscr(name, shape, dt):
        return nc.dram_tensor(name, list(shape), dt, kind="Internal").ap()

    def sb(name, shape, dt=F32):
        return stack.enter_context(nc.sbuf_tensor(name, list(shape), dt))

    def ps(name, shape=(128, 512), dt=F32):
        return stack.enter_context(nc.psum_tensor(name, list(shape), dt))

    xpT = din("xpT", [KC, 128, SEQ])
    xsT = din("xsT", [KC, 128, NS])
    w_in = din("w_in", [L, NCH, 128, KC * 128])
    w_out = din("w_out", [L, KC, 128, KCO * 128])
    w_o = din("w_o", [L, KC, 128, KC * 128])
    ppd = din("pp", [128, L * c.NPP])
    wsT = din("wsT", [L, a, 128, 128])
    bsr = din("bsr", [L, a, 128, 128])
    cst = din("cst", [128, 4 * 128])
    ropep = din("ropep", [128, 2, SEQ])
    ropes = din("ropes", [128, 2, 1])
    sca = din("sca", [128, L, a, 2, NS])
    scb = din("scb", [128, L, a, 30, NS])
    cache = [din("cache%d" % g, [L, NS, c.clen[g], 2, 4, 128]) for g in range(3)]

    o_ypT = dout("o_ypT", [KC, 128, SEQ])
    o_ysT = dout("o_ysT", [KC, 128, NS])
    o_ca_p = dout("o_ca_p", [128, L, a, 2])
    o_ca_s = dout("o_ca_s", [128, L, a, 2, NS])
    o_cb_p = dout("o_cb_p", [128, L, a, 30])
    o_cb_s = dout("o_cb_s", [128, L, a, 30, NS])
    o_kv_p = [dout("o_kv_p%d" % g, [L, 2, 4, 128, c.keep[g]]) for g in range(3)]
    o_kv_s = [dout("o_kv_s%d" % g, [L, 2, 4, 128, NS]) for g in range(3)]
    o_gv_s = dout("o_gv_s", [128, L, a, NS])
    out_bufs = [Buf("out%d" % i) for i in range(16)]
    OB = dict(yp=out_bufs[0], ys=out_bufs[1], ca_p=out_bufs[2], ca_s=out_bufs[3], cb_p=out_bufs[4],
              cb_s=out_bufs[5], kv_p=out_bufs[6], kv_s=out_bufs[7], gv=out_bufs[8])

    WG = 32
    _wbin = [[dscr("wb_in%d_%d" % (l, gi), [min(WG, NCH - gi * WG), 128, KC * 128], BF16)
              for gi in range((NCH + WG - 1) // WG)] for l in range(L)]

    class _WbIn:
        def __getitem__(self, lm):
            l, m = lm
            return _wbin[l][m // WG][m % WG]
    wb_in = _WbIn()
    _wbout = [dscr("wb_out%d" % l, [KC, 128, KCO * 128], BF16) for l in range(L)]
    _wbo = [dscr("wb_o%d" % l, [KC, 128, KC * 128], BF16) for l in range(L)]

    class _Wb2:
        def __init__(self, ts):
            self.ts = ts

        def __getitem__(self, lj):
            return self.ts[lj[0]][lj[1]]
    wb_out = _Wb2(_wbout)
    wb_o = _Wb2(_wbo)
    actT = [dscr("actT%d" % i, [KC, 128, SEQ], F32) for i in range(max(1, L - 1))]
    actsT = [dscr("actsT%d" % i, [KC, 128, NS], F32) for i in range(max(1, L - 1))]
    rT = dscr("rT", [KC, 128, TP], F32)
    xb = [dscr("xb%d" % i, [KC, 128, SEQ], BF16) for i in range(L)]
    xbs = [dscr("xbs%d" % i, [KC, 128, NS], BF16) for i in range(L)]
    B_xb = [[Buf("xb%d_%d" % (i, p)) for p in range(NP)] for i in range(L)]
    B_xbs = [Buf("xbs%d" % i) for i in range(L)]
    qT_s = [dscr("qT_s%d" % g, [128, 4, SEQ], BF16) for g in range(3)]
    kT_s = [dscr("kT_s%d" % g, [128, 4, SEQ], BF16) for g in range(3)]
    v_s = [dscr("v_s%d" % g, [4, SEQ, 128], BF16) for g in range(3)]
    B_wb = {}
    B_act = [[Buf("actT%d_%d" % (i, p)) for p in range(NP)] for i in range(len(actT))]
    B_acts = [Buf("actsT%d" % i) for i in range(len(actsT))]
    B_rT = [Buf("rT%d" % j) for j in range(KC)]
    B_q = [[Buf("q%d_%d" % (g, h)) for h in range(4)] for g in range(3)]
    B_k = [[Buf("k%d_%d" % (g, h)) for h in range(4)] for g in range(3)]
    B_v = [[Buf("v%d_%d" % (g, h)) for h in range(4)] for g in range(3)]

    NSLOT = 4
    wslot = [sb("wslot%d" % i, [128, KC * 128], BF16) for i in range(NSLOT)]
    B_ws = [Buf("ws%d" % i) for i in range(NSLOT)]
    wso = sb("wso", [128, KCO * 128], BF16)
    B_wso = Buf("wso")
    xT = sb("xT", [128, KC, TP], BF16)
    B_xT = Buf("xT")
    yT = sb("yT", [128, KCO, TP], BF16)
    B_y = [Buf("y%d" % i) for i in range(KCO)]
    mT = sb("mT", [128, KC, TP], BF16)
    B_m = [Buf("m%d" % i) for i in range(KC)]
    ycall = sb("ycall", [128, 4, SEQ], BF16)
    B_yc = [[Buf("yc%d_%d" % (h, p)) for p in range(c.NQ)] for h in range(4)]
    ycs = sb("ycs", [128, 4, NS], BF16)
    B_ycs = Buf("ycs")
    ppt = sb("ppt", [128, L * c.NPP])
    B_pp = Buf("pp")
    cstt = sb("cstt", [128, 4 * 128])
    B_cst = Buf("cst")
    identb = sb("identb", [128, 128], BF16)
    onesb = sb("onesb", [128, 128], BF16)
    onesf = sb("onesf", [128, 128])
    ULb = sb("ULb", [128, 256], BF16)
    Mg2 = sb("Mg2", [128, 4, 512], BF16)
    B_const = Buf("const")
    rope = sb("rope", [128, 2, TP])
    B_rope = Buf("rope")
    ropest = sb("ropest", [128, 2, 1])
    NW = 10
    wk = [sb("wk%d" % i, [128, TP]) for i in range(NW)]
    B_wk = [Buf("wk%d" % i) for i in range(NW)]
    wkb = [sb("wkb%d" % i, [128, TP], BF16) for i in range(8)]
    B_wkb = [Buf("wkb%d" % i) for i in range(8)]
    cbuf = sb("cbuf", [128, a, TP])
    B_cb = [Buf("cb%d" % i) for i in range(a)]
    ext = sb("ext", [128, (30 + TP)])
    B_ext = Buf("ext")
    hista = sb("hista", [128, a, 2 * NS])
    histb = sb("histb", [128, a, 30 * NS])
    B_ha = [Buf("ha%d" % i) for i in range(a)]
    B_hb = [Buf("hb%d" % i) for i in range(a)]
    wsm = sb("wsm", [128, a, 128], BF16)
    bsm = sb("bsm", [128, a, 128])
    B_wsm = Buf("wsm")
    vnT = sb("vnT", [128, a, TP], BF16)
    B_vn = [Buf("vn%d" % i) for i in range(a)]
    stat = [sb("stat%d" % i, [128, TP]) for i in range(3)]
    B_stat = [Buf("stat%d" % i) for i in range(3)]
    qd = [sb("qd%d" % g, [128, SEQ], BF16) for g in range(3)]
    kd = [sb("kd%d" % g, [128, SEQ], BF16) for g in range(3)]
    vd = [sb("vd%d" % g, [128, SEQ // 128, 128], BF16) for g in range(3)]
    B_qd = [Buf("qd%d" % g) for g in range(3)]
    B_kd = [Buf("kd%d" % g) for g in range(3)]
    B_vd = [Buf("vd%d" % g) for g in range(3)]
    pT = [sb("pT%d" % i, [128, 512], BF16) for i in range(3)]
    B_pT = [Buf("pT%d" % i) for i in range(3)]
    vst = sb("vst", [128, 16, 128], BF16)
    B_vst = Buf("vst")
    rdt = sb("rdt", [128, 512])
    B_rdt = Buf("rdt")
    sm = sb("sm", [128, 1024])
    B_sm = Buf("sm")
    smb = sb("smb", [128, 1024], BF16)
    B_smb = Buf("smb")

    pz = [ps("pz%d" % i) for i in range(3)]
    B_pz = [Buf("pz%d" % i) for i in range(3)]
    pst = [ps("pst%d" % i) for i in range(2)]
    B_pst = [Buf("pst%d" % i) for i in range(2)]
    pm = [ps("pm%d" % i) for i in range(2)]
    B_pm = [Buf("pm%d" % i) for i in range(2)]
    pmb = ps("pmb", (128, 1024), BF16)
    B_pmb = Buf("pmb")
    rot = {"pz": 0, "pm": 0, "wk": 0, "wkb": 0, "ws": 0, "pT": 0}

    def nxt(kind, n):
        i = rot[kind] % n
        rot[kind] += 1
        return i

    def PP(l, name, j):
        o = l * c.NPP + c.pp[name] + j
        return ppt[:, o:o + 1]

    T.dma("sp", ppt[:], ppd, writes=[B_pp])
    T.dma("sp", cstt[:], cst, writes=[B_cst])
    T.op("dve", lambda h: h.tensor_copy(out=identb[:], in_=cstt[:, 0:128]), [B_cst], [B_const])
    T.op("dve", lambda h: h.memset(onesb[:], 1.0), [], [B_const])
    T.op("dve", lambda h: h.memset(onesf[:], 1.0), [], [B_const])
    T.op("dve", lambda h: h.tensor_copy(out=ULb[:], in_=cstt[:, 128:384]), [B_cst], [B_const])
    for mi in range(4):
        for r in range(16):
            T.op("dve", lambda h, mi=mi, r=r: h.tensor_copy(
                out=Mg2[:, mi, r * 32:(r + 1) * 32], in_=cstt[:, 128 + mi * 32:128 + mi * 32 + 32]),
                [B_cst], [B_const])
    identf = cstt[:, 0:128]
    swapf = cstt[:, 384:512]

    def cast_w(src, dst, key):
        b = Buf(str(key))
        B_wb[key] = b
        T.dma("pool", dst, src, writes=[b], max_dma_last_dim=4096)

    def cast_layer(l):
        for m in list(range(c.cQ, c.cQ + 36)) + [m for m in range(NCH) if not (c.cQ <= m < c.cQ + 36)]:
            cast_w(w_in[l, m], wb_in[l, m], ("in", l, m))
        for j in range(KC):
            cast_w(w_out[l, j], wb_out[l, j], ("out", l, j))
        for j in range(KC):
            cast_w(w_o[l, j], wb_o[l, j], ("o", l, j))

    for kc in range(KC):
        T.dma("pool", xb[0][kc], xpT[kc], writes=B_xb[0], max_dma_last_dim=4096)
    for kc in range(KC):
        T.dma("pool", xbs[0][kc], xsT[kc], writes=[B_xbs[0]], max_dma_last_dim=4096)
    cast_q = []

    def cast_w_lazy(src, dst, key):
        cast_q.append((src, dst, key))

    def pump(n=1, need=None):
        while cast_q and (n > 0 or (need is not None and need not in B_wb)):
            s_, d_, k_ = cast_q.pop(0)
            cast_w(s_, d_, k_)
            n -= 1

    _cw = cast_w
    for l in range(L):
        for m in list(range(c.cQ, c.cQ + 36)) + [m for m in range(NCH) if not (c.cQ <= m < c.cQ + 36)]:
            cast_w_lazy(w_in[l, m], wb_in[l, m], ("in", l, m))
        for j in range(KC):
            cast_w_lazy(w_out[l, j], wb_out[l, j], ("out", l, j))
        for j in range(KC):
            cast_w_lazy(w_o[l, j], wb_o[l, j], ("o", l, j))

    def lin(key, wsrc, kcn, rhs_fn, rhs_bufs, ncol, consume):
        pump(1, need=key)
        si = nxt("ws", NSLOT)
        T.dma("sp", wslot[si][:, 0:kcn * 128], wsrc, reads=[B_wb[key]], writes=[B_ws[si]])
        pi = nxt("pz", 3)
        fns = []
        for kc in range(kcn):
            fns.append(lambda h, kc=kc, si=si, pi=pi: h.matmul(
                pz[pi][:, 0:ncol], lhsT=wslot[si][:, kc * 128:(kc + 1) * 128], rhs=rhs_fn(kc),
                start=(kc == 0), stop=(kc == kcn - 1)))
        T.group("pe", fns, reads=[B_ws[si]] + list(rhs_bufs), writes=[B_pz[pi]])
        consume(pz[pi][:, 0:ncol], B_pz[pi])

    def lin_in(l, m, ncol, consume):
        lin(("in", l, m), wb_in[l, m], KC, lambda kc: xT[:, kc, 0:ncol], [B_xT], ncol, consume)

    def evac(psrc, bsrc, func=AF.Copy, dt=F32, scale=None, bias=None):
        ncol = psrc.shape[-1]
        if dt == F32:
            i = nxt("wk", NW)
            dst, b = wk[i][:, 0:ncol], B_wk[i]
        else:
            i = nxt("wkb", 8)
            dst, b = wkb[i][:, 0:ncol], B_wkb[i]
        kw = {}
        if scale is not None:
            kw["scale"] = scale
        if bias is not None:
            kw["bias"] = bias
        rd = [bsrc, B_pp] if (scale is not None or bias is not None) else [bsrc]
        T.op("act", lambda h: h.activation(out=dst, in_=psrc, func=func, **kw), rd, [b])
        return dst, b

    def newwk(ncol):
        i = nxt("wk", NW)
        return wk[i][:, 0:ncol], B_wk[i]

    def ln_stats(tiles, ncol, nch_total):
        n = len(tiles)
        fns0, fns1 = [], []
        sq = []
        for i, (ap, b) in enumerate(tiles):
            s_ap, s_b = newwk(ncol)
            T.op("act", lambda h, ap=ap, s_ap=s_ap: h.activation(out=s_ap, in_=ap, func=AF.Square), [b], [s_b])
            sq.append((s_ap, s_b))
            T.op("pe", lambda h, ap=ap, i=i: h.matmul(pst[0][:, 0:ncol], lhsT=onesf[:], rhs=ap,
                                                      start=(i == 0), stop=(i == n - 1)),
                 [b, B_const], [B_pst[0]])
            T.op("pe", lambda h, s_ap=s_ap, i=i: h.matmul(pst[1][:, 0:ncol], lhsT=onesf[:], rhs=s_ap,
                                                          start=(i == 0), stop=(i == n - 1)),
                 [s_b, B_const], [B_pst[1]])
        mean, rstd, tmp = stat[0][:, 0:ncol], stat[1][:, 0:ncol], stat[2][:, 0:ncol]
        inv = 1.0 / nch_total
        T.op("act", lambda h: h.activation(out=mean, in_=pst[0][:, 0:ncol], func=AF.Copy, scale=inv),
             [B_pst[0]], [B_stat[0]])
        T.op("dve", lambda h: h.tensor_tensor(out=tmp, in0=mean, in1=mean, op=ALU.mult), [B_stat[0]], [B_stat[2]])
        T.op("dve", lambda h: h.scalar_tensor_tensor(out=tmp, in0=pst[1][:, 0:ncol], scalar=inv, in1=tmp,
                                                     op0=ALU.mult, op1=ALU.subtract),
             [B_pst[1], B_stat[2]], [B_stat[2]])
        T.op("dve", lambda h: h.tensor_scalar(out=tmp, in0=tmp, scalar1=LN_EPS, scalar2=None, op0=ALU.add),
             [B_stat[2]], [B_stat[2]])
        T.op("act", lambda h: h.activation(out=tmp, in_=tmp, func=AF.Sqrt), [B_stat[2]], [B_stat[2]])
        T.op("dve", lambda h: h.reciprocal(out=rstd, in_=tmp), [B_stat[2]], [B_stat[1]])
        return mean, rstd

    def normalize(ap, b, mean, rstd, out_ap, out_b):
        T.op("dve", lambda h: h.tensor_tensor(out=out_ap, in0=ap, in1=mean, op=ALU.subtract),
             [b, B_stat[0]], [out_b])
        T.op("dve", lambda h: h.tensor_tensor(out=out_ap, in0=out_ap, in1=rstd, op=ALU.mult),
             [out_b, B_stat[1]], [out_b])

    def load_xT(src_t, bsrc, col0, ncol):
        step = max(1, KC // 4)
        for k0 in range(0, KC, step):
            T.dma("pool", xT[:, k0:k0 + step, 0:ncol],
                  src_t[k0:k0 + step, :, col0:col0 + ncol].rearrange("k p t -> p k t"), reads=[bsrc], writes=[B_xT])

    def rope_apply(xap, xb, ncol, cos, sin):
        pi = nxt("pm", 2)
        T.op("pe", lambda h: h.matmul(pm[pi][:, 0:ncol], lhsT=swapf, rhs=xap, start=True, stop=True),
             [xb, B_cst], [B_pm[pi]])
        t1, b1 = newwk(ncol)
        T.op("dve", lambda h: h.tensor_tensor(out=t1, in0=xap, in1=cos, op=ALU.mult), [xb, B_rope], [b1])
        t2, b2 = newwk(ncol)
        T.op("dve", lambda h: h.tensor_tensor(out=t2, in0=pm[pi][:, 0:ncol], in1=sin, op=ALU.mult),
             [B_pm[pi], B_rope], [b2])
        T.op("dve", lambda h: h.tensor_tensor(out=t1, in0=t1, in1=t2, op=ALU.add), [b1, b2], [b1])
        return t1, b1

    def phase1_prompt(l, p, src_t, bsrc):
        t0 = p * TP
        load_xT(src_t, bsrc, t0, TP)
        T.dma("sp", rope[:], ropep[:, :, t0:t0 + TP], writes=[B_rope])
        cos, sin = rope[:, 0, :], rope[:, 1, :]
        for which, cbase, scr, Bs in (("q", c.cQ, qT_s, B_q), ("k", c.cK, kT_s, B_k)):
            for hh in range(NH):
                g, hs = hh // 4, hh % 4
                dil = DILS[g]
                res = {}
                lin_in(l, cbase + hh, TP, lambda pa, pb: res.update(x=evac(pa, pb)))
                xap, xb = res["x"]
                rt, rb = rope_apply(xap, xb, TP, cos, sin)
                if which == "k":
                    keep = c.keep[g]
                    lo = max(t0, SEQ - keep)
                    if lo < t0 + TP:
                        T.dma("sp", o_kv_p[g][l, 0, hs, :, lo - (SEQ - keep):t0 + TP - (SEQ - keep)],
                              rt[:, lo - t0:TP], reads=[rb], writes=[OB["kv_p"]])
                i = nxt("wkb", 8)
                dst, db = wkb[i], B_wkb[i]
                T.op("act", lambda h, dst=dst, rt=rt: h.activation(out=dst[:, 0:TP], in_=rt, func=AF.Copy), [rb], [db])
                T.dma("sp", scr[g][:, hs, t0:t0 + TP], dst[:, 0:TP], reads=[db], writes=[Bs[g][hs]])
        for hh in range(NH):
            g, hs = hh // 4, hh % 4
            dil = DILS[g]
            res = {}
            lin_in(l, c.cV + hh, TP, lambda pa, pb: res.update(x=evac(pa, pb)))
            xap, xb = res["x"]
            keep = c.keep[g]
            lo = max(t0, SEQ - keep)
            if lo < t0 + TP:
                T.dma("sp", o_kv_p[g][l, 1, hs, :, lo - (SEQ - keep):t0 + TP - (SEQ - keep)],
                      xap[:, lo - t0:TP], reads=[xb], writes=[OB["kv_p"]])
            i = nxt("wkb", 8)
            vb16, vbb = wkb[i], B_wkb[i]
            T.op("act", lambda h, vb16=vb16, xap=xap: h.activation(out=vb16[:], in_=xap, func=AF.Copy), [xb], [vbb])
            nm = TP // dil
            if dil == 1:
                blocks = [(vb16[:, j * 128:(j + 1) * 128], 128, j) for j in range(TP // 128)]
            else:
                v3 = vb16[:].rearrange("p (m r) -> p r m", r=dil)
                blocks = [(v3[:, r, :], nm, r) for r in range(dil)]
            for s0 in range(0, len(blocks), 8):
                sub = blocks[s0:s0 + 8]
                fns = [(lambda h, src=src, n=n, jj=jj: h.transpose(pmb[0:n, jj * 128:(jj + 1) * 128], src, identb[:]))
                       for jj, (src, n, j) in enumerate(sub)]
                T.group("pe", fns, [vbb, B_const], [B_pmb])
                n = sub[0][1]
                nb = len(sub)
                T.op("act", lambda h, n=n, nb=nb, s0=s0: h.activation(
                    out=vst[0:n, s0:s0 + nb, :], in_=pmb[0:n, 0:nb * 128].rearrange("p (b d) -> p b d", d=128),
                    func=AF.Copy), [B_pmb], [B_vst])
            M = SEQ // dil
            m0 = t0 // dil
            if dil == 1:
                T.dma("sp", v_s[g][hs, t0:t0 + TP, :].rearrange("(j p) d -> p j d", p=128), vst[:, 0:TP // 128, :],
                      reads=[B_vst], writes=[B_v[g][hs]])
            else:
                T.dma("sp", v_s[g][hs].rearrange("(r m) d -> m r d", r=dil)[m0:m0 + nm, :, :], vst[0:nm, 0:dil, :],
                      reads=[B_vst], writes=[B_v[g][hs]])

    SCALE = 1.0 / np.sqrt(HD)

    def phase2(l):
        for hs in range(4):
            for g in range(3):
                T.dma("sp", qd[g][:], qT_s[g][:, hs, :], reads=[B_q[g][hs]], writes=[B_qd[g]])
                T.dma("sp", kd[g][:], kT_s[g][:, hs, :], reads=[B_k[g][hs]], writes=[B_kd[g]])
                T.dma("sp", vd[g][:], v_s[g][hs].rearrange("(j p) d -> p j d", p=128), reads=[B_v[g][hs]],
                      writes=[B_vd[g]])
            for Qb in range(c.NQ):
                jobs = []
                t0 = Qb * 512
                for kb in range(4 * Qb - 1, 4 * Qb + 4):
                    if kb < 0:
                        continue
                    q_lo = max(kb, 4 * Qb)
                    q_hi = min(kb + 1, 4 * Qb + 3)
                    nq = (q_hi - q_lo + 1) * 128
                    moff = 0 if q_lo == kb else 128
                    jobs.append((0, kd[0][:, kb * 128:(kb + 1) * 128], 128, qd[0][:, q_lo * 128:q_lo * 128 + nq], nq,
                                 ULb[:, moff:moff + nq], vd[0][:, kb, :],
                                 lambda P, q_lo=q_lo, nq=nq, t0=t0: P[:, q_lo * 128 - t0:q_lo * 128 - t0 + nq]))
                M1 = SEQ // 4
                k1v = kd[1][:].rearrange("p (m r) -> p r m", r=4)
                q1v = qd[1][:].rearrange("p (m r) -> p r m", r=4)
                for r in range(4):
                    for kbm in (Qb - 1, Qb):
                        if kbm < 0:
                            continue
                        moff = 0 if kbm == Qb else 128
                        base = r * M1
                        jobs.append((1, k1v[:, r, kbm * 128:(kbm + 1) * 128], 128,
                                     q1v[:, r, Qb * 128:(Qb + 1) * 128], 128,
                                     ULb[:, moff:moff + 128], vd[1][:, (base + kbm * 128) // 128, :],
                                     lambda P, r=r: P.rearrange("p (m r) -> p r m", r=4)[:, r, :]))
                M2 = SEQ // 16
                m0 = 32 * Qb
                nk = m0 + 32
                g2jobs = []
                k2v = kd[2][:].rearrange("p (m r) -> p r m", r=16)
                q2v = qd[2][:].rearrange("p (m r) -> p r m", r=16)
                for r in range(16):
                    base = r * M2
                    g2jobs.append((k2v[:, r, 0:nk], q2v[:, r, m0:m0 + 32],
                                   vd[2][0:nk, base // 128, :] if M2 == 128 else None, r, base))
                plist = []
                for (g, lk, nkk, rq, nq, mk, vv, oc) in jobs:
                    pi = nxt("pz", 3)
                    T.op("pe", lambda h, lk=lk, rq=rq, nkk=nkk, nq=nq, pi=pi: h.matmul(
                        pz[pi][0:nkk, 0:nq], lhsT=lk, rhs=rq, start=True, stop=True),
                        [B_kd[g], B_qd[g]], [B_pz[pi]])
                    ti = nxt("pT", 3)
                    T.op("act", lambda h, pi=pi, ti=ti, nkk=nkk, nq=nq: h.activation(
                        out=pT[ti][0:nkk, 0:nq], in_=pz[pi][0:nkk, 0:nq], func=AF.Exp, scale=float(SCALE)),
                        [B_pz[pi]], [B_pT[ti]])
                    T.op("dve", lambda h, ti=ti, nkk=nkk, nq=nq, mk=mk: h.tensor_tensor(
                        out=pT[ti][0:nkk, 0:nq], in0=pT[ti][0:nkk, 0:nq], in1=mk, op=ALU.mult),
                        [B_pT[ti], B_const], [B_pT[ti]])
                    plist.append((g, ti, nkk, nq, vv, oc))
                    _pv(plist, Qb, hs, last=False)
                    plist = []
                pi = nxt("pz", 3)
                fns = []
                for (lk, rq, vv, r, base) in g2jobs:
                    fns.append(lambda h, lk=lk, rq=rq, r=r, pi=pi, nk=nk: h.matmul(
                        pz[pi][0:nk, r * 32:(r + 1) * 32], lhsT=lk, rhs=rq, start=True, stop=True))
                T.group("pe", fns, [B_kd[2], B_qd[2]], [B_pz[pi]])
                ti = nxt("pT", 3)
                T.op("act", lambda h, pi=pi, ti=ti, nk=nk: h.activation(out=pT[ti][0:nk, :], in_=pz[pi][0:nk, :],
                                                                 func=AF.Exp, scale=float(SCALE)),
                     [B_pz[pi]], [B_pT[ti]])
                T.op("dve", lambda h, ti=ti, Qb=Qb, nk=nk: h.tensor_tensor(out=pT[ti][0:nk, :], in0=pT[ti][0:nk, :],
                                                                    in1=Mg2[0:nk, Qb % 4, :], op=ALU.mult),
                     [B_pT[ti], B_const], [B_pT[ti]])
                pl = []
                for (lk, rq, vv, r, base) in g2jobs:
                    vblk = vd[2][0:nk, (base * 1) // 128, :] if False else None
                    pl.append((2, ti, nk, 32, ("g2", r, base), (lambda P, r=r: P.rearrange("p (m r) -> p r m", r=16)[:, r, :]), r))
                _pv_g2(pl, ti, nk, Qb, hs)
                rd_ap, rd_b = rdt[:], B_rdt
                T.op("dve", lambda h, rd_ap=rd_ap: h.reciprocal(out=rd_ap, in_=pst[1][:]), [B_pst[1]], [rd_b])
                T.op("dve", lambda h, rd_ap=rd_ap, hs=hs, t0=t0: h.tensor_tensor(
                    out=ycall[:, hs, t0:t0 + 512], in0=pst[0][:], in1=rd_ap, op=ALU.mult),
                    [B_pst[0], rd_b], [B_yc[hs][Qb]])
                acc_state["first"] = True

    acc_state = {"first": True}

    def _acc_mm(lhsT_v, nkk, rhs, ocf, reads):
        first = acc_state["first"]
        acc_state["first"] = False
        T.op("pe", lambda h: h.matmul(ocf(pst[0][:]), lhsT=lhsT_v, rhs=rhs, start=first, stop=False,
                                      skip_group_check=True), reads, [B_pst[0]])
        T.op("pe", lambda h: h.matmul(ocf(pst[1][:]), lhsT=onesb[0:nkk, :], rhs=rhs, start=first, stop=False,
                                      skip_group_check=True), reads + [B_const], [B_pst[1]])

    def _pv(plist, Qb, hs, last):
        for (g, ti, nkk, nq, vv, oc) in plist:
            _acc_mm(vv, nkk, pT[ti][0:nkk, 0:nq], oc, [B_pT[ti], B_vd[g]])

    def _pv_g2(pl, ti, nk, Qb, hs):
        M2 = SEQ // 16
        for (g, ti_, nkk, nq, tag, oc, r) in pl:
            base = r * M2
            j0, p0 = divmod(base, 128)
            if p0 + nk <= 128:
                vv = vd[2][p0:p0 + nk, j0, :]
                _acc_mm_g2(vv, p0, nk, pT[ti][0:nk, r * 32:(r + 1) * 32], oc, ti)
            else:
                raise NotImplementedError

    def _acc_mm_g2(vv, p0, nk, rhs, ocf, ti):
        first = acc_state["first"]
        acc_state["first"] = False
        T.op("pe", lambda h: h.matmul(ocf(pst[0][:]), lhsT=vv, rhs=rhs, start=first, stop=False,
                                      skip_group_check=True), [B_pT[ti], B_vd[2]], [B_pst[0]])
        T.op("pe", lambda h: h.matmul(ocf(pst[1][:]), lhsT=onesb[0:nk, :], rhs=rhs, start=first, stop=False,
                                      skip_group_check=True), [B_pT[ti], B_const], [B_pst[1]])

    def evac_to(psrc, bsrc, dst, dstb, func=AF.Copy, scale=None, bias=None):
        kw = {}
        if scale is not None:
            kw["scale"] = scale
        if bias is not None:
            kw["bias"] = bias
        rd = [bsrc, B_pp] if kw else [bsrc]
        T.op("act", lambda h: h.activation(out=dst, in_=psrc, func=func, **kw), rd, [dstb])

    def setup_layer(l):
        for j in range(a):
            t_ap, t_b = newwk(128)
            T.dma("sp", t_ap, wsT[l, j], writes=[t_b])
            T.op("dve", lambda h, t_ap=t_ap, j=j: h.tensor_tensor(out=wsm[:, j, :], in0=t_ap, in1=cstt[:, 128:256],
                                                                  op=ALU.mult), [t_b, B_cst], [B_wsm])
            T.dma("sp", bsm[:, j, :], bsr[l, j], writes=[B_wsm])

    def phase3(l, ncol, ns, src_ap_fn, bsrc, dst_ap_fn, dstb, t0, sample, last_pass):
        if sample:
            load_xT(xbs[l], B_xbs[l], 0, ncol)
        else:
            load_xT(xb[l], B_xb[l][t0 // TP], t0, ncol)
        res = {}

        def grab(pa, pb):
            res["p"] = (pa, pb)

        for j in range(a):
            h2 = 2 * ns
            lin_in(l, c.cA[1] + j, ncol, lambda pa, pb: res.update(c=evac(pa, pb)))
            tcap, tcb = res["c"]
            lin_in(l, c.cA[2] + j, ncol, grab)
            pa, pb = res["p"]
            T.op("dve", lambda h, j=j, h2=h2: h.tensor_copy(out=ext[:, 0:h2], in_=hista[:, j, 0:h2]), [B_ha[j]], [B_ext])
            T.op("dve", lambda h, pa=pa, tcap=tcap, h2=h2: h.tensor_tensor(out=ext[:, h2:h2 + ncol], in0=tcap, in1=pa,
                                                                          op=ALU.mult), [pb, tcb], [B_ext])
            T.op("dve", lambda h, j=j, h2=h2: h.tensor_copy(out=hista[:, j, 0:h2], in_=ext[:, ncol:ncol + h2]),
                 [B_ext], [B_ha[j]])
            acc, accb = newwk(ncol)
            T.op("dve", lambda h, acc=acc, j=j: h.tensor_scalar(out=acc, in0=ext[:, 0:ncol], scalar1=PP(l, "caw", j * 3),
                                                                scalar2=None, op0=ALU.mult), [B_ext, B_pp], [accb])
            for k in (1, 2):
                T.op("dve", lambda h, acc=acc, j=j, k=k: h.scalar_tensor_tensor(
                    out=acc, in0=ext[:, k * ns:k * ns + ncol], scalar=PP(l, "caw", j * 3 + k), in1=acc,
                    op0=ALU.mult, op1=ALU.add), [B_ext, B_pp, accb], [accb])
            lin_in(l, c.cA[3] + j, ncol, lambda pa, pb: res.update(z=evac(pa, pb, func=AF.Silu)))
            szap, szb = res["z"]
            lin_in(l, c.cA[0] + j, ncol, grab)
            pa, pb = res["p"]
            T.op("dve", lambda h, acc=acc, pa=pa: h.tensor_tensor(out=acc, in0=acc, in1=pa, op=ALU.mult), [accb, pb], [accb])
            T.op("dve", lambda h, acc=acc, szap=szap, j=j: h.tensor_tensor(out=yT[:, j, 0:ncol], in0=acc, in1=szap,
                                                                          op=ALU.mult), [accb, szb], [B_y[j]])
            if sample:
                T.dma("sp", o_ca_s[:, l, j, :, :], hista[:, j, 0:h2].rearrange("p (k s) -> p k s", s=ns),
                      reads=[B_ha[j]], writes=[OB["ca_s"]])
            elif last_pass:
                T.dma("sp", o_ca_p[:, l, j, :], hista[:, j, 0:2], reads=[B_ha[j]], writes=[OB["ca_p"]])
        for j in range(a):
            h30 = 30 * ns
            lin_in(l, c.cB[1] + j, ncol, lambda pa, pb: res.update(g=evac(pa, pb, func=AF.Sigmoid)))
            tg, tgb = res["g"]
            lin_in(l, c.cB[0] + j, ncol, grab)
            pa, pb = res["p"]
            T.op("dve", lambda h, j=j, h30=h30: h.tensor_copy(out=ext[:, 0:h30], in_=histb[:, j, 0:h30]), [B_hb[j]], [B_ext])
            T.op("dve", lambda h, pa=pa, tg=tg, h30=h30: h.tensor_tensor(out=ext[:, h30:h30 + ncol], in0=tg, in1=pa,
                                                                        op=ALU.mult), [pb, tgb], [B_ext])
            T.op("dve", lambda h, j=j, h30=h30: h.tensor_copy(out=histb[:, j, 0:h30], in_=ext[:, ncol:ncol + h30]),
                 [B_ext], [B_hb[j]])
            cb = cbuf[:, j, 0:ncol]
            T.op("dve", lambda h, cb=cb, j=j: h.tensor_scalar(out=cb, in0=ext[:, 0:ncol], scalar1=PP(l, "cbw", j * 31),
                                                              scalar2=PP(l, "cbb", j), op0=ALU.mult, op1=ALU.add),
                 [B_ext, B_pp], [B_cb[j]])
            for k in range(1, 31):
                T.op("dve", lambda h, cb=cb, j=j, k=k: h.scalar_tensor_tensor(
                    out=cb, in0=ext[:, k * ns:k * ns + ncol], scalar=PP(l, "cbw", j * 31 + k), in1=cb,
                    op0=ALU.mult, op1=ALU.add), [B_ext, B_pp, B_cb[j]], [B_cb[j]])
            lin_in(l, c.cB[2] + j, ncol, lambda pa, pb, j=j: evac_to(pa, pb, yT[:, a + j, 0:ncol], B_y[a + j], func=AF.Silu))
            if sample:
                T.dma("sp", o_cb_s[:, l, j, :, :], histb[:, j, 0:h30].rearrange("p (k s) -> p k s", s=ns),
                      reads=[B_hb[j]], writes=[OB["cb_s"]])
            elif last_pass:
                T.dma("sp", o_cb_p[:, l, j, :], histb[:, j, 0:30], reads=[B_hb[j]], writes=[OB["cb_p"]])
        mean, rstd = ln_stats([(cbuf[:, j, 0:ncol], B_cb[j]) for j in range(a)], ncol, c.WA)
        for j in range(a):
            n_ap, n_b = newwk(ncol)
            normalize(cbuf[:, j, 0:ncol], B_cb[j], mean, rstd, n_ap, n_b)
            T.op("act", lambda h, n_ap=n_ap, j=j: h.activation(out=n_ap, in_=n_ap, func=AF.Silu,
                                                               scale=PP(l, "lbg", j), bias=PP(l, "lbb", j)),
                 [n_b, B_pp], [n_b])
            T.op("dve", lambda h, n_ap=n_ap, j=j: h.tensor_tensor(out=yT[:, a + j, 0:ncol], in0=n_ap,
                                                                  in1=yT[:, a + j, 0:ncol], op=ALU.mult),
                 [n_b, B_y[a + j]], [B_y[a + j]])
        for hs in range(4):
            lin_in(l, c.cZ + hs, ncol, lambda pa, pb: res.update(z=evac(pa, pb, func=AF.Silu)))
            szap, szb = res["z"]
            if sample:
                T.op("dve", lambda h, szap=szap, hs=hs: h.tensor_tensor(out=yT[:, 2 * a + hs, 0:ncol], in0=ycs[:, hs, :],
                                                                        in1=szap, op=ALU.mult),
                     [szb, B_ycs], [B_y[2 * a + hs]])
            else:
                p = t0 // 512
                T.op("dve", lambda h, szap=szap, hs=hs: h.tensor_tensor(out=yT[:, 2 * a + hs, 0:ncol],
                                                                        in0=ycall[:, hs, t0:t0 + ncol], in1=szap,
                                                                        op=ALU.mult),
                     [szb, B_yc[hs][p]], [B_y[2 * a + hs]])
        for j in range(a):
            lin_in(l, c.cD[1] + j, ncol, lambda pa, pb, j=j: evac_to(pa, pb, cbuf[:, j, 0:ncol], B_cb[j]))
        mean, rstd = ln_stats([(cbuf[:, j, 0:ncol], B_cb[j]) for j in range(a)], ncol, c.WA)
        for j in range(a):
            yj = 2 * a + 4 + j
            n_ap, n_b = newwk(ncol)
            normalize(cbuf[:, j, 0:ncol], B_cb[j], mean, rstd, n_ap, n_b)
            mix, mixb = newwk(ncol)
            if sample:
                T.op("act", lambda h, n_ap=n_ap, j=j: h.activation(out=n_ap, in_=n_ap, func=AF.Identity,
                                                                   scale=PP(l, "ldg", j), bias=PP(l, "ldb", j)),
                     [n_b, B_pp], [n_b])
                T.dma("sp", o_gv_s[:, l, j, :], n_ap, reads=[n_b], writes=[OB["gv"]])
                T.op("dve", lambda h, n_ap=n_ap, mix=mix, j=j: h.tensor_scalar(
                    out=mix, in0=n_ap, scalar1=PP(l, "ws00", j), scalar2=PP(l, "bs0", j), op0=ALU.mult, op1=ALU.add),
                    [n_b, B_pp], [mixb])
            else:
                i = nxt("wkb", 8)
                vb, vbb = wkb[i], B_wkb[i]
                T.op("act", lambda h, n_ap=n_ap, vb=vb, j=j: h.activation(out=vb[:, 0:ncol], in_=n_ap, func=AF.Identity,
                                                                         scale=PP(l, "ldg", j), bias=PP(l, "ldb", j)),
                     [n_b, B_pp], [vbb])
                nq = ncol // 128
                fns = [(lambda h, q=q, vb=vb: h.transpose(pmb[:, q * 128:(q + 1) * 128], vb[:, q * 128:(q + 1) * 128],
                                                          identb[:])) for q in range(nq)]
                T.group("pe", fns, [vbb, B_const], [B_pmb])
                T.op("act", lambda h, j=j: h.activation(out=vnT[:, j, 0:ncol], in_=pmb[:, 0:ncol], func=AF.Copy),
                     [B_pmb], [B_vn[j]])
                pi = nxt("pm", 2)
                fns = [(lambda h, q=q, j=j, pi=pi: h.matmul(pm[pi][:, q * 128:(q + 1) * 128],
                                                            lhsT=vnT[:, j, q * 128:(q + 1) * 128], rhs=wsm[:, j, :],
                                                            start=True, stop=True)) for q in range(nq)]
                T.group("pe", fns, [B_vn[j], B_wsm], [B_pm[pi]])
                for q in range(nq):
                    T.op("dve", lambda h, q=q, j=j, pi=pi, mix=mix: h.tensor_tensor(
                        out=mix[:, q * 128:(q + 1) * 128], in0=pm[pi][:, q * 128:(q + 1) * 128], in1=bsm[:, j, :],
                        op=ALU.add), [B_pm[pi], B_wsm], [mixb])
            lin_in(l, c.cD[0] + j, ncol, grab)
            pa, pb = res["p"]
            T.op("dve", lambda h, mix=mix, pa=pa: h.tensor_tensor(out=mix, in0=mix, in1=pa, op=ALU.mult), [mixb, pb], [mixb])
            lin_in(l, c.cD[2] + j, ncol, lambda pa, pb: res.update(z=evac(pa, pb, func=AF.Silu)))
            szap, szb = res["z"]
            T.op("dve", lambda h, mix=mix, szap=szap, yj=yj: h.tensor_tensor(out=yT[:, yj, 0:ncol], in0=mix, in1=szap,
                                                                            op=ALU.mult), [mixb, szb], [B_y[yj]])
        kr = ((0, a), (a, 2 * a), (2 * a, 2 * a + 4), (2 * a + 4, 3 * a + 4))
        for jo in range(KC):
            pump(1, need=("out", l, jo))
            T.dma("pool", wso[:], wb_out[l, jo], reads=[B_wb[("out", l, jo)]], writes=[B_wso])
            macc, maccb = newwk(ncol)
            for bi in range(4):
                lin_in(l, c.cG + bi * KC + jo, ncol,
                       lambda pa, pb, bi=bi: res.update(g=evac(pa, pb, func=AF.Sigmoid, bias=PP(l, "bg", bi * KC + jo))))
                sg, sgb = res["g"]
                pi = nxt("pm", 2)
                k0, k1 = kr[bi]
                fns = [(lambda h, kc=kc, pi=pi, k0=k0, k1=k1: h.matmul(
                    pm[pi][:, 0:ncol], lhsT=wso[:, kc * 128:(kc + 1) * 128], rhs=yT[:, kc, 0:ncol],
                    start=(kc == k0), stop=(kc == k1 - 1))) for kc in range(k0, k1)]
                T.group("pe", fns, [B_wso] + [B_y[kc] for kc in range(k0, k1)], [B_pm[pi]])
                if bi == 0:
                    T.op("dve", lambda h, sg=sg, pi=pi, macc=macc: h.tensor_tensor(out=macc, in0=sg, in1=pm[pi][:, 0:ncol],
                                                                                  op=ALU.mult), [sgb, B_pm[pi]], [maccb])
                else:
                    T.op("dve", lambda h, sg=sg, pi=pi: h.tensor_tensor(out=sg, in0=sg, in1=pm[pi][:, 0:ncol], op=ALU.mult),
                         [sgb, B_pm[pi]], [sgb])
                    if bi < 3:
                        T.op("dve", lambda h, sg=sg, macc=macc: h.tensor_tensor(out=macc, in0=macc, in1=sg, op=ALU.add),
                             [sgb, maccb], [maccb])
                    else:
                        T.op("dve", lambda h, sg=sg, macc=macc, jo=jo: h.tensor_tensor(out=mT[:, jo, 0:ncol], in0=macc,
                                                                                      in1=sg, op=ALU.add),
                             [sgb, maccb], [B_m[jo]])
        for jo in range(KC):
            xr, xrb = newwk(ncol)
            T.dma("sp", xr, src_ap_fn(jo), reads=[bsrc], writes=[xrb])
            lin(("o", l, jo), wb_o[l, jo], KC, lambda kc: mT[:, kc, 0:ncol], B_m, ncol, grab)
            pa, pb = res["p"]
            T.op("dve", lambda h, xr=xr, pa=pa: h.scalar_tensor_tensor(out=xr, in0=xr, scalar=float(c.ALPHA), in1=pa,
                                                                       op0=ALU.mult, op1=ALU.add), [xrb, pb], [xrb])
            sq, sqb = newwk(ncol)
            T.op("act", lambda h, xr=xr, sq=sq: h.activation(out=sq, in_=xr, func=AF.Square), [xrb], [sqb])
            T.op("pe", lambda h, xr=xr, jo=jo: h.matmul(pst[0][:, 0:ncol], lhsT=onesf[:], rhs=xr, start=(jo == 0),
                                                        stop=(jo == KC - 1)), [xrb, B_const], [B_pst[0]])
            T.op("pe", lambda h, sq=sq, jo=jo: h.matmul(pst[1][:, 0:ncol], lhsT=onesf[:], rhs=sq, start=(jo == 0),
                                                        stop=(jo == KC - 1)), [sqb, B_const], [B_pst[1]])
            T.dma("sp", rT[jo, :, 0:ncol], xr, reads=[xrb], writes=[B_rT[jo]])
        mean, rstd = ln_finish(ncol, D)
        for jo in range(KC):
            xr, xrb = newwk(ncol)
            T.dma("sp", xr, rT[jo, :, 0:ncol], reads=[B_rT[jo]], writes=[xrb])
            normalize(xr, xrb, mean, rstd, xr, xrb)
            T.op("act", lambda h, xr=xr, jo=jo: h.activation(out=xr, in_=xr, func=AF.Identity, scale=PP(l, "lg", jo),
                                                             bias=PP(l, "lb", jo)), [xrb, B_pp], [xrb])
            T.dma("sp", dst_ap_fn(jo), xr, reads=[xrb], writes=[dstb])
            if l + 1 < L:
                xbt, xbtb = newwkb()
                T.op("act", lambda h, xbt=xbt, xr=xr: h.activation(out=xbt[:, 0:ncol], in_=xr, func=AF.Copy), [xrb], [xbtb])
                if sample:
                    T.dma("sp", xbs[l + 1][jo, :, :], xbt[:, 0:ncol], reads=[xbtb], writes=[B_xbs[l + 1]])
                else:
                    T.dma("sp", xb[l + 1][jo, :, t0:t0 + ncol], xbt[:, 0:ncol], reads=[xbtb], writes=[B_xb[l + 1][t0 // TP]])

    def ln_finish(ncol, nch_total):
        mean, rstd, tmp = stat[0][:, 0:ncol], stat[1][:, 0:ncol], stat[2][:, 0:ncol]
        inv = 1.0 / nch_total
        T.op("act", lambda h: h.activation(out=mean, in_=pst[0][:, 0:ncol], func=AF.Copy, scale=inv),
             [B_pst[0]], [B_stat[0]])
        T.op("dve", lambda h: h.tensor_tensor(out=tmp, in0=mean, in1=mean, op=ALU.mult), [B_stat[0]], [B_stat[2]])
        T.op("dve", lambda h: h.scalar_tensor_tensor(out=tmp, in0=pst[1][:, 0:ncol], scalar=inv, in1=tmp,
                                                     op0=ALU.mult, op1=ALU.subtract),
             [B_pst[1], B_stat[2]], [B_stat[2]])
        T.op("dve", lambda h: h.tensor_scalar(out=tmp, in0=tmp, scalar1=LN_EPS, scalar2=None, op0=ALU.add),
             [B_stat[2]], [B_stat[2]])
        T.op("act", lambda h: h.activation(out=tmp, in_=tmp, func=AF.Sqrt), [B_stat[2]], [B_stat[2]])
        T.op("dve", lambda h: h.reciprocal(out=rstd, in_=tmp), [B_stat[2]], [B_stat[1]])
        return mean, rstd

    def newwkb():
        i = nxt("wkb", 8)
        return wkb[i], B_wkb[i]

    def sample_attn(l, src_fn, srcb):
        load_xT(xbs[l], B_xbs[l], 0, NS)
        T.dma("sp", ropest[:], ropes, writes=[B_rope])
        res = {}
        for which, cbase, off in (("q", c.cQ, 0), ("k", c.cK, 12 * NS), ("v", c.cV, 24 * NS)):
            for hh in range(NH):
                g, hs = hh // 4, hh % 4
                lin_in(l, cbase + hh, NS, lambda pa, pb: res.update(x=evac(pa, pb)))
                xap, xb = res["x"]
                dst = sm[:, off + hh * NS:off + (hh + 1) * NS]
                if which == "v":
                    T.op("dve", lambda h, dst=dst, xap=xap: h.tensor_copy(out=dst, in_=xap), [xb], [B_sm])
                else:
                    pi = nxt("pm", 2)
                    T.op("pe", lambda h, pi=pi, xap=xap: h.matmul(pm[pi][:, 0:NS], lhsT=swapf, rhs=xap, start=True, stop=True),
                         [xb, B_cst], [B_pm[pi]])
                    t2, b2 = newwk(NS)
                    T.op("dve", lambda h, t2=t2, pi=pi: h.tensor_scalar(out=t2, in0=pm[pi][:, 0:NS], scalar1=ropest[:, 1, 0:1],
                                                                        scalar2=None, op0=ALU.mult), [B_pm[pi], B_rope], [b2])
                    T.op("dve", lambda h, dst=dst, xap=xap, t2=t2: h.scalar_tensor_tensor(
                        out=dst, in0=xap, scalar=ropest[:, 0, 0:1], in1=t2, op0=ALU.mult, op1=ALU.add),
                        [xb, b2, B_rope], [B_sm])
                if which != "q":
                    T.dma("sp", o_kv_s[g][l, 0 if which == "k" else 1, hs, :, :], dst, reads=[B_sm], writes=[OB["kv_s"]])
        T.op("act", lambda h: h.activation(out=smb[:, 0:36 * NS], in_=sm[:, 0:36 * NS], func=AF.Copy), [B_sm], [B_smb])
        first = [True]
        for s in range(NS):
            for hs in range(4):
                col = hs * NS + s
                for g in range(3):
                    dil = DILS[g]
                    hh = g * 4 + hs
                    kc_t, kc_b = newwk(128)
                    vc_t, vc_b = newwk(128)
                    T.dma("sp", kc_t, cache[g][l, s, :, 0, hs, :].rearrange("(i r) d -> r i d", r=dil)[0], writes=[kc_b])
                    T.dma("sp", vc_t, cache[g][l, s, :, 1, hs, :].rearrange("(i r) d -> r i d", r=dil)[0], writes=[vc_b])
                    pi = nxt("pm", 2)
                    T.op("pe", lambda h, pi=pi, kc_t=kc_t: h.transpose(pm[pi][:, 0:128], kc_t, identf), [kc_b, B_cst], [B_pm[pi]])
                    kTb, kTbb = newwkb()
                    T.op("act", lambda h, kTb=kTb, pi=pi: h.activation(out=kTb[:, 0:128], in_=pm[pi][:, 0:128], func=AF.Copy),
                         [B_pm[pi]], [kTbb])
                    kcol = 12 * NS + hh * NS + s
                    T.op("act", lambda h, kTb=kTb, kcol=kcol: h.activation(out=kTb[:, 128:129], in_=smb[:, kcol:kcol + 1],
                                                                          func=AF.Copy), [B_smb, kTbb], [kTbb])
                    qcol = hh * NS + s
                    pz_i = nxt("pz", 3)
                    T.op("pe", lambda h, pz_i=pz_i, qcol=qcol, kTb=kTb: h.matmul(
                        pz[pz_i][0:1, 0:129], lhsT=smb[:, qcol:qcol + 1], rhs=kTb[:, 0:129], start=True, stop=True),
                        [B_smb, kTbb], [B_pz[pz_i]])
                    prow, prowb = newwk(129)
                    T.op("act", lambda h, prow=prow, pz_i=pz_i: h.activation(out=prow[0:1, :], in_=pz[pz_i][0:1, 0:129],
                                                                            func=AF.Exp, scale=float(SCALE)),
                         [B_pz[pz_i]], [prowb])
                    pi2 = nxt("pm", 2)
                    T.op("pe", lambda h, pi2=pi2, prow=prow: h.transpose(pm[pi2][:, 0:1], prow[0:1, 0:128], cstt[0:1, 0:1]),
                         [prowb, B_cst], [B_pm[pi2]])
                    pcol, pcolb = newwkb()
                    T.op("act", lambda h, pcol=pcol, pi2=pi2: h.activation(out=pcol[:, 0:1], in_=pm[pi2][:, 0:1], func=AF.Copy),
                         [B_pm[pi2]], [pcolb])
                    T.op("act", lambda h, pcol=pcol, prow=prow: h.activation(out=pcol[0:1, 8:9], in_=prow[0:1, 128:129],
                                                                            func=AF.Copy), [prowb, pcolb], [pcolb])
                    vcol = 24 * NS + hh * NS + s
                    pi3 = nxt("pm", 2)
                    T.op("pe", lambda h, pi3=pi3, vcol=vcol: h.transpose(pm[pi3][0:1, 0:128], sm[:, vcol:vcol + 1], identf),
                         [B_sm, B_cst], [B_pm[pi3]])
                    vb_t, vb_b = newwkb()
                    T.op("act", lambda h, vb_t=vb_t, pi3=pi3: h.activation(out=vb_t[0:1, 128:256], in_=pm[pi3][0:1, 0:128],
                                                                          func=AF.Copy), [B_pm[pi3]], [vb_b])
                    T.op("act", lambda h, vb_t=vb_t, vc_t=vc_t: h.activation(out=vb_t[:, 0:128], in_=vc_t, func=AF.Copy),
                         [vc_b, vb_b], [vb_b])
                    f0 = first[0]
                    first[0] = False
                    T.op("pe", lambda h, vb_t=vb_t, pcol=pcol, col=col, f0=f0: h.matmul(
                        pst[0][:, col:col + 1], lhsT=vb_t[:, 0:128], rhs=pcol[:, 0:1], start=f0, stop=False,
                        skip_group_check=True), [vb_b, pcolb], [B_pst[0]])
                    T.op("pe", lambda h, vb_t=vb_t, pcol=pcol, col=col: h.matmul(
                        pst[0][:, col:col + 1], lhsT=vb_t[0:1, 128:256], rhs=pcol[0:1, 8:9], start=False, stop=False,
                        skip_group_check=True), [vb_b, pcolb], [B_pst[0]])
                    T.op("pe", lambda h, pcol=pcol, col=col, f0=f0: h.matmul(
                        pst[1][:, col:col + 1], lhsT=onesb[:, :], rhs=pcol[:, 0:1], start=f0, stop=False,
                        skip_group_check=True), [B_const, pcolb], [B_pst[1]])
                    T.op("pe", lambda h, pcol=pcol, col=col: h.matmul(
                        pst[1][:, col:col + 1], lhsT=onesb[0:1, :], rhs=pcol[0:1, 8:9], start=False, stop=False,
                        skip_group_check=True), [B_const, pcolb], [B_pst[1]])
        rd_ap, rd_b = newwk(4 * NS)
        T.op("dve", lambda h: h.reciprocal(out=rd_ap, in_=pst[1][:, 0:4 * NS]), [B_pst[1]], [rd_b])
        T.op("dve", lambda h: h.tensor_tensor(out=ycs[:].rearrange("p h s -> p (h s)"), in0=pst[0][:, 0:4 * NS], in1=rd_ap,
                                              op=ALU.mult), [B_pst[0], rd_b], [B_ycs])

    def sample_layer(l):
        lastl = (l == L - 1)
        src_t = xsT if l == 0 else actsT[(l - 1) % len(actsT)]
        srcb = Bxs if l == 0 else B_acts[(l - 1) % len(actsT)]
        sfn = lambda jo, src_t=src_t: src_t[jo, :, :]
        sample_attn(l, sfn, srcb)
        for j in range(a):
            T.dma("sp", hista[:, j, :].rearrange("p (k s) -> p k s", s=NS), sca[:, l, j], writes=[B_ha[j]])
            T.dma("sp", histb[:, j, :].rearrange("p (k s) -> p k s", s=NS), scb[:, l, j], writes=[B_hb[j]])
        if lastl:
            dfn, dbuf = (lambda jo: o_ysT[jo, :, :]), OB["ys"]
        else:
            dfn, dbuf = (lambda jo, l=l: actsT[l % len(actsT)][jo, :, :]), B_acts[l % len(actsT)]
        phase3(l, NS, NS, sfn, srcb, dfn, dbuf, 0, True, False)

    Bxs = Buf("xsin")
    Bx = Buf("xin")
    for l in range(min(L, c.LIMIT_L)):
        lastl = (l == L - 1)
        src_t = xpT if l == 0 else actT[(l - 1) % len(actT)]
        srcb = (lambda p: Bx) if l == 0 else (lambda p, l=l: B_act[(l - 1) % len(actT)][p])
        setup_layer(l)
        for p in range(NP):
            phase1_prompt(l, p, xb[l], B_xb[l][p])
        if c.DO_ATTN:
            phase2(l)
        for j in range(a):
            T.op("dve", lambda h, j=j: h.memset(hista[:, j, :], 0.0), [], [B_ha[j]])
            T.op("dve", lambda h, j=j: h.memset(histb[:, j, :], 0.0), [], [B_hb[j]])
        for p in range(NP):
            t0 = p * TP
            if lastl:
                dfn, dbuf = (lambda jo, t0=t0: o_ypT[jo, :, t0:t0 + TP]), OB["yp"]
            else:
                dfn, dbuf = (lambda jo, t0=t0, l=l: actT[l % len(actT)][jo, :, t0:t0 + TP]), B_act[l % len(actT)][p]
            phase3(l, TP, 1, (lambda jo, t0=t0, src_t=src_t: src_t[jo, :, t0:t0 + TP]), srcb(p), dfn, dbuf, t0,
                   False, p == NP - 1)
        if c.DO_SAMPLE:
            sample_layer(l)
    T.finish(out_bufs)
    T.replay()
    return nc


def _chunk_w(w, kcn, ncn):
    return np.ascontiguousarray(w.reshape(kcn, 128, ncn, 128).transpose(2, 1, 0, 3)).reshape(ncn, 128, kcn * 128)


def _fm(x, nchunk):
    return np.ascontiguousarray(x.reshape(x.shape[0], nchunk, 128).transpose(1, 2, 0))


def _rope_tables(pos):
    half = HD // 2
    inv = (ROPE_THETA ** (-np.arange(half, dtype=np.float32) / half)).astype(np.float32)
    ang = pos.astype(np.float32)[:, None] * inv[None, :]
    cos, sin = np.cos(ang).astype(np.float32), np.sin(ang).astype(np.float32)
    cosT = np.concatenate([cos, cos], 1).T
    sinT = np.concatenate([-sin, sin], 1).T
    return np.ascontiguousarray(np.stack([cosT, sinT], 1)).astype(np.float32)


def _consts():
    i = np.arange(128)
    ident = np.eye(128, dtype=np.float32)
    U = (i[:, None] <= i[None, :]).astype(np.float32)
    Lm = (i[None, :] <= i[:, None]).astype(np.float32)
    sw = np.zeros((128, 128), np.float32)
    sw[(i + 64) % 128, i] = 1.0
    return np.ascontiguousarray(np.concatenate([ident, U, Lm, sw], 1))


def prep_inputs(c, inp):
    L, a, KC, NS = c.L, c.a, c.KC, c.NS
    f = lambda k: np.asarray(inp[k], dtype=np.float32)
    w_in = np.stack([_chunk_w(f("w_in")[l], KC, c.NCH) for l in range(L)])
    wcat = [np.concatenate([f("w_out_a")[l], f("w_out_b")[l], f("w_out_c")[l], f("w_out_d")[l]], 0) for l in range(L)]
    w_out = np.stack([_chunk_w(wcat[l], c.KCO, KC) for l in range(L)])
    w_o = np.stack([_chunk_w(f("w_o")[l], KC, KC) for l in range(L)])
    pp = np.zeros((128, L, c.NPP), np.float32)

    def put(name, l, arr):
        n = arr.shape[0] // 128
        o = c.pp[name]
        pp[:, l, o:o + n] = arr.reshape(n, 128).T

    for l in range(L):
        caw = f("conv_a_w")[l]
        pp[:, l, c.pp["caw"]:c.pp["caw"] + 3 * a] = caw.reshape(3, a, 128).transpose(2, 1, 0).reshape(128, 3 * a)
        cbw = f("conv_b_w")[l]
        pp[:, l, c.pp["cbw"]:c.pp["cbw"] + 31 * a] = cbw.reshape(31, a, 128).transpose(2, 1, 0).reshape(128, 31 * a)
        put("cbb", l, f("conv_b_bias")[l]); put("lbg", l, f("ln_b_g")[l]); put("lbb", l, f("ln_b_b")[l])
        put("ldg", l, f("ln_d_g")[l]); put("ldb", l, f("ln_d_b")[l]); put("bg", l, f("b_gate")[l])
        put("lg", l, f("ln_g")[l]); put("lb", l, f("ln_b")[l])
        pp[:, l, c.pp["ws00"]:c.pp["ws00"] + a] = f("w_s")[l][:, 0, 0][None, :]
        pp[:, l, c.pp["bs0"]:c.pp["bs0"] + a] = f("b_s")[l][:, 0][None, :]
    pp = np.ascontiguousarray(pp.reshape(128, L * c.NPP))
    wsT = np.ascontiguousarray(f("w_s").transpose(0, 1, 3, 2))
    bsr = np.ascontiguousarray(np.broadcast_to(f("b_s")[:, :, None, :], (L, a, 128, 128)))
    cst = _consts()
    ropep = _rope_tables(np.arange(c.SEQ))
    ropes = _rope_tables(np.array([c.PAST]))
    maps = []
    for ci in range(c.NCORES):
        s0 = ci * NS
        sca = f("state_conv_a")[:, s0:s0 + NS]
        scb = f("state_conv_b")[:, s0:s0 + NS]
        m = {
            "xpT": _fm(f("x_prompt")[ci], KC),
            "xsT": _fm(f("x_sample")[s0:s0 + NS, 0], KC),
            "w_in": w_in, "w_out": w_out, "w_o": w_o, "pp": pp, "wsT": wsT, "bsr": bsr, "cst": cst,
            "ropep": ropep, "ropes": ropes,
            "sca": np.ascontiguousarray(sca.reshape(L, NS, 2, a, 128).transpose(4, 0, 3, 2, 1)),
            "scb": np.ascontiguousarray(scb.reshape(L, NS, 30, a, 128).transpose(4, 0, 3, 2, 1)),
        }
        for g, k in enumerate(("cache_kv_w128", "cache_kv_w512", "cache_kv_w2048")):
            m["cache%d" % g] = np.ascontiguousarray(f(k)[:, s0:s0 + NS])
        maps.append(m)
    return maps


def assemble(c, res):
    L, a, NS, nco = c.L, c.a, c.NS, c.NCORES
    R = res

    def unfm(x):
        return np.ascontiguousarray(x.transpose(2, 0, 1).reshape(x.shape[2], -1))

    y_p = np.stack([unfm(R[i]["o_ypT"]) for i in range(nco)])
    y_s = np.concatenate([unfm(R[i]["o_ysT"]) for i in range(nco)])[:, None, :]
    ca_p = np.stack([R[i]["o_ca_p"].transpose(1, 3, 2, 0).reshape(L, 2, a * 128) for i in range(nco)], 1)
    ca_s = np.concatenate([R[i]["o_ca_s"].transpose(1, 4, 3, 2, 0).reshape(L, NS, 2, a * 128) for i in range(nco)], 1)
    cb_p = np.stack([R[i]["o_cb_p"].transpose(1, 3, 2, 0).reshape(L, 30, a * 128) for i in range(nco)], 1)
    cb_s = np.concatenate([R[i]["o_cb_s"].transpose(1, 4, 3, 2, 0).reshape(L, NS, 30, a * 128) for i in range(nco)], 1)
    outs = [y_p, y_s, ca_p, ca_s, cb_p, cb_s]
    for g in range(3):
        kp = np.stack([R[i]["o_kv_p%d" % g].transpose(0, 4, 1, 2, 3) for i in range(nco)], 1)
        ks = np.concatenate([R[i]["o_kv_s%d" % g].transpose(0, 4, 1, 2, 3) for i in range(nco)], 1)[:, :, None]
        outs += [kp, ks]
    gv = np.concatenate([R[i]["o_gv_s"].transpose(1, 3, 2, 0).reshape(L, NS, a * 128) for i in range(nco)], 1)[:, :, None, :]
    outs.append(gv)
    return tuple(np.ascontiguousarray(o, dtype=np.float32) for o in outs)


def run_cfg(c, inp, trace=False):
    nc = build(c)
    maps = prep_inputs(c, inp)
    res = run_bass_kernel_spmd(nc, maps, core_ids=list(range(c.NCORES)), **({"trace": True} if trace else {}))
    return assemble(c, res.results), res


def kernel(**inputs):
    c = Cfg()
    outs, _ = run_cfg(c, inputs)
    return outs
```

```python
import numpy as np
from contextlib import ExitStack
import concourse.bass as bass
import concourse.mybir as mybir
from concourse.bass_utils import run_bass_kernel_spmd

F32 = mybir.dt.float32
BF16 = mybir.dt.bfloat16
AF = mybir.ActivationFunctionType
ALU = mybir.AluOpType

HD = 128
NH = 12
WINDOWS = (128, 512, 2048)
DILS = (1, 4, 16)
LN_EPS = 1e-5
ROPE_THETA = 10000.0
SELF_WAIT = True


class Cfg:
    def __init__(self, D=4096, SEQ=2048, L=2, NS=2, PAST=16384, NCORES=4):
        self.D, self.SEQ, self.L, self.NS, self.PAST, self.NCORES = D, SEQ, L, NS, PAST, NCORES
        self.KC = D // 128
        self.a = D // 512
        self.WA = D // 4
        a = self.a
        self.cA = (0, a, 2 * a, 3 * a)
        self.cB = (4 * a, 5 * a, 6 * a)
        self.cQ, self.cK, self.cV = 7 * a, 7 * a + 12, 7 * a + 24
        self.cZ = 7 * a + 36
        self.cD = (7 * a + 40, 8 * a + 40, 9 * a + 40)
        self.cG = 10 * a + 40
        self.NCH = self.cG + 4 * self.KC
        self.KCO = 3 * a + 4
        self.TP = 256
        self.NP = SEQ // self.TP
        self.QT = 512
        self.NQ = SEQ // 512
        self.ALPHA = (2 * L) ** 0.25
        o = 0
        self.pp = {}
        for name, n in (("caw", 3 * a), ("cbw", 31 * a), ("cbb", a), ("lbg", a), ("lbb", a),
                        ("ldg", a), ("ldb", a), ("bg", 4 * self.KC), ("lg", self.KC), ("lb", self.KC),
                        ("ws00", a), ("bs0", a)):
            self.pp[name] = o
            o += n
        self.NPP = o
        self.keep = [min(w, SEQ) for w in WINDOWS]
        self.LIMIT_L = L
        self.DO_ATTN = True
        self.DO_SAMPLE = True
        self.clen = [min(w, PAST) for w in WINDOWS]


class Buf:
    __slots__ = ("name", "w", "r")

    def __init__(self, name):
        self.name, self.w, self.r = name, None, {}


class Eng:
    def __init__(self, name, sem):
        self.name, self.sem, self.cnt, self.seen, self.prog = name, sem, 0, {}, []
        self.dsems, self.dnext = [], 0


class Tracker:
    def __init__(self, nc, stack):
        self.nc, self.stack = nc, stack
        self.E = {}
        for n in ("pe", "act", "dve", "pool", "sp"):
            self.E[n] = Eng(n, stack.enter_context(nc.semaphore("s_" + n)))
        for n, k in (("sp", 8), ("pool", 16), ("act", 2)):
            for i in range(k):
                self.E[n].dsems.append([stack.enter_context(nc.semaphore("d_%s%d" % (n, i))), 0])
        self.final = []

    def _wait(self, e, toks):
        need = {}
        for sem, val in toks:
            if sem is e.sem and not SELF_WAIT:
                continue
            k = id(sem)
            if e.seen.get(k, 0) < val and need.get(k, (None, 0))[1] < val:
                need[k] = (sem, val)
        for k, (sem, val) in need.items():
            e.seen[k] = val
            e.prog.append(lambda h, sem=sem, val=val: h.wait_ge(sem, val))

    def _deps(self, reads, writes):
        toks = []
        for b in reads:
            if b.w:
                toks.append(b.w)
        for b in writes:
            if b.w:
                toks.append(b.w)
            toks.extend(b.r.values())
        return toks

    def _mark(self, tok, reads, writes):
        for b in reads:
            b.r[id(tok[0])] = tok
        for b in writes:
            b.w, b.r = tok, {}

    def op(self, en, fn, reads=(), writes=()):
        self.group(en, [fn], reads, writes)

    def group(self, en, fns, reads=(), writes=()):
        e = self.E[en]
        toks = self._deps(reads, writes)
        if en == "pe":
            toks = [t for t in toks if t[0] is not e.sem]
        self._wait(e, toks)
        e.cnt += 1
        sem, cnt = e.sem, e.cnt
        for f in fns[:-1]:
            e.prog.append(lambda h, f=f: f(h))
        last = fns[-1]
        e.prog.append(lambda h, f=last, sem=sem: f(h).then_inc(sem, 1))
        self._mark((sem, cnt), reads, writes)

    def dma(self, qn, out, in_, reads=(), writes=(), **kw):
        e = self.E[qn]
        slot = e.dsems[e.dnext % len(e.dsems)]
        e.dnext += 1
        toks = self._deps(reads, writes)
        if slot[1] > 0:
            toks.append((slot[0], slot[1]))
        self._wait(e, toks)
        slot[1] += 16
        sem, val = slot[0], slot[1]
        e.prog.append(lambda h, sem=sem: h.dma_start(out=out, in_=in_, **kw).then_inc(sem, 16))
        self._mark((sem, val), reads, writes)
        return (sem, val)

    def finish(self, bufs):
        e = self.E["sp"]
        toks = []
        for b in bufs:
            if b.w:
                toks.append(b.w)
        for q in ("sp", "pool", "act"):
            for s in self.E[q].dsems:
                if s[1] > 0:
                    toks.append((s[0], s[1]))
        self._wait(e, toks)

    def replay(self):
        nc = self.nc
        with nc.Block() as block:
            @block.tensor
            def _(h):
                for f in self.E["pe"].prog:
                    f(h)

            @block.scalar
            def _(h):
                for f in self.E["act"].prog:
                    f(h)

            @block.vector
            def _(h):
                for f in self.E["dve"].prog:
                    f(h)

            @block.gpsimd
            def _(h):
                for f in self.E["pool"].prog:
                    f(h)

            @block.sync
            def _(h):
                for f in self.E["sp"].prog:
                    f(h)


def build(cfg):
    c = cfg
    D, SEQ, L, NS, KC, a, TP, NP = c.D, c.SEQ, c.L, c.NS, c.KC, c.a, c.TP, c.NP
    NCH, KCO = c.NCH, c.KCO
    nc = bass.Bass("TRN2", target_bir_lowering=False)
    stack = ExitStack()
    T = Tracker(nc, stack)

    def din(name, shape, dt=F32):
        return nc.dram_tensor(name, list(shape), dt, kind="ExternalInput").ap()

    def dout(name, shape, dt=F32):
        return nc.dram_tensor(name, list(shape), dt, kind="ExternalOutput").ap()

    def dscr(name, shape, dt):
        return nc.dram_tensor(name, list(shape), dt, kind="Internal").ap()

    def sb(name, shape, dt=F32):
        return stack.enter_context(nc.sbuf_tensor(name, list(shape), dt))

    def ps(name, shape=(128, 512), dt=F32):
        return stack.enter_context(nc.psum_tensor(name, list(shape), dt))

    xpT = din("xpT", [KC, 128, SEQ])
    xsT = din("xsT", [KC, 128, NS])
    w_in = din("w_in", [L, NCH, 128, KC * 128])
    w_out = din("w_out", [L, KC, 128, KCO * 128])
    w_o = din("w_o", [L, KC, 128, KC * 128])
    ppd = din("pp", [128, L * c.NPP])
    wsT = din("wsT", [L, a, 128, 128])
    bsr = din("bsr", [L, a, 128, 128])
    cst = din("cst", [128, 4 * 128])
    ropep = din("ropep", [128, 2, SEQ])
    ropes = din("ropes", [128, 2, 1])
    sca = din("sca", [128, L, a, 2, NS])
    scb = din("scb", [128, L, a, 30, NS])
    cache = [din("cache%d" % g, [L, NS, c.clen[g], 2, 4, 128]) for g in range(3)]

    o_ypT = dout("o_ypT", [KC, 128, SEQ])
    o_ysT = dout("o_ysT", [KC, 128, NS])
    o_ca_p = dout("o_ca_p", [128, L, a, 2])
    o_ca_s = dout("o_ca_s", [128, L, a, 2, NS])
    o_cb_p = dout("o_cb_p", [128, L, a, 30])
    o_cb_s = dout("o_cb_s", [128, L, a, 30, NS])
    o_kv_p = [dout("o_kv_p%d" % g, [L, 2, 4, 128, c.keep[g]]) for g in range(3)]
    o_kv_s = [dout("o_kv_s%d" % g, [L, 2, 4, 128, NS]) for g in range(3)]
    o_gv_s = dout("o_gv_s", [128, L, a, NS])
    out_bufs = [Buf("out%d" % i) for i in range(16)]
    OB = dict(yp=out_bufs[0], ys=out_bufs[1], ca_p=out_bufs[2], ca_s=out_bufs[3], cb_p=out_bufs[4],
              cb_s=out_bufs[5], kv_p=out_bufs[6], kv_s=out_bufs[7], gv=out_bufs[8])

    WG = 32
    _wbin = [[dscr("wb_in%d_%d" % (l, gi), [min(WG, NCH - gi * WG), 128, KC * 128], BF16)
              for gi in range((NCH + WG - 1) // WG)] for l in range(L)]

    class _WbIn:
        def __getitem__(self, lm):
            l, m = lm
            return _wbin[l][m // WG][m % WG]
    wb_in = _WbIn()
    _wbout = [dscr("wb_out%d" % l, [KC, 128, KCO * 128], BF16) for l in range(L)]
    _wbo = [dscr("wb_o%d" % l, [KC, 128, KC * 128], BF16) for l in range(L)]

    class _Wb2:
        def __init__(self, ts):
            self.ts = ts

        def __getitem__(self, lj):
            return self.ts[lj[0]][lj[1]]
    wb_out = _Wb2(_wbout)
    wb_o = _Wb2(_wbo)
    actT = [dscr("actT%d" % i, [KC, 128, SEQ], F32) for i in range(max(1, L - 1))]
    actsT = [dscr("actsT%d" % i, [KC, 128, NS], F32) for i in range(max(1, L - 1))]
    rT = dscr("rT", [KC, 128, TP], F32)
    xb = [dscr("xb%d" % i, [KC, 128, SEQ], BF16) for i in range(L)]
    xbs = [dscr("xbs%d" % i, [KC, 128, NS], BF16) for i in range(L)]
    B_xb = [[Buf("xb%d_%d" % (i, p)) for p in range(NP)] for i in range(L)]
    B_xbs = [Buf("xbs%d" % i) for i in range(L)]
    qT_s = [dscr("qT_s%d" % g, [128, 4, SEQ], BF16) for g in range(3)]
    kT_s = [dscr("kT_s%d" % g, [128, 4, SEQ], BF16) for g in range(3)]
    v_s = [dscr("v_s%d" % g, [4, SEQ, 128], BF16) for g in range(3)]
    B_wb = {}
    B_act = [[Buf("actT%d_%d" % (i, p)) for p in range(NP)] for i in range(len(actT))]
    B_acts = [Buf("actsT%d" % i) for i in range(len(actsT))]
    B_rT = [Buf("rT%d" % j) for j in range(KC)]
    B_q = [[Buf("q%d_%d" % (g, h)) for h in range(4)] for g in range(3)]
    B_k = [[Buf("k%d_%d" % (g, h)) for h in range(4)] for g in range(3)]
    B_v = [[Buf("v%d_%d" % (g, h)) for h in range(4)] for g in range(3)]

    NSLOT = 4
    wslot = [sb("wslot%d" % i, [128, KC * 128], BF16) for i in range(NSLOT)]
    B_ws = [Buf("ws%d" % i) for i in range(NSLOT)]
    wso = sb("wso", [128, KCO * 128], BF16)
    B_wso = Buf("wso")
    xT = sb("xT", [128, KC, TP], BF16)
    B_xT = Buf("xT")
    yT = sb("yT", [128, KCO, TP], BF16)
    B_y = [Buf("y%d" % i) for i in range(KCO)]
    mT = sb("mT", [128, KC, TP], BF16)
    B_m = [Buf("m%d" % i) for i in range(KC)]
    ycall = sb("ycall", [128, 4, SEQ], BF16)
    B_yc = [[Buf("yc%d_%d" % (h, p)) for p in range(c.NQ)] for h in range(4)]
    ycs = sb("ycs", [128, 4, NS], BF16)
    B_ycs = Buf("ycs")
    ppt = sb("ppt", [128, L * c.NPP])
    B_pp = Buf("pp")
    cstt = sb("cstt", [128, 4 * 128])
    B_cst = Buf("cst")
    identb = sb("identb", [128, 128], BF16)
    onesb = sb("onesb", [128, 128], BF16)
    onesf = sb("onesf", [128, 128])
    ULb = sb("ULb", [128, 256], BF16)
    Mg2 = sb("Mg2", [128, 4, 512], BF16)
    B_const = Buf("const")
    rope = sb("rope", [128, 2, TP])
    B_rope = Buf("rope")
    ropest = sb("ropest", [128, 2, 1])
    NW = 10
    wk = [sb("wk%d" % i, [128, TP]) for i in range(NW)]
    B_wk = [Buf("wk%d" % i) for i in range(NW)]
    wkb = [sb("wkb%d" % i, [128, TP], BF16) for i in range(8)]
    B_wkb = [Buf("wkb%d" % i) for i in range(8)]
    cbuf = sb("cbuf", [128, a, TP])
    B_cb = [Buf("cb%d" % i) for i in range(a)]
    ext = sb("ext", [128, (30 + TP)])
    B_ext = Buf("ext")
    hista = sb("hista", [128, a, 2 * NS])
    histb = sb("histb", [128, a, 30 * NS])
    B_ha = [Buf("ha%d" % i) for i in range(a)]
    B_hb = [Buf("hb%d" % i) for i in range(a)]
    wsm = sb("wsm", [128, a, 128], BF16)
    bsm = sb("bsm", [128, a, 128])
    B_wsm = Buf("wsm")
    vnT = sb("vnT", [128, a, TP], BF16)
    B_vn = [Buf("vn%d" % i) for i in range(a)]
    stat = [sb("stat%d" % i, [128, TP]) for i in range(3)]
    B_stat = [Buf("stat%d" % i) for i in range(3)]
    qd = [sb("qd%d" % g, [128, SEQ], BF16) for g in range(3)]
    kd = [sb("kd%d" % g, [128, SEQ], BF16) for g in range(3)]
    vd = [sb("vd%d" % g, [128, SEQ // 128, 128], BF16) for g in range(3)]
    B_qd = [Buf("qd%d" % g) for g in range(3)]
    B_kd = [Buf("kd%d" % g) for g in range(3)]
    B_vd = [Buf("vd%d" % g) for g in range(3)]
    pT = [sb("pT%d" % i, [128, 512], BF16) for i in range(3)]
    B_pT = [Buf("pT%d" % i) for i in range(3)]
    vst = sb("vst", [128, 16, 128], BF16)
    B_vst = Buf("vst")
    rdt = sb("rdt", [128, 512])
    B_rdt = Buf("rdt")
    sm = sb("sm", [128, 1024])
    B_sm = Buf("sm")
    smb = sb("smb", [128, 1024], BF16)
    B_smb = Buf("smb")

    pz = [ps("pz%d" % i) for i in range(3)]
    B_pz = [Buf("pz%d" % i) for i in range(3)]
    pst = [ps("pst%d" % i) for i in range(2)]
    B_pst = [Buf("pst%d" % i) for i in range(2)]
    pm = [ps("pm%d" % i) for i in range(2)]
    B_pm = [Buf("pm%d" % i) for i in range(2)]
    pmb = ps("pmb", (128, 1024), BF16)
    B_pmb = Buf("pmb")
    rot = {"pz": 0, "pm": 0, "wk": 0, "wkb": 0, "ws": 0, "pT": 0}

    def nxt(kind, n):
        i = rot[kind] % n
        rot[kind] += 1
        return i

    def PP(l, name, j):
        o = l * c.NPP + c.pp[name] + j
        return ppt[:, o:o + 1]

    T.dma("pool", ppt[:], ppd, writes=[B_pp])
    T.dma("pool", cstt[:], cst, writes=[B_cst])
    T.op("dve", lambda h: h.tensor_copy(out=identb[:], in_=cstt[:, 0:128]), [B_cst], [B_const])
    T.op("dve", lambda h: h.memset(onesb[:], 1.0), [], [B_const])
    T.op("dve", lambda h: h.memset(onesf[:], 1.0), [], [B_const])
    T.op("dve", lambda h: h.tensor_copy(out=ULb[:], in_=cstt[:, 128:384]), [B_cst], [B_const])
    for mi in range(4):
        for r in range(16):
            T.op("dve", lambda h, mi=mi, r=r: h.tensor_copy(
                out=Mg2[:, mi, r * 32:(r + 1) * 32], in_=cstt[:, 128 + mi * 32:128 + mi * 32 + 32]),
                [B_cst], [B_const])
    identf = cstt[:, 0:128]
    swapf = cstt[:, 384:512]

    def cast_w(src, dst, key):
        b = Buf(str(key))
        B_wb[key] = b
        T.dma("pool", dst, src, writes=[b], max_dma_last_dim=4096)

    def cast_layer(l):
        for m in list(range(c.cQ, c.cQ + 36)) + [m for m in range(NCH) if not (c.cQ <= m < c.cQ + 36)]:
            cast_w(w_in[l, m], wb_in[l, m], ("in", l, m))
        for j in range(KC):
            cast_w(w_out[l, j], wb_out[l, j], ("out", l, j))
        for j in range(KC):
            cast_w(w_o[l, j], wb_o[l, j], ("o", l, j))

    for kc in range(KC):
        T.dma("pool", xb[0][kc], xpT[kc], writes=B_xb[0], max_dma_last_dim=4096)
    for kc in range(KC):
        T.dma("pool", xbs[0][kc], xsT[kc], writes=[B_xbs[0]], max_dma_last_dim=4096)
    cast_q = []

    def cast_w_lazy(src, dst, key):
        cast_q.append((src, dst, key))

    def pump(n=1, need=None):
        while cast_q and (n > 0 or (need is not None and need not in B_wb)):
            s_, d_, k_ = cast_q.pop(0)
            cast_w(s_, d_, k_)
            n -= 1

    _cw = cast_w
    for l in range(L):
        for m in list(range(c.cQ, c.cQ + 36)) + [m for m in range(NCH) if not (c.cQ <= m < c.cQ + 36)]:
            cast_w_lazy(w_in[l, m], wb_in[l, m], ("in", l, m))
        for j in range(KC):
            cast_w_lazy(w_out[l, j], wb_out[l, j], ("out", l, j))
        for j in range(KC):
            cast_w_lazy(w_o[l, j], wb_o[l, j], ("o", l, j))

    def lin(key, wsrc, kcn, rhs_fn, rhs_bufs, ncol, consume):
        pump(1, need=key)
        si = nxt("ws", NSLOT)
        T.dma("sp", wslot[si][:, 0:kcn * 128], wsrc, reads=[B_wb[key]], writes=[B_ws[si]])
        pi = nxt("pz", 3)
        fns = []
        for kc in range(kcn):
            fns.append(lambda h, kc=kc, si=si, pi=pi: h.matmul(
                pz[pi][:, 0:ncol], lhsT=wslot[si][:, kc * 128:(kc + 1) * 128], rhs=rhs_fn(kc),
                start=(kc == 0), stop=(kc == kcn - 1)))
        T.group("pe", fns, reads=[B_ws[si]] + list(rhs_bufs), writes=[B_pz[pi]])
        consume(pz[pi][:, 0:ncol], B_pz[pi])

    def lin_in(l, m, ncol, consume):
        lin(("in", l, m), wb_in[l, m], KC, lambda kc: xT[:, kc, 0:ncol], [B_xT], ncol, consume)

    def evac(psrc, bsrc, func=AF.Copy, dt=F32, scale=None, bias=None):
        ncol = psrc.shape[-1]
        if dt == F32:
            i = nxt("wk", NW)
            dst, b = wk[i][:, 0:ncol], B_wk[i]
        else:
            i = nxt("wkb", 8)
            dst, b = wkb[i][:, 0:ncol], B_wkb[i]
        kw = {}
        if scale is not None:
            kw["scale"] = scale
        if bias is not None:
            kw["bias"] = bias
        rd = [bsrc, B_pp] if (scale is not None or bias is not None) else [bsrc]
        T.op("act", lambda h: h.activation(out=dst, in_=psrc, func=func, **kw), rd, [b])
        return dst, b

    def newwk(ncol):
        i = nxt("wk", NW)
        return wk[i][:, 0:ncol], B_wk[i]

    def ln_stats(tiles, ncol, nch_total):
        n = len(tiles)
        fns0, fns1 = [], []
        sq = []
        for i, (ap, b) in enumerate(tiles):
            s_ap, s_b = newwk(ncol)
            T.op("act", lambda h, ap=ap, s_ap=s_ap: h.activation(out=s_ap, in_=ap, func=AF.Square), [b], [s_b])
            sq.append((s_ap, s_b))
            T.op("pe", lambda h, ap=ap, i=i: h.matmul(pst[0][:, 0:ncol], lhsT=onesf[:], rhs=ap,
                                                      start=(i == 0), stop=(i == n - 1)),
                 [b, B_const], [B_pst[0]])
            T.op("pe", lambda h, s_ap=s_ap, i=i: h.matmul(pst[1][:, 0:ncol], lhsT=onesf[:], rhs=s_ap,
                                                          start=(i == 0), stop=(i == n - 1)),
                 [s_b, B_const], [B_pst[1]])
        mean, rstd, tmp = stat[0][:, 0:ncol], stat[1][:, 0:ncol], stat[2][:, 0:ncol]
        inv = 1.0 / nch_total
        T.op("act", lambda h: h.activation(out=mean, in_=pst[0][:, 0:ncol], func=AF.Copy, scale=inv),
             [B_pst[0]], [B_stat[0]])
        T.op("dve", lambda h: h.tensor_tensor(out=tmp, in0=mean, in1=mean, op=ALU.mult), [B_stat[0]], [B_stat[2]])
        T.op("dve", lambda h: h.scalar_tensor_tensor(out=tmp, in0=pst[1][:, 0:ncol], scalar=inv, in1=tmp,
                                                     op0=ALU.mult, op1=ALU.subtract),
             [B_pst[1], B_stat[2]], [B_stat[2]])
        T.op("dve", lambda h: h.tensor_scalar(out=tmp, in0=tmp, scalar1=LN_EPS, scalar2=None, op0=ALU.add),
             [B_stat[2]], [B_stat[2]])
        T.op("act", lambda h: h.activation(out=tmp, in_=tmp, func=AF.Sqrt), [B_stat[2]], [B_stat[2]])
        T.op("dve", lambda h: h.reciprocal(out=rstd, in_=tmp), [B_stat[2]], [B_stat[1]])
        return mean, rstd

    def normalize(ap, b, mean, rstd, out_ap, out_b):
        T.op("dve", lambda h: h.tensor_tensor(out=out_ap, in0=ap, in1=mean, op=ALU.subtract),
             [b, B_stat[0]], [out_b])
        T.op("dve", lambda h: h.tensor_tensor(out=out_ap, in0=out_ap, in1=rstd, op=ALU.mult),
             [out_b, B_stat[1]], [out_b])

    def load_xT(src_t, bsrc, col0, ncol):
        step = max(1, KC // 4)
        for k0 in range(0, KC, step):
            T.dma("pool", xT[:, k0:k0 + step, 0:ncol],
                  src_t[k0:k0 + step, :, col0:col0 + ncol].rearrange("k p t -> p k t"), reads=[bsrc], writes=[B_xT])

    def rope_apply(xap, xb, ncol, cos, sin):
        pi = nxt("pm", 2)
        T.op("pe", lambda h: h.matmul(pm[pi][:, 0:ncol], lhsT=swapf, rhs=xap, start=True, stop=True),
             [xb, B_cst], [B_pm[pi]])
        t1, b1 = newwk(ncol)
        T.op("dve", lambda h: h.tensor_tensor(out=t1, in0=xap, in1=cos, op=ALU.mult), [xb, B_rope], [b1])
        t2, b2 = newwk(ncol)
        T.op("dve", lambda h: h.tensor_tensor(out=t2, in0=pm[pi][:, 0:ncol], in1=sin, op=ALU.mult),
             [B_pm[pi], B_rope], [b2])
        T.op("dve", lambda h: h.tensor_tensor(out=t1, in0=t1, in1=t2, op=ALU.add), [b1, b2], [b1])
        return t1, b1

    def phase1_prompt(l, p, src_t, bsrc):
        t0 = p * TP
        load_xT(src_t, bsrc, t0, TP)
        T.dma("pool", rope[:], ropep[:, :, t0:t0 + TP], writes=[B_rope])
        cos, sin = rope[:, 0, :], rope[:, 1, :]
        for which, cbase, scr, Bs in (("q", c.cQ, qT_s, B_q), ("k", c.cK, kT_s, B_k)):
            for hh in range(NH):
                g, hs = hh // 4, hh % 4
                dil = DILS[g]
                res = {}
                lin_in(l, cbase + hh, TP, lambda pa, pb: res.update(x=evac(pa, pb)))
                xap, xb = res["x"]
                rt, rb = rope_apply(xap, xb, TP, cos, sin)
                if which == "k":
                    keep = c.keep[g]
                    lo = max(t0, SEQ - keep)
                    if lo < t0 + TP:
                        T.dma("pool", o_kv_p[g][l, 0, hs, :, lo - (SEQ - keep):t0 + TP - (SEQ - keep)],
                              rt[:, lo - t0:TP], reads=[rb], writes=[OB["kv_p"]])
                i = nxt("wkb", 8)
                dst, db = wkb[i], B_wkb[i]
                T.op("act", lambda h, dst=dst, rt=rt: h.activation(out=dst[:, 0:TP], in_=rt, func=AF.Copy), [rb], [db])
                T.dma("pool", scr[g][:, hs, t0:t0 + TP], dst[:, 0:TP], reads=[db], writes=[Bs[g][hs]])
        for hh in range(NH):
            g, hs = hh // 4, hh % 4
            dil = DILS[g]
            res = {}
            lin_in(l, c.cV + hh, TP, lambda pa, pb: res.update(x=evac(pa, pb)))
            xap, xb = res["x"]
            keep = c.keep[g]
            lo = max(t0, SEQ - keep)
            if lo < t0 + TP:
                T.dma("pool", o_kv_p[g][l, 1, hs, :, lo - (SEQ - keep):t0 + TP - (SEQ - keep)],
                      xap[:, lo - t0:TP], reads=[xb], writes=[OB["kv_p"]])
            i = nxt("wkb", 8)
            vb16, vbb = wkb[i], B_wkb[i]
            T.op("act", lambda h, vb16=vb16, xap=xap: h.activation(out=vb16[:], in_=xap, func=AF.Copy), [xb], [vbb])
            nm = TP // dil
            if dil == 1:
                blocks = [(vb16[:, j * 128:(j + 1) * 128], 128, j) for j in range(TP // 128)]
            else:
                v3 = vb16[:].rearrange("p (m r) -> p r m", r=dil)
                blocks = [(v3[:, r, :], nm, r) for r in range(dil)]
            for s0 in range(0, len(blocks), 8):
                sub = blocks[s0:s0 + 8]
                fns = [(lambda h, src=src, n=n, jj=jj: h.transpose(pmb[0:n, jj * 128:(jj + 1) * 128], src, identb[:]))
                       for jj, (src, n, j) in enumerate(sub)]
                T.group("pe", fns, [vbb, B_const], [B_pmb])
                n = sub[0][1]
                nb = len(sub)
                T.op("act", lambda h, n=n, nb=nb, s0=s0: h.activation(
                    out=vst[0:n, s0:s0 + nb, :], in_=pmb[0:n, 0:nb * 128].rearrange("p (b d) -> p b d", d=128),
                    func=AF.Copy), [B_pmb], [B_vst])
            M = SEQ // dil
            m0 = t0 // dil
            if dil == 1:
                T.dma("pool", v_s[g][hs, t0:t0 + TP, :].rearrange("(j p) d -> p j d", p=128), vst[:, 0:TP // 128, :],
                      reads=[B_vst], writes=[B_v[g][hs]])
            else:
                T.dma("pool", v_s[g][hs].rearrange("(r m) d -> m r d", r=dil)[m0:m0 + nm, :, :], vst[0:nm, 0:dil, :],
                      reads=[B_vst], writes=[B_v[g][hs]])

    SCALE = 1.0 / np.sqrt(HD)

    def phase2(l):
        for hs in range(4):
            for g in range(3):
                T.dma("pool", qd[g][:], qT_s[g][:, hs, :], reads=[B_q[g][hs]], writes=[B_qd[g]])
                T.dma("pool", kd[g][:], kT_s[g][:, hs, :], reads=[B_k[g][hs]], writes=[B_kd[g]])
                T.dma("pool", vd[g][:], v_s[g][hs].rearrange("(j p) d -> p j d", p=128), reads=[B_v[g][hs]],
                      writes=[B_vd[g]])
            for Qb in range(c.NQ):
                jobs = []
                t0 = Qb * 512
                for kb in range(4 * Qb - 1, 4 * Qb + 4):
                    if kb < 0:
                        continue
                    q_lo = max(kb, 4 * Qb)
                    q_hi = min(kb + 1, 4 * Qb + 3)
                    nq = (q_hi - q_lo + 1) * 128
                    moff = 0 if q_lo == kb else 128
                    jobs.append((0, kd[0][:, kb * 128:(kb + 1) * 128], 128, qd[0][:, q_lo * 128:q_lo * 128 + nq], nq,
                                 ULb[:, moff:moff + nq], vd[0][:, kb, :],
                                 lambda P, q_lo=q_lo, nq=nq, t0=t0: P[:, q_lo * 128 - t0:q_lo * 128 - t0 + nq]))
                M1 = SEQ // 4
                k1v = kd[1][:].rearrange("p (m r) -> p r m", r=4)
                q1v = qd[1][:].rearrange("p (m r) -> p r m", r=4)
                for r in range(4):
                    for kbm in (Qb - 1, Qb):
                        if kbm < 0:
                            continue
                        moff = 0 if kbm == Qb else 128
                        base = r * M1
                        jobs.append((1, k1v[:, r, kbm * 128:(kbm + 1) * 128], 128,
                                     q1v[:, r, Qb * 128:(Qb + 1) * 128], 128,
                                     ULb[:, moff:moff + 128], vd[1][:, (base + kbm * 128) // 128, :],
                                     lambda P, r=r: P.rearrange("p (m r) -> p r m", r=4)[:, r, :]))
                M2 = SEQ // 16
                m0 = 32 * Qb
                nk = m0 + 32
                g2jobs = []
                k2v = kd[2][:].rearrange("p (m r) -> p r m", r=16)
                q2v = qd[2][:].rearrange("p (m r) -> p r m", r=16)
                for r in range(16):
                    base = r * M2
                    g2jobs.append((k2v[:, r, 0:nk], q2v[:, r, m0:m0 + 32],
                                   vd[2][0:nk, base // 128, :] if M2 == 128 else None, r, base))
                plist = []
                for (g, lk, nkk, rq, nq, mk, vv, oc) in jobs:
                    pi = nxt("pz", 3)
                    T.op("pe", lambda h, lk=lk, rq=rq, nkk=nkk, nq=nq, pi=pi: h.matmul(
                        pz[pi][0:nkk, 0:nq], lhsT=lk, rhs=rq, start=True, stop=True),
                        [B_kd[g], B_qd[g]], [B_pz[pi]])
                    ti = nxt("pT", 3)
                    T.op("act", lambda h, pi=pi, ti=ti, nkk=nkk, nq=nq: h.activation(
                        out=pT[ti][0:nkk, 0:nq], in_=pz[pi][0:nkk, 0:nq], func=AF.Exp, scale=float(SCALE)),
                        [B_pz[pi]], [B_pT[ti]])
                    T.op("dve", lambda h, ti=ti, nkk=nkk, nq=nq, mk=mk: h.tensor_tensor(
                        out=pT[ti][0:nkk, 0:nq], in0=pT[ti][0:nkk, 0:nq], in1=mk, op=ALU.mult),
                        [B_pT[ti], B_const], [B_pT[ti]])
                    plist.append((g, ti, nkk, nq, vv, oc))
                    _pv(plist, Qb, hs, last=False)
                    plist = []
                pi = nxt("pz", 3)
                fns = []
                for (lk, rq, vv, r, base) in g2jobs:
                    fns.append(lambda h, lk=lk, rq=rq, r=r, pi=pi, nk=nk: h.matmul(
                        pz[pi][0:nk, r * 32:(r + 1) * 32], lhsT=lk, rhs=rq, start=True, stop=True))
                T.group("pe", fns, [B_kd[2], B_qd[2]], [B_pz[pi]])
                ti = nxt("pT", 3)
                T.op("act", lambda h, pi=pi, ti=ti, nk=nk: h.activation(out=pT[ti][0:nk, :], in_=pz[pi][0:nk, :],
                                                                 func=AF.Exp, scale=float(SCALE)),
                     [B_pz[pi]], [B_pT[ti]])
                T.op("dve", lambda h, ti=ti, Qb=Qb, nk=nk: h.tensor_tensor(out=pT[ti][0:nk, :], in0=pT[ti][0:nk, :],
                                                                    in1=Mg2[0:nk, Qb % 4, :], op=ALU.mult),
                     [B_pT[ti], B_const], [B_pT[ti]])
                pl = []
                for (lk, rq, vv, r, base) in g2jobs:
                    vblk = vd[2][0:nk, (base * 1) // 128, :] if False else None
                    pl.append((2, ti, nk, 32, ("g2", r, base), (lambda P, r=r: P.rearrange("p (m r) -> p r m", r=16)[:, r, :]), r))
                _pv_g2(pl, ti, nk, Qb, hs)
                rd_ap, rd_b = rdt[:], B_rdt
                T.op("dve", lambda h, rd_ap=rd_ap: h.reciprocal(out=rd_ap, in_=pst[1][:]), [B_pst[1]], [rd_b])
                T.op("dve", lambda h, rd_ap=rd_ap, hs=hs, t0=t0: h.tensor_tensor(
                    out=ycall[:, hs, t0:t0 + 512], in0=pst[0][:], in1=rd_ap, op=ALU.mult),
                    [B_pst[0], rd_b], [B_yc[hs][Qb]])
                acc_state["first"] = True

    acc_state = {"first": True}

    def _acc_mm(lhsT_v, nkk, rhs, ocf, reads):
        first = acc_state["first"]
        acc_state["first"] = False
        T.op("pe", lambda h: h.matmul(ocf(pst[0][:]), lhsT=lhsT_v, rhs=rhs, start=first, stop=False,
                                      skip_group_check=True), reads, [B_pst[0]])
        T.op("pe", lambda h: h.matmul(ocf(pst[1][:]), lhsT=onesb[0:nkk, :], rhs=rhs, start=first, stop=False,
                                      skip_group_check=True), reads + [B_const], [B_pst[1]])

    def _pv(plist, Qb, hs, last):
        for (g, ti, nkk, nq, vv, oc) in plist:
            _acc_mm(vv, nkk, pT[ti][0:nkk, 0:nq], oc, [B_pT[ti], B_vd[g]])

    def _pv_g2(pl, ti, nk, Qb, hs):
        M2 = SEQ // 16
        for (g, ti_, nkk, nq, tag, oc, r) in pl:
            base = r * M2
            j0, p0 = divmod(base, 128)
            if p0 + nk <= 128:
                vv = vd[2][p0:p0 + nk, j0, :]
                _acc_mm_g2(vv, p0, nk, pT[ti][0:nk, r * 32:(r + 1) * 32], oc, ti)
            else:
                raise NotImplementedError

    def _acc_mm_g2(vv, p0, nk, rhs, ocf, ti):
        first = acc_state["first"]
        acc_state["first"] = False
        T.op("pe", lambda h: h.matmul(ocf(pst[0][:]), lhsT=vv, rhs=rhs, start=first, stop=False,
                                      skip_group_check=True), [B_pT[ti], B_vd[2]], [B_pst[0]])
        T.op("pe", lambda h: h.matmul(ocf(pst[1][:]), lhsT=onesb[0:nk, :], rhs=rhs, start=first, stop=False,
                                      skip_group_check=True), [B_pT[ti], B_const], [B_pst[1]])

    def evac_to(psrc, bsrc, dst, dstb, func=AF.Copy, scale=None, bias=None):
        kw = {}
        if scale is not None:
            kw["scale"] = scale
        if bias is not None:
            kw["bias"] = bias
        rd = [bsrc, B_pp] if kw else [bsrc]
        T.op("act", lambda h: h.activation(out=dst, in_=psrc, func=func, **kw), rd, [dstb])

    def setup_layer(l):
        for j in range(a):
            t_ap, t_b = newwk(128)
            T.dma("pool", t_ap, wsT[l, j], writes=[t_b])
            T.op("dve", lambda h, t_ap=t_ap, j=j: h.tensor_tensor(out=wsm[:, j, :], in0=t_ap, in1=cstt[:, 128:256],
                                                                  op=ALU.mult), [t_b, B_cst], [B_wsm])
            T.dma("pool", bsm[:, j, :], bsr[l, j], writes=[B_wsm])

    def phase3(l, ncol, ns, src_ap_fn, bsrc, dst_ap_fn, dstb, t0, sample, last_pass):
        if sample:
            load_xT(xbs[l], B_xbs[l], 0, ncol)
        else:
            load_xT(xb[l], B_xb[l][t0 // TP], t0, ncol)
        res = {}

        def grab(pa, pb):
            res["p"] = (pa, pb)

        for j in range(a):
            h2 = 2 * ns
            lin_in(l, c.cA[1] + j, ncol, lambda pa, pb: res.update(c=evac(pa, pb)))
            tcap, tcb = res["c"]
            lin_in(l, c.cA[2] + j, ncol, grab)
            pa, pb = res["p"]
            T.op("dve", lambda h, j=j, h2=h2: h.tensor_copy(out=ext[:, 0:h2], in_=hista[:, j, 0:h2]), [B_ha[j]], [B_ext])
            T.op("dve", lambda h, pa=pa, tcap=tcap, h2=h2: h.tensor_tensor(out=ext[:, h2:h2 + ncol], in0=tcap, in1=pa,
                                                                          op=ALU.mult), [pb, tcb], [B_ext])
            T.op("dve", lambda h, j=j, h2=h2: h.tensor_copy(out=hista[:, j, 0:h2], in_=ext[:, ncol:ncol + h2]),
                 [B_ext], [B_ha[j]])
            acc, accb = newwk(ncol)
            T.op("dve", lambda h, acc=acc, j=j: h.tensor_scalar(out=acc, in0=ext[:, 0:ncol], scalar1=PP(l, "caw", j * 3),
                                                                scalar2=None, op0=ALU.mult), [B_ext, B_pp], [accb])
            for k in (1, 2):
                T.op("dve", lambda h, acc=acc, j=j, k=k: h.scalar_tensor_tensor(
                    out=acc, in0=ext[:, k * ns:k * ns + ncol], scalar=PP(l, "caw", j * 3 + k), in1=acc,
                    op0=ALU.mult, op1=ALU.add), [B_ext, B_pp, accb], [accb])
            lin_in(l, c.cA[3] + j, ncol, lambda pa, pb: res.update(z=evac(pa, pb, func=AF.Silu)))
            szap, szb = res["z"]
            lin_in(l, c.cA[0] + j, ncol, grab)
            pa, pb = res["p"]
            T.op("dve", lambda h, acc=acc, pa=pa: h.tensor_tensor(out=acc, in0=acc, in1=pa, op=ALU.mult), [accb, pb], [accb])
            T.op("dve", lambda h, acc=acc, szap=szap, j=j: h.tensor_tensor(out=yT[:, j, 0:ncol], in0=acc, in1=szap,
                                                                          op=ALU.mult), [accb, szb], [B_y[j]])
            if sample:
                T.dma("pool", o_ca_s[:, l, j, :, :], hista[:, j, 0:h2].rearrange("p (k s) -> p k s", s=ns),
                      reads=[B_ha[j]], writes=[OB["ca_s"]])
            elif last_pass:
                T.dma("pool", o_ca_p[:, l, j, :], hista[:, j, 0:2], reads=[B_ha[j]], writes=[OB["ca_p"]])
        for j in range(a):
            h30 = 30 * ns
            lin_in(l, c.cB[1] + j, ncol, lambda pa, pb: res.update(g=evac(pa, pb, func=AF.Sigmoid)))
            tg, tgb = res["g"]
            lin_in(l, c.cB[0] + j, ncol, grab)
            pa, pb = res["p"]
            T.op("dve", lambda h, j=j, h30=h30: h.tensor_copy(out=ext[:, 0:h30], in_=histb[:, j, 0:h30]), [B_hb[j]], [B_ext])
            T.op("dve", lambda h, pa=pa, tg=tg, h30=h30: h.tensor_tensor(out=ext[:, h30:h30 + ncol], in0=tg, in1=pa,
                                                                        op=ALU.mult), [pb, tgb], [B_ext])
            T.op("dve", lambda h, j=j, h30=h30: h.tensor_copy(out=histb[:, j, 0:h30], in_=ext[:, ncol:ncol + h30]),
                 [B_ext], [B_hb[j]])
            cb = cbuf[:, j, 0:ncol]
            T.op("dve", lambda h, cb=cb, j=j: h.tensor_scalar(out=cb, in0=ext[:, 0:ncol], scalar1=PP(l, "cbw", j * 31),
                                                              scalar2=PP(l, "cbb", j), op0=ALU.mult, op1=ALU.add),
                 [B_ext, B_pp], [B_cb[j]])
            for k in range(1, 31):
                T.op("dve", lambda h, cb=cb, j=j, k=k: h.scalar_tensor_tensor(
                    out=cb, in0=ext[:, k * ns:k * ns + ncol], scalar=PP(l, "cbw", j * 31 + k), in1=cb,
                    op0=ALU.mult, op1=ALU.add), [B_ext, B_pp, B_cb[j]], [B_cb[j]])
            lin_in(l, c.cB[2] + j, ncol, lambda pa, pb, j=j: evac_to(pa, pb, yT[:, a + j, 0:ncol], B_y[a + j], func=AF.Silu))
            if sample:
                T.dma("pool", o_cb_s[:, l, j, :, :], histb[:, j, 0:h30].rearrange("p (k s) -> p k s", s=ns),
                      reads=[B_hb[j]], writes=[OB["cb_s"]])
            elif last_pass:
                T.dma("pool", o_cb_p[:, l, j, :], histb[:, j, 0:30], reads=[B_hb[j]], writes=[OB["cb_p"]])
        mean, rstd = ln_stats([(cbuf[:, j, 0:ncol], B_cb[j]) for j in range(a)], ncol, c.WA)
        for j in range(a):
            n_ap, n_b = newwk(ncol)
            normalize(cbuf[:, j, 0:ncol], B_cb[j], mean, rstd, n_ap, n_b)
            T.op("act", lambda h, n_ap=n_ap, j=j: h.activation(out=n_ap, in_=n_ap, func=AF.Silu,
                                                               scale=PP(l, "lbg", j), bias=PP(l, "lbb", j)),
                 [n_b, B_pp], [n_b])
            T.op("dve", lambda h, n_ap=n_ap, j=j: h.tensor_tensor(out=yT[:, a + j, 0:ncol], in0=n_ap,
                                                                  in1=yT[:, a + j, 0:ncol], op=ALU.mult),
                 [n_b, B_y[a + j]], [B_y[a + j]])
        for hs in range(4):
            lin_in(l, c.cZ + hs, ncol, lambda pa, pb: res.update(z=evac(pa, pb, func=AF.Silu)))
            szap, szb = res["z"]
            if sample:
                T.op("dve", lambda h, szap=szap, hs=hs: h.tensor_tensor(out=yT[:, 2 * a + hs, 0:ncol], in0=ycs[:, hs, :],
                                                                        in1=szap, op=ALU.mult),
                     [szb, B_ycs], [B_y[2 * a + hs]])
            else:
                p = t0 // 512
                T.op("dve", lambda h, szap=szap, hs=hs: h.tensor_tensor(out=yT[:, 2 * a + hs, 0:ncol],
                                                                        in0=ycall[:, hs, t0:t0 + ncol], in1=szap,
                                                                        op=ALU.mult),
                     [szb, B_yc[hs][p]], [B_y[2 * a + hs]])
        for j in range(a):
            lin_in(l, c.cD[1] + j, ncol, lambda pa, pb, j=j: evac_to(pa, pb, cbuf[:, j, 0:ncol], B_cb[j]))
        mean, rstd = ln_stats([(cbuf[:, j, 0:ncol], B_cb[j]) for j in range(a)], ncol, c.WA)
        for j in range(a):
            yj = 2 * a + 4 + j
            n_ap, n_b = newwk(ncol)
            normalize(cbuf[:, j, 0:ncol], B_cb[j], mean, rstd, n_ap, n_b)
            mix, mixb = newwk(ncol)
            if sample:
                T.op("act", lambda h, n_ap=n_ap, j=j: h.activation(out=n_ap, in_=n_ap, func=AF.Identity,
                                                                   scale=PP(l, "ldg", j), bias=PP(l, "ldb", j)),
                     [n_b, B_pp], [n_b])
                T.dma("pool", o_gv_s[:, l, j, :], n_ap, reads=[n_b], writes=[OB["gv"]])
                T.op("dve", lambda h, n_ap=n_ap, mix=mix, j=j: h.tensor_scalar(
                    out=mix, in0=n_ap, scalar1=PP(l, "ws00", j), scalar2=PP(l, "bs0", j), op0=ALU.mult, op1=ALU.add),
                    [n_b, B_pp], [mixb])
            else:
                i = nxt("wkb", 8)
                vb, vbb = wkb[i], B_wkb[i]
                T.op("act", lambda h, n_ap=n_ap, vb=vb, j=j: h.activation(out=vb[:, 0:ncol], in_=n_ap, func=AF.Identity,
                                                                         scale=PP(l, "ldg", j), bias=PP(l, "ldb", j)),
                     [n_b, B_pp], [vbb])
                nq = ncol // 128
                fns = [(lambda h, q=q, vb=vb: h.transpose(pmb[:, q * 128:(q + 1) * 128], vb[:, q * 128:(q + 1) * 128],
                                                          identb[:])) for q in range(nq)]
                T.group("pe", fns, [vbb, B_const], [B_pmb])
                T.op("act", lambda h, j=j: h.activation(out=vnT[:, j, 0:ncol], in_=pmb[:, 0:ncol], func=AF.Copy),
                     [B_pmb], [B_vn[j]])
                pi = nxt("pm", 2)
                fns = [(lambda h, q=q, j=j, pi=pi: h.matmul(pm[pi][:, q * 128:(q + 1) * 128],
                                                            lhsT=vnT[:, j, q * 128:(q + 1) * 128], rhs=wsm[:, j, :],
                                                            start=True, stop=True)) for q in range(nq)]
                T.group("pe", fns, [B_vn[j], B_wsm], [B_pm[pi]])
                for q in range(nq):
                    T.op("dve", lambda h, q=q, j=j, pi=pi, mix=mix: h.tensor_tensor(
                        out=mix[:, q * 128:(q + 1) * 128], in0=pm[pi][:, q * 128:(q + 1) * 128], in1=bsm[:, j, :],
                        op=ALU.add), [B_pm[pi], B_wsm], [mixb])
            lin_in(l, c.cD[0] + j, ncol, grab)
            pa, pb = res["p"]
            T.op("dve", lambda h, mix=mix, pa=pa: h.tensor_tensor(out=mix, in0=mix, in1=pa, op=ALU.mult), [mixb, pb], [mixb])
            lin_in(l, c.cD[2] + j, ncol, lambda pa, pb: res.update(z=evac(pa, pb, func=AF.Silu)))
            szap, szb = res["z"]
            T.op("dve", lambda h, mix=mix, szap=szap, yj=yj: h.tensor_tensor(out=yT[:, yj, 0:ncol], in0=mix, in1=szap,
                                                                            op=ALU.mult), [mixb, szb], [B_y[yj]])
        kr = ((0, a), (a, 2 * a), (2 * a, 2 * a + 4), (2 * a + 4, 3 * a + 4))
        for jo in range(KC):
            pump(1, need=("out", l, jo))
            T.dma("pool", wso[:], wb_out[l, jo], reads=[B_wb[("out", l, jo)]], writes=[B_wso])
            macc, maccb = newwk(ncol)
            for bi in range(4):
                lin_in(l, c.cG + bi * KC + jo, ncol,
                       lambda pa, pb, bi=bi: res.update(g=evac(pa, pb, func=AF.Sigmoid, bias=PP(l, "bg", bi * KC + jo))))
                sg, sgb = res["g"]
                pi = nxt("pm", 2)
                k0, k1 = kr[bi]
                fns = [(lambda h, kc=kc, pi=pi, k0=k0, k1=k1: h.matmul(
                    pm[pi][:, 0:ncol], lhsT=wso[:, kc * 128:(kc + 1) * 128], rhs=yT[:, kc, 0:ncol],
                    start=(kc == k0), stop=(kc == k1 - 1))) for kc in range(k0, k1)]
                T.group("pe", fns, [B_wso] + [B_y[kc] for kc in range(k0, k1)], [B_pm[pi]])
                if bi == 0:
                    T.op("dve", lambda h, sg=sg, pi=pi, macc=macc: h.tensor_tensor(out=macc, in0=sg, in1=pm[pi][:, 0:ncol],
                                                                                  op=ALU.mult), [sgb, B_pm[pi]], [maccb])
                else:
                    T.op("dve", lambda h, sg=sg, pi=pi: h.tensor_tensor(out=sg, in0=sg, in1=pm[pi][:, 0:ncol], op=ALU.mult),
                         [sgb, B_pm[pi]], [sgb])
                    if bi < 3:
                        T.op("dve", lambda h, sg=sg, macc=macc: h.tensor_tensor(out=macc, in0=macc, in1=sg, op=ALU.add),
                             [sgb, maccb], [maccb])
                    else:
                        T.op("dve", lambda h, sg=sg, macc=macc, jo=jo: h.tensor_tensor(out=mT[:, jo, 0:ncol], in0=macc,
                                                                                      in1=sg, op=ALU.add),
                             [sgb, maccb], [B_m[jo]])
        pend = [None]
        for jo in range(KC):
            xr, xrb = newwk(ncol)
            T.dma("pool", xr, src_ap_fn(jo), reads=[bsrc], writes=[xrb])
            lin(("o", l, jo), wb_o[l, jo], KC, lambda kc: mT[:, kc, 0:ncol], B_m, ncol, grab)
            if pend[0] is not None:
                pend[0]()
                pend[0] = None
            pa, pb = res["p"]
            T.op("dve", lambda h, xr=xr, pa=pa: h.scalar_tensor_tensor(out=xr, in0=xr, scalar=float(c.ALPHA), in1=pa,
                                                                       op0=ALU.mult, op1=ALU.add), [xrb, pb], [xrb])
            sq, sqb = newwk(ncol)
            T.op("act", lambda h, xr=xr, sq=sq: h.activation(out=sq, in_=xr, func=AF.Square), [xrb], [sqb])
            def _stats(xr=xr, xrb=xrb, sq=sq, sqb=sqb, jo=jo):
                T.op("pe", lambda h: h.matmul(pst[0][:, 0:ncol], lhsT=onesf[:], rhs=xr, start=(jo == 0),
                                              stop=(jo == KC - 1)), [xrb, B_const], [B_pst[0]])
                T.op("pe", lambda h: h.matmul(pst[1][:, 0:ncol], lhsT=onesf[:], rhs=sq, start=(jo == 0),
                                              stop=(jo == KC - 1)), [sqb, B_const], [B_pst[1]])
            pend[0] = _stats
            T.dma("pool", rT[jo, :, 0:ncol], xr, reads=[xrb], writes=[B_rT[jo]])
        if pend[0] is not None:
            pend[0]()
        mean, rstd = ln_finish(ncol, D)
        for jo in range(KC):
            xr, xrb = newwk(ncol)
            T.dma("pool", xr, rT[jo, :, 0:ncol], reads=[B_rT[jo]], writes=[xrb])
            normalize(xr, xrb, mean, rstd, xr, xrb)
            T.op("act", lambda h, xr=xr, jo=jo: h.activation(out=xr, in_=xr, func=AF.Identity, scale=PP(l, "lg", jo),
                                                             bias=PP(l, "lb", jo)), [xrb, B_pp], [xrb])
            T.dma("pool", dst_ap_fn(jo), xr, reads=[xrb], writes=[dstb])
            if l + 1 < L:
                xbt, xbtb = newwkb()
                T.op("act", lambda h, xbt=xbt, xr=xr: h.activation(out=xbt[:, 0:ncol], in_=xr, func=AF.Copy), [xrb], [xbtb])
                if sample:
                    T.dma("pool", xbs[l + 1][jo, :, :], xbt[:, 0:ncol], reads=[xbtb], writes=[B_xbs[l + 1]])
                else:
                    T.dma("pool", xb[l + 1][jo, :, t0:t0 + ncol], xbt[:, 0:ncol], reads=[xbtb], writes=[B_xb[l + 1][t0 // TP]])

    def ln_finish(ncol, nch_total):
        mean, rstd, tmp = stat[0][:, 0:ncol], stat[1][:, 0:ncol], stat[2][:, 0:ncol]
        inv = 1.0 / nch_total
        T.op("act", lambda h: h.activation(out=mean, in_=pst[0][:, 0:ncol], func=AF.Copy, scale=inv),
             [B_pst[0]], [B_stat[0]])
        T.op("dve", lambda h: h.tensor_tensor(out=tmp, in0=mean, in1=mean, op=ALU.mult), [B_stat[0]], [B_stat[2]])
        T.op("dve", lambda h: h.scalar_tensor_tensor(out=tmp, in0=pst[1][:, 0:ncol], scalar=inv, in1=tmp,
                                                     op0=ALU.mult, op1=ALU.subtract),
             [B_pst[1], B_stat[2]], [B_stat[2]])
        T.op("dve", lambda h: h.tensor_scalar(out=tmp, in0=tmp, scalar1=LN_EPS, scalar2=None, op0=ALU.add),
             [B_stat[2]], [B_stat[2]])
        T.op("act", lambda h: h.activation(out=tmp, in_=tmp, func=AF.Sqrt), [B_stat[2]], [B_stat[2]])
        T.op("dve", lambda h: h.reciprocal(out=rstd, in_=tmp), [B_stat[2]], [B_stat[1]])
        return mean, rstd

    def newwkb():
        i = nxt("wkb", 8)
        return wkb[i], B_wkb[i]

    def sample_attn(l, src_fn, srcb):
        load_xT(xbs[l], B_xbs[l], 0, NS)
        T.dma("pool", ropest[:], ropes, writes=[B_rope])
        res = {}
        for which, cbase, off in (("q", c.cQ, 0), ("k", c.cK, 12 * NS), ("v", c.cV, 24 * NS)):
            for hh in range(NH):
                g, hs = hh // 4, hh % 4
                lin_in(l, cbase + hh, NS, lambda pa, pb: res.update(x=evac(pa, pb)))
                xap, xb = res["x"]
                dst = sm[:, off + hh * NS:off + (hh + 1) * NS]
                if which == "v":
                    T.op("dve", lambda h, dst=dst, xap=xap: h.tensor_copy(out=dst, in_=xap), [xb], [B_sm])
                else:
                    pi = nxt("pm", 2)
                    T.op("pe", lambda h, pi=pi, xap=xap: h.matmul(pm[pi][:, 0:NS], lhsT=swapf, rhs=xap, start=True, stop=True),
                         [xb, B_cst], [B_pm[pi]])
                    t2, b2 = newwk(NS)
                    T.op("dve", lambda h, t2=t2, pi=pi: h.tensor_scalar(out=t2, in0=pm[pi][:, 0:NS], scalar1=ropest[:, 1, 0:1],
                                                                        scalar2=None, op0=ALU.mult), [B_pm[pi], B_rope], [b2])
                    T.op("dve", lambda h, dst=dst, xap=xap, t2=t2: h.scalar_tensor_tensor(
                        out=dst, in0=xap, scalar=ropest[:, 0, 0:1], in1=t2, op0=ALU.mult, op1=ALU.add),
                        [xb, b2, B_rope], [B_sm])
                if which != "q":
                    T.dma("pool", o_kv_s[g][l, 0 if which == "k" else 1, hs, :, :], dst, reads=[B_sm], writes=[OB["kv_s"]])
        T.op("act", lambda h: h.activation(out=smb[:, 0:36 * NS], in_=sm[:, 0:36 * NS], func=AF.Copy), [B_sm], [B_smb])
        first = [True]
        for s in range(NS):
            for hs in range(4):
                col = hs * NS + s
                for g in range(3):
                    dil = DILS[g]
                    hh = g * 4 + hs
                    kc_t, kc_b = newwk(128)
                    vc_t, vc_b = newwk(128)
                    T.dma("pool", kc_t, cache[g][l, s, :, 0, hs, :].rearrange("(i r) d -> r i d", r=dil)[0], writes=[kc_b])
                    T.dma("pool", vc_t, cache[g][l, s, :, 1, hs, :].rearrange("(i r) d -> r i d", r=dil)[0], writes=[vc_b])
                    pi = nxt("pm", 2)
                    T.op("pe", lambda h, pi=pi, kc_t=kc_t: h.transpose(pm[pi][:, 0:128], kc_t, identf), [kc_b, B_cst], [B_pm[pi]])
                    kTb, kTbb = newwkb()
                    T.op("act", lambda h, kTb=kTb, pi=pi: h.activation(out=kTb[:, 0:128], in_=pm[pi][:, 0:128], func=AF.Copy),
                         [B_pm[pi]], [kTbb])
                    kcol = 12 * NS + hh * NS + s
                    T.op("act", lambda h, kTb=kTb, kcol=kcol: h.activation(out=kTb[:, 128:129], in_=smb[:, kcol:kcol + 1],
                                                                          func=AF.Copy), [B_smb, kTbb], [kTbb])
                    qcol = hh * NS + s
                    pz_i = nxt("pz", 3)
                    T.op("pe", lambda h, pz_i=pz_i, qcol=qcol, kTb=kTb: h.matmul(
                        pz[pz_i][0:1, 0:129], lhsT=smb[:, qcol:qcol + 1], rhs=kTb[:, 0:129], start=True, stop=True),
                        [B_smb, kTbb], [B_pz[pz_i]])
                    prow, prowb = newwk(129)
                    T.op("act", lambda h, prow=prow, pz_i=pz_i: h.activation(out=prow[0:1, :], in_=pz[pz_i][0:1, 0:129],
                                                                            func=AF.Exp, scale=float(SCALE)),
                         [B_pz[pz_i]], [prowb])
                    pi2 = nxt("pm", 2)
                    T.op("pe", lambda h, pi2=pi2, prow=prow: h.transpose(pm[pi2][:, 0:1], prow[0:1, 0:128], cstt[0:1, 0:1]),
                         [prowb, B_cst], [B_pm[pi2]])
                    pcol, pcolb = newwkb()
                    T.op("act", lambda h, pcol=pcol, pi2=pi2: h.activation(out=pcol[:, 0:1], in_=pm[pi2][:, 0:1], func=AF.Copy),
                         [B_pm[pi2]], [pcolb])
                    T.op("act", lambda h, pcol=pcol, prow=prow: h.activation(out=pcol[0:1, 8:9], in_=prow[0:1, 128:129],
                                                                            func=AF.Copy), [prowb, pcolb], [pcolb])
                    vcol = 24 * NS + hh * NS + s
                    pi3 = nxt("pm", 2)
                    T.op("pe", lambda h, pi3=pi3, vcol=vcol: h.transpose(pm[pi3][0:1, 0:128], sm[:, vcol:vcol + 1], identf),
                         [B_sm, B_cst], [B_pm[pi3]])
                    vb_t, vb_b = newwkb()
                    T.op("act", lambda h, vb_t=vb_t, pi3=pi3: h.activation(out=vb_t[0:1, 128:256], in_=pm[pi3][0:1, 0:128],
                                                                          func=AF.Copy), [B_pm[pi3]], [vb_b])
                    T.op("act", lambda h, vb_t=vb_t, vc_t=vc_t: h.activation(out=vb_t[:, 0:128], in_=vc_t, func=AF.Copy),
                         [vc_b, vb_b], [vb_b])
                    f0 = first[0]
                    first[0] = False
                    T.op("pe", lambda h, vb_t=vb_t, pcol=pcol, col=col, f0=f0: h.matmul(
                        pst[0][:, col:col + 1], lhsT=vb_t[:, 0:128], rhs=pcol[:, 0:1], start=f0, stop=False,
                        skip_group_check=True), [vb_b, pcolb], [B_pst[0]])
                    T.op("pe", lambda h, vb_t=vb_t, pcol=pcol, col=col: h.matmul(
                        pst[0][:, col:col + 1], lhsT=vb_t[0:1, 128:256], rhs=pcol[0:1, 8:9], start=False, stop=False,
                        skip_group_check=True), [vb_b, pcolb], [B_pst[0]])
                    T.op("pe", lambda h, pcol=pcol, col=col, f0=f0: h.matmul(
                        pst[1][:, col:col + 1], lhsT=onesb[:, :], rhs=pcol[:, 0:1], start=f0, stop=False,
                        skip_group_check=True), [B_const, pcolb], [B_pst[1]])
                    T.op("pe", lambda h, pcol=pcol, col=col: h.matmul(
                        pst[1][:, col:col + 1], lhsT=onesb[0:1, :], rhs=pcol[0:1, 8:9], start=False, stop=False,
                        skip_group_check=True), [B_const, pcolb], [B_pst[1]])
        rd_ap, rd_b = newwk(4 * NS)
        T.op("dve", lambda h: h.reciprocal(out=rd_ap, in_=pst[1][:, 0:4 * NS]), [B_pst[1]], [rd_b])
        T.op("dve", lambda h: h.tensor_tensor(out=ycs[:].rearrange("p h s -> p (h s)"), in0=pst[0][:, 0:4 * NS], in1=rd_ap,
                                              op=ALU.mult), [B_pst[0], rd_b], [B_ycs])

    def sample_layer(l):
        lastl = (l == L - 1)
        src_t = xsT if l == 0 else actsT[(l - 1) % len(actsT)]
        srcb = Bxs if l == 0 else B_acts[(l - 1) % len(actsT)]
        sfn = lambda jo, src_t=src_t: src_t[jo, :, :]
        sample_attn(l, sfn, srcb)
        for j in range(a):
            T.dma("pool", hista[:, j, :].rearrange("p (k s) -> p k s", s=NS), sca[:, l, j], writes=[B_ha[j]])
            T.dma("pool", histb[:, j, :].rearrange("p (k s) -> p k s", s=NS), scb[:, l, j], writes=[B_hb[j]])
        if lastl:
            dfn, dbuf = (lambda jo: o_ysT[jo, :, :]), OB["ys"]
        else:
            dfn, dbuf = (lambda jo, l=l: actsT[l % len(actsT)][jo, :, :]), B_acts[l % len(actsT)]
        phase3(l, NS, NS, sfn, srcb, dfn, dbuf, 0, True, False)

    Bxs = Buf("xsin")
    Bx = Buf("xin")
    for l in range(min(L, c.LIMIT_L)):
        lastl = (l == L - 1)
        src_t = xpT if l == 0 else actT[(l - 1) % len(actT)]
        srcb = (lambda p: Bx) if l == 0 else (lambda p, l=l: B_act[(l - 1) % len(actT)][p])
        setup_layer(l)
        for p in range(NP):
            phase1_prompt(l, p, xb[l], B_xb[l][p])
        if c.DO_ATTN:
            phase2(l)
        for j in range(a):
            T.op("dve", lambda h, j=j: h.memset(hista[:, j, :], 0.0), [], [B_ha[j]])
            T.op("dve", lambda h, j=j: h.memset(histb[:, j, :], 0.0), [], [B_hb[j]])
        for p in range(NP):
            t0 = p * TP
            if lastl:
                dfn, dbuf = (lambda jo, t0=t0: o_ypT[jo, :, t0:t0 + TP]), OB["yp"]
            else:
                dfn, dbuf = (lambda jo, t0=t0, l=l: actT[l % len(actT)][jo, :, t0:t0 + TP]), B_act[l % len(actT)][p]
            phase3(l, TP, 1, (lambda jo, t0=t0, src_t=src_t: src_t[jo, :, t0:t0 + TP]), srcb(p), dfn, dbuf, t0,
                   False, p == NP - 1)
        if c.DO_SAMPLE:
            sample_layer(l)
    T.finish(out_bufs)
    T.replay()
    return nc


def _chunk_w(w, kcn, ncn):
    return np.ascontiguousarray(w.reshape(kcn, 128, ncn, 128).transpose(2, 1, 0, 3)).reshape(ncn, 128, kcn * 128)


def _fm(x, nchunk):
    return np.ascontiguousarray(x.reshape(x.shape[0], nchunk, 128).transpose(1, 2, 0))


def _rope_tables(pos):
    half = HD // 2
    inv = (ROPE_THETA ** (-np.arange(half, dtype=np.float32) / half)).astype(np.float32)
    ang = pos.astype(np.float32)[:, None] * inv[None, :]
    cos, sin = np.cos(ang).astype(np.float32), np.sin(ang).astype(np.float32)
    cosT = np.concatenate([cos, cos], 1).T
    sinT = np.concatenate([-sin, sin], 1).T
    return np.ascontiguousarray(np.stack([cosT, sinT], 1)).astype(np.float32)


def _consts():
    i = np.arange(128)
    ident = np.eye(128, dtype=np.float32)
    U = (i[:, None] <= i[None, :]).astype(np.float32)
    Lm = (i[None, :] <= i[:, None]).astype(np.float32)
    sw = np.zeros((128, 128), np.float32)
    sw[(i + 64) % 128, i] = 1.0
    return np.ascontiguousarray(np.concatenate([ident, U, Lm, sw], 1))


def prep_inputs(c, inp):
    L, a, KC, NS = c.L, c.a, c.KC, c.NS
    f = lambda k: np.asarray(inp[k], dtype=np.float32)
    w_in = np.stack([_chunk_w(f("w_in")[l], KC, c.NCH) for l in range(L)])
    wcat = [np.concatenate([f("w_out_a")[l], f("w_out_b")[l], f("w_out_c")[l], f("w_out_d")[l]], 0) for l in range(L)]
    w_out = np.stack([_chunk_w(wcat[l], c.KCO, KC) for l in range(L)])
    w_o = np.stack([_chunk_w(f("w_o")[l], KC, KC) for l in range(L)])
    pp = np.zeros((128, L, c.NPP), np.float32)

    def put(name, l, arr):
        n = arr.shape[0] // 128
        o = c.pp[name]
        pp[:, l, o:o + n] = arr.reshape(n, 128).T

    for l in range(L):
        caw = f("conv_a_w")[l]
        pp[:, l, c.pp["caw"]:c.pp["caw"] + 3 * a] = caw.reshape(3, a, 128).transpose(2, 1, 0).reshape(128, 3 * a)
        cbw = f("conv_b_w")[l]
        pp[:, l, c.pp["cbw"]:c.pp["cbw"] + 31 * a] = cbw.reshape(31, a, 128).transpose(2, 1, 0).reshape(128, 31 * a)
        put("cbb", l, f("conv_b_bias")[l]); put("lbg", l, f("ln_b_g")[l]); put("lbb", l, f("ln_b_b")[l])
        put("ldg", l, f("ln_d_g")[l]); put("ldb", l, f("ln_d_b")[l]); put("bg", l, f("b_gate")[l])
        put("lg", l, f("ln_g")[l]); put("lb", l, f("ln_b")[l])
        pp[:, l, c.pp["ws00"]:c.pp["ws00"] + a] = f("w_s")[l][:, 0, 0][None, :]
        pp[:, l, c.pp["bs0"]:c.pp["bs0"] + a] = f("b_s")[l][:, 0][None, :]
    pp = np.ascontiguousarray(pp.reshape(128, L * c.NPP))
    wsT = np.ascontiguousarray(f("w_s").transpose(0, 1, 3, 2))
    bsr = np.ascontiguousarray(np.broadcast_to(f("b_s")[:, :, None, :], (L, a, 128, 128)))
    cst = _consts()
    ropep = _rope_tables(np.arange(c.SEQ))
    ropes = _rope_tables(np.array([c.PAST]))
    maps = []
    for ci in range(c.NCORES):
        s0 = ci * NS
        sca = f("state_conv_a")[:, s0:s0 + NS]
        scb = f("state_conv_b")[:, s0:s0 + NS]
        m = {
            "xpT": _fm(f("x_prompt")[ci], KC),
            "xsT": _fm(f("x_sample")[s0:s0 + NS, 0], KC),
            "w_in": w_in, "w_out": w_out, "w_o": w_o, "pp": pp, "wsT": wsT, "bsr": bsr, "cst": cst,
            "ropep": ropep, "ropes": ropes,
            "sca": np.ascontiguousarray(sca.reshape(L, NS, 2, a, 128).transpose(4, 0, 3, 2, 1)),
            "scb": np.ascontiguousarray(scb.reshape(L, NS, 30, a, 128).transpose(4, 0, 3, 2, 1)),
        }
        for g, k in enumerate(("cache_kv_w128", "cache_kv_w512", "cache_kv_w2048")):
            m["cache%d" % g] = np.ascontiguousarray(f(k)[:, s0:s0 + NS])
        maps.append(m)
    return maps


def assemble(c, res):
    L, a, NS, nco = c.L, c.a, c.NS, c.NCORES
    R = res

    def unfm(x):
        return np.ascontiguousarray(x.transpose(2, 0, 1).reshape(x.shape[2], -1))

    y_p = np.stack([unfm(R[i]["o_ypT"]) for i in range(nco)])
    y_s = np.concatenate([unfm(R[i]["o_ysT"]) for i in range(nco)])[:, None, :]
    ca_p = np.stack([R[i]["o_ca_p"].transpose(1, 3, 2, 0).reshape(L, 2, a * 128) for i in range(nco)], 1)
    ca_s = np.concatenate([R[i]["o_ca_s"].transpose(1, 4, 3, 2, 0).reshape(L, NS, 2, a * 128) for i in range(nco)], 1)
    cb_p = np.stack([R[i]["o_cb_p"].transpose(1, 3, 2, 0).reshape(L, 30, a * 128) for i in range(nco)], 1)
    cb_s = np.concatenate([R[i]["o_cb_s"].transpose(1, 4, 3, 2, 0).reshape(L, NS, 30, a * 128) for i in range(nco)], 1)
    outs = [y_p, y_s, ca_p, ca_s, cb_p, cb_s]
    for g in range(3):
        kp = np.stack([R[i]["o_kv_p%d" % g].transpose(0, 4, 1, 2, 3) for i in range(nco)], 1)
        ks = np.concatenate([R[i]["o_kv_s%d" % g].transpose(0, 4, 1, 2, 3) for i in range(nco)], 1)[:, :, None]
        outs += [kp, ks]
    gv = np.concatenate([R[i]["o_gv_s"].transpose(1, 3, 2, 0).reshape(L, NS, a * 128) for i in range(nco)], 1)[:, :, None, :]
    outs.append(gv)
    return tuple(np.ascontiguousarray(o, dtype=np.float32) for o in outs)


def run_cfg(c, inp, trace=False):
    nc = build(c)
    maps = prep_inputs(c, inp)
    res = run_bass_kernel_spmd(nc, maps, core_ids=list(range(c.NCORES)), **({"trace": True} if trace else {}))
    return assemble(c, res.results), res


def kernel(**inputs):
    c = Cfg()
    outs, _ = run_cfg(c, inputs)
    return outs
```
